# Optimizing a Trainium2 kernel written in Bass

```python
import jax, jax.numpy as jnp
from jax import lax
import numpy as np

D_MODEL = 1024
BATCH = 4
SEQ = 4096
DEPTH = 2

CHUNK = 64
QBLOCK = 128
HEAD_DIM = 64
EPS = 1e-6
H_FOX = 8
H_CHK = 8
N_LEFT_CHUNKS = 8
BAND = (N_LEFT_CHUNKS + 1) * CHUNK
REL_CLIP = 256
N_REL = CHUNK - 1 + REL_CLIP + 1
FORGET_BIAS = 3.0
H_SB = 8
H_MLA = 8
Q_LORA = 384
KV_LORA = 256
NOPE_DIM = 64
ROPE_DIM = 32
V_DIM = 64
ROPE_THETA = 10000.0
D_FF = 4 * D_MODEL
N_EVEN = (DEPTH + 1) // 2
N_ODD = DEPTH // 2

W_FOX = H_FOX * HEAD_DIM
W_CHK = H_CHK * HEAD_DIM
W_SB = H_SB * HEAD_DIM
W_MLA = H_MLA * V_DIM
SPLIT_AB = [W_FOX, W_FOX, W_FOX, H_FOX, W_CHK, W_CHK, W_CHK]
SPLIT_CD = [W_SB, W_SB, W_SB, Q_LORA, KV_LORA, ROPE_DIM]
IN_AB = sum(SPLIT_AB)
IN_CD = sum(SPLIT_CD)
MIX_AB = W_FOX + W_CHK
MIX_CD = W_SB + W_MLA

kernel_name = "chunk_causal_hybrid_fox_chunkrel_stickbreak_mla"


def _split(h, sizes):
    return jnp.split(h, np.cumsum(sizes)[:-1].tolist(), axis=-1)


def rmsnorm(x, g):
    xf = x.astype(jnp.float32)
    y = xf * lax.rsqrt(jnp.mean(xf * xf, axis=-1, keepdims=True) + EPS)
    return (y * g.astype(jnp.float32)).astype(x.dtype)


def rope(x, positions):
    half = ROPE_DIM // 2
    inv_freq = ROPE_THETA ** (-jnp.arange(half, dtype=jnp.float32) / half)
    ang = positions.astype(jnp.float32)[..., None] * inv_freq
    cos = jnp.cos(ang)[:, :, None, :]
    sin = jnp.sin(ang)[:, :, None, :]
    x1 = x[..., :half].astype(jnp.float32)
    x2 = x[..., half:].astype(jnp.float32)
    out = jnp.concatenate([x1 * cos - x2 * sin, x2 * cos + x1 * sin], axis=-1)
    return out.astype(x.dtype)


def _sweep_query_blocks(block_fn, seq):
    out = lax.map(block_fn, jnp.arange(seq // QBLOCK))
    nb, b, qb, h, dv = out.shape
    return jnp.moveaxis(out, 0, 1).reshape(b, nb * qb, h, dv)


def fox_attention(q, k, v, log_f):
    seq = q.shape[1]
    scale = HEAD_DIM ** -0.5
    cum = jnp.transpose(jnp.cumsum(log_f, axis=1), (0, 2, 1))
    k_pos = jnp.arange(seq)

    def block(i):
        start = i * QBLOCK
        qb = lax.dynamic_slice_in_dim(q, start, QBLOCK, axis=1)
        cq = lax.dynamic_slice_in_dim(cum, start, QBLOCK, axis=2)
        q_pos = start + jnp.arange(QBLOCK)
        s = jnp.einsum('bqhd,bkhd->bhqk', qb, k).astype(jnp.float32) * scale
        s = s + cq[..., :, None] - cum[..., None, :]
        mask = k_pos[None, :] <= q_pos[:, None]
        p = jax.nn.softmax(jnp.where(mask, s, -jnp.inf), axis=-1)
        return jnp.einsum('bhqk,bkhd->bqhd', p.astype(v.dtype), v)

    return _sweep_query_blocks(block, seq)


def chunked_relpos_attention(q, k, v, rel_bias):
    b, seq, h, d = q.shape
    nc = seq // CHUNK
    left = N_LEFT_CHUNKS * CHUNK
    scale = HEAD_DIM ** -0.5

    def band(t):
        tp = jnp.pad(t, ((0, 0), (left, 0), (0, 0), (0, 0)))
        tp = tp.reshape(b, nc + N_LEFT_CHUNKS, CHUNK, h, d)
        return jnp.concatenate([tp[:, i:i + nc] for i in range(N_LEFT_CHUNKS + 1)], axis=2)

    kb, vb = band(k), band(v)
    qc = q.reshape(b, nc, CHUNK, h, d)
    rel = np.arange(CHUNK)[:, None] + left - np.arange(BAND)[None, :]
    rel_idx = np.clip(rel, -(CHUNK - 1), REL_CLIP) + (CHUNK - 1)
    bias = rel_bias[:, rel_idx].astype(jnp.float32)
    s = jnp.einsum('bcqhd,bckhd->bhcqk', qc, kb).astype(jnp.float32) * scale
    s = s + bias[None, :, None]
    key_abs = jnp.arange(nc)[:, None] * CHUNK - left + jnp.arange(BAND)[None, :]
    valid = (key_abs >= 0)[None, None, :, None, :]
    p = jax.nn.softmax(jnp.where(valid, s, -jnp.inf), axis=-1)
    out = jnp.einsum('bhcqk,bckhd->bcqhd', p.astype(v.dtype), vb)
    return out.reshape(b, seq, h, d)


def stick_breaking_attention(q, k, v):
    seq = q.shape[1]
    scale = HEAD_DIM ** -0.5
    k_pos = jnp.arange(seq)

    def block(i):
        start = i * QBLOCK
        qb = lax.dynamic_slice_in_dim(q, start, QBLOCK, axis=1)
        q_pos = start + jnp.arange(QBLOCK)
        z = jnp.einsum('bqhd,bkhd->bhqk', qb, k).astype(jnp.float32) * scale
        mask = k_pos[None, :] < q_pos[:, None]
        log_beta = jax.nn.log_sigmoid(z)
        log_keep = jnp.where(mask, jax.nn.log_sigmoid(-z), 0.0)
        suffix = lax.cumsum(log_keep, axis=3, reverse=True) - log_keep
        a = jnp.where(mask, jnp.exp(log_beta + suffix), 0.0)
        return jnp.einsum('bhqk,bkhd->bqhd', a.astype(v.dtype), v)

    return _sweep_query_blocks(block, seq)


def mla_attention(q_nope, q_rope, k_nope, k_rope, v):
    seq = q_nope.shape[1]
    scale = (NOPE_DIM + ROPE_DIM) ** -0.5
    k_chunk = jnp.arange(seq) // CHUNK

    def block(i):
        start = i * QBLOCK
        qn = lax.dynamic_slice_in_dim(q_nope, start, QBLOCK, axis=1)
        qr = lax.dynamic_slice_in_dim(q_rope, start, QBLOCK, axis=1)
        q_chunk = (start + jnp.arange(QBLOCK)) // CHUNK
        s = (jnp.einsum('bqhd,bkhd->bhqk', qn, k_nope)
             + jnp.einsum('bqhr,bkr->bhqk', qr, k_rope)).astype(jnp.float32) * scale
        mask = k_chunk[None, :] <= q_chunk[:, None]
        p = jax.nn.softmax(jnp.where(mask, s, -jnp.inf), axis=-1)
        return jnp.einsum('bhqk,bkhd->bqhd', p.astype(v.dtype), v)

    return _sweep_query_blocks(block, seq)


def even_mixer(h, w_in, b_forget, rel_bias, w_out):
    b, s, _ = h.shape
    qa, ka, va, fa, qb, kb, vb = _split(h @ w_in, SPLIT_AB)
    heads = lambda t, n: t.reshape(b, s, n, HEAD_DIM)
    log_f = jax.nn.log_sigmoid((fa + b_forget).astype(jnp.float32))
    o_a = fox_attention(heads(qa, H_FOX), heads(ka, H_FOX), heads(va, H_FOX), log_f)
    o_b = chunked_relpos_attention(heads(qb, H_CHK), heads(kb, H_CHK), heads(vb, H_CHK), rel_bias)
    o = jnp.concatenate([o_a.reshape(b, s, W_FOX), o_b.reshape(b, s, W_CHK)], axis=-1)
    return o @ w_out


def odd_mixer(h, positions, w_in, q_norm, kv_norm, w_uq, w_ukv, w_out):
    b, s, _ = h.shape
    qc, kc, vc, c_q, c_kv, k_r = _split(h @ w_in, SPLIT_CD)
    heads = lambda t, n, d: t.reshape(b, s, n, d)
    o_c = stick_breaking_attention(heads(qc, H_SB, HEAD_DIM), heads(kc, H_SB, HEAD_DIM), heads(vc, H_SB, HEAD_DIM))
    q_full = heads(rmsnorm(c_q, q_norm) @ w_uq, H_MLA, NOPE_DIM + ROPE_DIM)
    q_nope, q_rope = q_full[..., :NOPE_DIM], rope(q_full[..., NOPE_DIM:], positions)
    kv_full = heads(rmsnorm(c_kv, kv_norm) @ w_ukv, H_MLA, NOPE_DIM + V_DIM)
    k_nope, v_d = kv_full[..., :NOPE_DIM], kv_full[..., NOPE_DIM:]
    k_rope = rope(k_r[:, :, None, :], positions)[:, :, 0, :]
    o_d = mla_attention(q_nope, q_rope, k_nope, k_rope, v_d)
    o = jnp.concatenate([o_c.reshape(b, s, W_SB), o_d.reshape(b, s, W_MLA)], axis=-1)
    return o @ w_out


def squared_relu_mlp(h, w_up, w_down):
    return jnp.square(jax.nn.relu(h @ w_up)) @ w_down


def setup_inputs(seed: int = 0) -> dict:
    key = jax.random.key(seed)
    ks = jax.random.split(key, 20)
    nrm = lambda k, shape, fan_in: jax.random.normal(k, shape, jnp.float32) * fan_in ** -0.5
    gain = lambda k, shape: 1.0 + 0.05 * jax.random.normal(k, shape, jnp.float32)
    x = jax.random.normal(ks[0], (BATCH, SEQ, D_MODEL), jnp.float32)
    offset = jax.random.randint(ks[1], (BATCH, 1), 0, 100000, dtype=jnp.int32)
    positions = offset + jnp.arange(SEQ, dtype=jnp.int32)[None, :]
    return {
        "x": x,
        "positions": positions,
        "norm_mix": gain(ks[2], (DEPTH, D_MODEL)),
        "norm_mlp": gain(ks[3], (DEPTH, D_MODEL)),
        "norm_final": gain(ks[4], (D_MODEL,)),
        "w_in_ab": nrm(ks[5], (N_EVEN, D_MODEL, IN_AB), D_MODEL),
        "b_forget": FORGET_BIAS + 0.5 * jax.random.normal(ks[6], (N_EVEN, H_FOX), jnp.float32),
        "rel_bias": 0.2 * jax.random.normal(ks[7], (N_EVEN, H_CHK, N_REL), jnp.float32),
        "w_out_ab": nrm(ks[8], (N_EVEN, MIX_AB, D_MODEL), MIX_AB),
        "w_in_cd": nrm(ks[9], (N_ODD, D_MODEL, IN_CD), D_MODEL),
        "q_norm": gain(ks[10], (N_ODD, Q_LORA)),
        "kv_norm": gain(ks[11], (N_ODD, KV_LORA)),
        "w_uq": nrm(ks[12], (N_ODD, Q_LORA, H_MLA * (NOPE_DIM + ROPE_DIM)), Q_LORA),
        "w_ukv": nrm(ks[13], (N_ODD, KV_LORA, H_MLA * (NOPE_DIM + V_DIM)), KV_LORA),
        "w_out_cd": nrm(ks[14], (N_ODD, MIX_CD, D_MODEL), MIX_CD),
        "w_up": nrm(ks[15], (DEPTH, D_MODEL, D_FF), D_MODEL),
        "w_down": nrm(ks[16], (DEPTH, D_FF, D_MODEL), D_FF),
    }


def reference(x, positions, norm_mix, norm_mlp, norm_final, w_in_ab, b_forget, rel_bias, w_out_ab,
              w_in_cd, q_norm, kv_norm, w_uq, w_ukv, w_out_cd, w_up, w_down):
    for layer in range(DEPTH):
        h = rmsnorm(x, norm_mix[layer])
        if layer % 2 == 0:
            e = layer // 2
            x = x + even_mixer(h, w_in_ab[e], b_forget[e], rel_bias[e], w_out_ab[e])
        else:
            o = layer // 2
            x = x + odd_mixer(h, positions, w_in_cd[o], q_norm[o], kv_norm[o], w_uq[o], w_ukv[o], w_out_cd[o])
        x = x + squared_relu_mlp(rmsnorm(x, norm_mlp[layer]), w_up[layer], w_down[layer])
    return rmsnorm(x, norm_final)
```

```python
import numpy as np
import ml_dtypes
import concourse.bass as bass
import concourse.mybir as mybir
from concourse.bass_utils import run_bass_kernel_spmd
from contextlib import ExitStack

F32 = mybir.dt.float32
BF16 = mybir.dt.bfloat16
I32 = mybir.dt.int32
AF = mybir.ActivationFunctionType
ALU = mybir.AluOpType

NCORES = 8
FUSED = True
D = 1024
T = 2048
SEQ = 4096
NTB = T // 512
EPS = 1e-6
NEG = -30000.0
MLA_SCALE = float(96 ** -0.5)
INV_FREQ_BITS = [0x3f800000, 0x3f0ff59a, 0x3ea1e89b, 0x3e361887, 0x3dcccccd, 0x3d6655c3, 0x3d0186e2, 0x3c91ad39,
                 0x3c23d70a, 0x3bb8449c, 0x3b4f3e37, 0x3ae91528, 0x3a83126f, 0x3a136a16, 0x39a5cb5f, 0x393a7753]

L0_QKF = 0
L0_VF = L0_QKF + 512 * T
L0_QKC = L0_VF + T * 256
L0_VC = L0_QKC + 512 * T
L0_SIZE = L0_VC + T * 256
L1_SBQK = 0
L1_SBV = L1_SBQK + 512 * T
L1_MQ = L1_SBV + T * 256
L1_MKN = L1_MQ + 4 * 96 * T
L1_KR = L1_MKN + 256 * T
L1_MV = L1_KR + 32 * T
L1_SIZE = L1_MV + T * 256


class Buf:
    __slots__ = ("w", "r")

    def __init__(self):
        self.w = None
        self.r = []


class Op:
    __slots__ = ("eng", "fn", "deps", "dma", "sig", "sem", "val", "prev_use", "cc", "aux")

    def __init__(self, eng, fn, dma):
        self.eng = eng
        self.fn = fn
        self.dma = dma
        self.cc = False
        self.aux = False
        self.deps = []
        self.sig = False
        self.sem = None
        self.val = 0
        self.prev_use = None


class Sched:
    ENGS = ("pe", "act", "dve", "pool", "sp")
    NDMASEM = {"sp": 16, "pool": 8, "act": 4}

    def __init__(self, nc):
        self.nc = nc
        self.ops = {e: [] for e in self.ENGS}
        self.since_bar = []

    def add(self, eng, fn, r=(), w=(), dma=False, nobar=False):
        op = Op(eng, fn, dma)
        deps = {}
        for b in r:
            if b.w is not None:
                deps[id(b.w)] = b.w
        for b in w:
            if b.w is not None:
                deps[id(b.w)] = b.w
            for x in b.r:
                deps[id(x)] = x
        for b in r:
            b.r.append(op)
        for b in w:
            b.w = op
            b.r = []
        for d in deps.values():
            if d is op:
                continue
            if (not d.dma) and (not dma) and d.eng == "pe" and eng == "pe":
                continue
            op.deps.append(d)
            d.sig = True
        if dma:
            op.sig = True
            if not nobar:
                self.since_bar.append(op)
        self.ops[eng].append(op)
        return op

    def barrier(self):
        lasts = []
        for e in self.ENGS:
            for op in reversed(self.ops[e]):
                if (not op.dma) and op.fn is not None and not op.aux:
                    lasts.append(op)
                    break
        dmas = self.since_bar
        self.since_bar = []
        for e in self.ENGS:
            op = Op(e, None, False)
            for d in lasts:
                if d.eng != e:
                    op.deps.append(d)
                    d.sig = True
            for d in reversed(dmas):
                op.deps.append(d)
            self.ops[e].append(op)

    def mm(self, out, lhsT, rhs, start=True, stop=True, r=(), w=(), **kw):
        return self.add("pe", lambda e: e.matmul(out, lhsT, rhs, start=start, stop=stop, **kw), r, w)

    def act(self, out, in_, func, bias=None, scale=None, r=(), w=()):
        kw = {}
        if bias is not None:
            kw["bias"] = bias
        if scale is not None:
            kw["scale"] = scale
        return self.add("act", lambda e: e.activation(out, in_, func, **kw), r, w)

    def tt(self, eng, out, in0, in1, op, r=(), w=()):
        return self.add(eng, lambda e: e.tensor_tensor(out, in0, in1, op), r, w)

    def ts(self, eng, out, in0, s1, op0, s2=None, op1=None, r=(), w=()):
        if op1 is None:
            return self.add(eng, lambda e: e.tensor_scalar(out, in0, s1, None, op0), r, w)
        return self.add(eng, lambda e: e.tensor_scalar(out, in0, s1, s2, op0, op1), r, w)

    def stt(self, out, in0, scalar, in1, op0, op1, r=(), w=()):
        return self.add("dve", lambda e: e.scalar_tensor_tensor(out, in0, scalar, in1, op0, op1), r, w)

    def copy(self, eng, out, in_, r=(), w=()):
        if eng == "act":
            return self.add("act", lambda e: e.copy(out, in_), r, w)
        return self.add(eng, lambda e: e.tensor_copy(out, in_), r, w)

    def memset(self, eng, ap, val, r=(), w=()):
        return self.add(eng, lambda e: e.memset(ap, val), r, w)

    def recip(self, out, in_, r=(), w=()):
        return self.add("dve", lambda e: e.reciprocal(out, in_), r, w)

    def dma(self, q, out, in_, r=(), w=()):
        return self.add(q, lambda e: e.dma_start(out=out, in_=in_), r, w, dma=True)

    def collective(self, fn, r=(), w=(), nobar=False):
        op = self.add("pool", fn, r, w, dma=True, nobar=nobar)
        op.cc = True
        return op

    def final_wait(self, eng, bufs):
        return self.add(eng, None, r=bufs)

    def emit(self):
        nc = self.nc
        with ExitStack() as es:
            block = es.enter_context(nc.Block())
            csem = {}
            for e in ("pe", "act", "dve", "pool"):
                csem[e] = es.enter_context(nc.semaphore("c_" + e))
            ccsem = es.enter_context(nc.semaphore("cc_sem"))
            dsem = {}
            for q, n in self.NDMASEM.items():
                dsem[q] = [es.enter_context(nc.semaphore("d_%s%d" % (q, i))) for i in range(n)]
            for e in self.ENGS:
                cnt = 0
                dcnt = 0
                uses = {}
                ccnt = 0
                for op in self.ops[e]:
                    if op.cc:
                        ccnt += 1
                        op.sem = ccsem
                        op.val = ccnt
                    elif op.dma:
                        pool = dsem[e]
                        k = dcnt % len(pool)
                        dcnt += 1
                        op.sem = pool[k]
                        prev = uses.get(k)
                        op.prev_use = prev
                        op.val = (prev.val if prev is not None else 0) + 16
                        uses[k] = op
                    elif op.sig:
                        cnt += 1
                        op.sem = csem[e]
                        op.val = cnt
            sched = self

            def run(eng_name):
                def body(e):
                    waited = {}

                    def wait(sem, val):
                        key = id(sem)
                        if waited.get(key, 0) >= val:
                            return
                        waited[key] = val
                        e.wait_ge(sem, val)

                    for op in sched.ops[eng_name]:
                        for d in op.deps:
                            wait(d.sem, d.val)
                        if op.dma and (not op.cc) and op.prev_use is not None:
                            wait(op.prev_use.sem, op.prev_use.val)
                        if op.fn is None:
                            continue
                        inst = op.fn(e)
                        if op.sig:
                            if op.cc:
                                inst.then_inc(op.sem)
                            else:
                                inst.then_inc(op.sem, 16 if op.dma else 1)

                return body

            block.tensor(run("pe"))
            block.scalar(run("act"))
            block.vector(run("dve"))
            block.gpsimd(run("pool"))
            block.sync(run("sp"))
        return nc


class Ring:
    def __init__(self, items):
        self.items = items
        self.i = 0

    def next(self):
        it = self.items[self.i % len(self.items)]
        self.i += 1
        return it


class Prog:
    def __init__(self):
        self.nc = bass.Bass("TRN2", target_bir_lowering=False)
        self.S = Sched(self.nc)
        self.es = ExitStack()
        self.outbufs = []
        self.nname = 0

    def name(self, p):
        self.nname += 1
        return "%s_%d" % (p, self.nname)

    def din(self, name, shape, dt):
        return self.nc.dram_tensor(name, list(shape), dt, kind="ExternalInput").ap()

    def dout(self, name, shape, dt):
        return self.nc.dram_tensor(name, list(shape), dt, kind="ExternalOutput").ap()

    def dint(self, name, shape, dt):
        return self.nc.dram_tensor(name, list(shape), dt).ap()

    def sb(self, es, shape, dt, name="t"):
        return es.enter_context(self.nc.sbuf_tensor(self.name(name), list(shape), dt))

    def ps(self, es, shape=(128, 512), dt=F32, name="p"):
        return es.enter_context(self.nc.psum_tensor(self.name(name), list(shape), dt))

    def store(self, q, dram_ap, sb_ap, r):
        b = Buf()
        self.S.dma(q, dram_ap, sb_ap, r=r, w=[b])
        self.outbufs.append(b)
        return b


def flat(ap2, off, shape):
    n = int(np.prod(shape))
    v = ap2[off:off + n]
    if len(shape) == 2:
        return v.rearrange("(a b) -> a b", b=shape[1])
    if len(shape) == 3:
        return v.rearrange("(a b c) -> a b c", b=shape[1], c=shape[2])
    return v


class RowCtx:
    def __init__(self, P, es, cmat):
        self.P = P
        self.es = es
        self.xsq = Ring([(P.sb(es, [128, 512], BF16, "xsq"), Buf()) for _ in range(3)])
        self.rstd = Ring([(P.sb(es, [128, 512], F32, "rstd"), Buf()) for _ in range(2)])
        self.wslab = Ring([(P.sb(es, [128, 4096], BF16, "wslab"), Buf()) for _ in range(3)])
        self.mmps = Ring([(P.ps(es), Buf()) for _ in range(4)])
        self.ssps = Ring([(P.ps(es), Buf()) for _ in range(2)])
        self.stage = Ring([(P.sb(es, [128, 2048], BF16, "stg"), Buf()) for _ in range(2)])
        self.stage_s = Ring([(P.sb(es, [128, 512], BF16, "stgs"), Buf()) for _ in range(3)])
        self.cmat = cmat

    def load_slab(self, src3, nk, ncols):
        t, b = self.wslab.next()
        v = t[:, 0:nk * ncols].rearrange("p (k c) -> p k c", c=ncols)
        self.P.S.dma("pool", v, src3, w=[b])
        return v, b


def rms_block(R, src3, src_b, nk, gain, gain_b, dim, emit_out):
    S = R.P.S
    ones = R.cmat[0][:, 1, :]
    pst, psb = R.ssps.next()
    for kc in range(nk):
        xq, xqb = R.xsq.next()
        S.act(xq[:], src3[:, kc, :], AF.Square, r=[src_b], w=[xqb])
        S.mm(pst[:], ones, xq[:], start=(kc == 0), stop=(kc == nk - 1), r=[xqb, R.cmat[1]], w=[psb])
    rt, rb = R.rstd.next()
    S.act(rt[:], pst[:], AF.Ln, bias=R.eps[:, 0:1], scale=1.0 / dim, r=[psb, R.eps_b], w=[rb])
    S.act(rt[:], rt[:], AF.Exp, scale=-0.5, r=[rb], w=[rb])
    for kc in range(nk):
        emit_out(kc, rt, rb)


def rms_fm(R, src, src_b, nk, gain, gain_b, dst, dst_b, dim, out_dram=None):
    P, S = R.P, R.P.S
    for tb in range(NTB):
        ts_ = slice(tb * 512, (tb + 1) * 512)

        def emit_out(kc, rt, rb, ts_=ts_):
            if out_dram is None:
                S.stt(dst[:, kc, ts_], src[:, kc, ts_], gain[:, kc:kc + 1], rt[:], ALU.mult, ALU.mult,
                      r=[src_b, gain_b, rb], w=[dst_b])
            else:
                ot, ob = R.ostage.next()
                S.stt(ot[:], src[:, kc, ts_], gain[:, kc:kc + 1], rt[:], ALU.mult, ALU.mult,
                      r=[src_b, gain_b, rb], w=[ob])
                P.store("sp", out_dram[kc * 128:(kc + 1) * 128, ts_], ot[:], [ob])

        rms_block(R, src[:, :, ts_], src_b, nk, gain, gain_b, dim, emit_out)


def lin_fm(R, wsrc, nk, ncols, rhs, rhs_b, epilogue, mrows=None):
    S = R.P.S
    wt, wb = R.load_slab(wsrc, nk, ncols)
    chunks = []
    c0 = 0
    while c0 < ncols:
        m = min(128, ncols - c0) if mrows is None else mrows
        chunks.append((c0, m))
        c0 += m
    for ci, (c0, m) in enumerate(chunks):
        for tb in range(NTB):
            pt, pb = R.mmps.next()
            for kc in range(nk):
                S.mm(pt[0:m, :], wt[:, kc, c0:c0 + m], rhs[:, kc, tb * 512:(tb + 1) * 512],
                     start=(kc == 0), stop=(kc == nk - 1), r=[wb, rhs_b], w=[pb])
            epilogue(ci, tb, pt, pb)


def lin_tm(R, wsrc, nk, ncols, lhs, lhs_b, epilogue):
    S = R.P.S
    wt, wb = R.load_slab(wsrc, nk, ncols)
    for tt_ in range(T // 128):
        pt, pb = R.mmps.next()
        for kc in range(nk):
            S.mm(pt[:, 0:ncols], lhs[:, kc, tt_ * 128:(tt_ + 1) * 128], wt[:, kc, 0:ncols],
                 start=(kc == 0), stop=(kc == nk - 1), r=[wb, lhs_b], w=[pb])
        epilogue(tt_, pt, pb)


def wview(w2, r0, nrows, c0, ncols):
    return w2[r0:r0 + nrows, c0:c0 + ncols].rearrange("(kc p) c -> p kc c", p=128)


def store_fm_chunk(R, dram2, row0, scale):
    S = R.P.S
    state = {}

    def epi(ci, tb, pt, pb):
        if tb == 0:
            state["st"] = R.stage.next()
        st, sbuf = state["st"]
        S.act(st[:, tb * 512:(tb + 1) * 512], pt[:], AF.Copy, scale=scale(ci) if callable(scale) else scale,
              r=[pb], w=[sbuf])
        if tb == NTB - 1:
            R.P.store("sp", dram2[row0 + ci * 128: row0 + (ci + 1) * 128, :], st[:], [sbuf])

    return epi


def store_tm(R, dram2, ncols, col0=0):
    S = R.P.S

    def epi(tt_, pt, pb):
        st, sbuf = R.stage_s.next()
        S.copy("dve", st[:, 0:ncols], pt[:, 0:ncols], r=[pb], w=[sbuf])
        R.P.store("sp", dram2[tt_ * 128:(tt_ + 1) * 128, col0:col0 + ncols], st[:, 0:ncols], [sbuf])

    return epi


def load_small(P, es, S, dram, shape, dt, q="sp"):
    t = P.sb(es, shape, dt, "sm")
    b = Buf()
    S.dma(q, t[:], dram, w=[b])
    return t, b


def load_gain(P, es, S, dram2, nk):
    t = P.sb(es, [128, nk], F32, "gain")
    b = Buf()
    S.dma("sp", t[:], dram2, w=[b])
    return t, b


def setup_row(P, es, cmat_d):
    S = P.S
    cm = P.sb(es, [128, 5, 128], BF16, "cmat")
    cmb = Buf()
    S.dma("sp", cm[:], cmat_d, w=[cmb])
    R = RowCtx(P, es, (cm, cmb))
    R.eps = P.sb(es, [128, 1], F32, "eps")
    R.eps_b = Buf()
    S.memset("dve", R.eps[:], EPS, w=[R.eps_b])
    return R


def phase_a0(P, R, xT, xT_b, hT, hT_b, d):
    S, es = P.S, R.es
    g, gb = load_gain(P, es, S, d["norm_mix0"], 8)
    rms_fm(R, xT, xT_b, 8, g, gb, hT, hT_b, D)
    xin = d["xin0"]
    w = d["w_in0"]
    for s in range(4):
        grp = s // 2
        qk = flat(xin[grp], L0_QKF if s % 2 == 0 else L0_QKC, (512, T))
        lin_fm(R, wview(w, 0, 1024, s * 512, 512), 8, 512, hT, hT_b,
               store_fm_chunk(R, qk, 0, lambda ci: 0.125 if ci < 2 else 1.0))
    for grp in range(2):
        vf = flat(xin[grp], L0_VF, (T, 256))
        vc = flat(xin[grp], L0_VC, (T, 256))

        def epi_v0(tt_, pt, pb, vf=vf, vc=vc):
            st, sbuf = R.stage_s.next()
            S.copy("act", st[:], pt[:], r=[pb], w=[sbuf])
            P.store("sp", vf[tt_ * 128:(tt_ + 1) * 128, :], st[:, 0:256], [sbuf])
            P.store("sp", vc[tt_ * 128:(tt_ + 1) * 128, :], st[:, 256:512], [sbuf])

        lin_tm(R, wview(w, 0, 1024, 2048 + grp * 512, 512), 8, 512, hT, hT_b, epi_v0)
    nb, nbb = load_small(P, es, S, d["b_forget"], [8, 1], F32)
    S.ts("dve", nb[:], nb[:], -1.0, ALU.mult, r=[nbb], w=[nbb])
    lf = P.sb(es, [8, T], F32, "lf")
    lfb = Buf()

    def epi_f(ci, tb, pt, pb):
        sl = slice(tb * 512, (tb + 1) * 512)
        S.act(lf[:, sl], pt[0:8, :], AF.Exp, bias=nb[:, 0:1], scale=-1.0, r=[pb, nbb], w=[lfb])
        S.act(lf[:, sl], lf[:, sl], AF.Ln, bias=1.0, r=[lfb], w=[lfb])
        S.ts("dve", lf[:, sl], lf[:, sl], -1.0, ALU.mult, r=[lfb], w=[lfb])

    lin_fm(R, wview(w, 0, 1024, 3072, 8), 8, 8, hT, hT_b, epi_f)
    for grp in range(2):
        P.store("sp", d["xinf0"][grp], lf[grp * 4:(grp + 1) * 4, :], [lfb])


def phase_b(P, R, es, xT, xT_b, hT, hT_b, d, L):
    S = P.S
    yout = d["yout%d" % L]
    S.dma("sp", hT[:], yout.rearrange("r (c p) t -> p (r c) t", p=128), w=[hT_b])
    wo = d["w_out%d" % L]
    for s in range(2):
        def epi(ci, tb, pt, pb, s=s):
            dc = s * 4 + ci
            sl = slice(tb * 512, (tb + 1) * 512)
            S.tt("dve", xT[:, dc, sl], pt[:], xT[:, dc, sl], ALU.add, r=[pb, xT_b], w=[xT_b])

        lin_fm(R, wview(wo, 0, 1024, s * 512, 512), 8, 512, hT, hT_b, epi)
    g, gb = load_gain(P, es, S, d["norm_mlp%d" % L], 8)
    rms_fm(R, xT, xT_b, 8, g, gb, hT, hT_b, D)
    wu, wd = d["w_up%d" % L], d["w_down%d" % L]
    with ExitStack() as es2:
        acts = Ring([(P.sb(es2, [128, 4, T], BF16, "act"), Buf()) for _ in range(2)])
        relu = Ring([(P.sb(es2, [128, 512], F32, "relu"), Buf()) for _ in range(2)])

        def up(s):
            at, ab = acts.next()

            def epi(ci, tb, pt, pb):
                rt, rb = relu.next()
                S.act(rt[:], pt[:], AF.Relu, r=[pb], w=[rb])
                S.act(at[:, ci, tb * 512:(tb + 1) * 512], rt[:], AF.Square, r=[rb], w=[ab])

            lin_fm(R, wview(wu, 0, 1024, s * 512, 512), 8, 512, hT, hT_b, epi)
            return at, ab

        def down(s, at, ab):
            wv, wb = R.load_slab(wd[s * 512:(s + 1) * 512, :].rearrange("(kc p) c -> p kc c", p=128), 4, 1024)
            for dc in range(8):
                for tb in range(NTB):
                    pt, pb = R.mmps.next()
                    sl = slice(tb * 512, (tb + 1) * 512)
                    for kc in range(4):
                        S.mm(pt[:], wv[:, kc, dc * 128:(dc + 1) * 128], at[:, kc, sl],
                             start=(kc == 0), stop=(kc == 3), r=[wb, ab], w=[pb])
                    S.tt("dve", xT[:, dc, sl], pt[:], xT[:, dc, sl], ALU.add, r=[pb, xT_b], w=[xT_b])

        prev = None
        for s in range(8):
            cur = up(s)
            if prev is not None:
                down(s - 1, *prev)
            prev = cur
        down(7, *prev)
        S.barrier()


def phase_a1(P, R, xT, xT_b, d):
    S = P.S
    xin = d["xin1"]
    w = d["w_in1"]
    p = slice(64, 96)
    with ExitStack() as es2:
        cqn = P.sb(es2, [128, 3, T], BF16, "cqn")
        ckvn = P.sb(es2, [128, 2, T], BF16, "ckvn")
        tab = P.sb(es2, [96, 2, T], BF16, "ropetab")
        cqnb, ckvnb, tabb = Buf(), Buf(), Buf()
        with ExitStack() as es3:
            hT = P.sb(es3, [128, 8, T], BF16, "hT")
            hT_b = Buf()
            g, gb = load_gain(P, es3, S, d["norm_mix1"], 8)
            rms_fm(R, xT, xT_b, 8, g, gb, hT, hT_b, D)
            rc, rcb = load_small(P, es3, S, d["ropec"], [96, 2], F32)
            posi = P.sb(es3, [96, 512], I32, "posi")
            ang = P.sb(es3, [96, 512], F32, "ang")
            kk = P.sb(es3, [96, 512], F32, "kk")
            rr = P.sb(es3, [96, 512], F32, "rr")
            mm_ = P.sb(es3, [96, 512], F32, "mm")
            ab_ = Buf()
            TWO_PI = 2.0 * np.pi
            C1 = 6.28125
            C2 = float(np.float32(TWO_PI - C1))
            MAGIC = 12582912.0

            def wrap(x):
                S.ts("dve", mm_[p, :], x[p, :], float(np.pi), ALU.is_gt, r=[ab_], w=[ab_])
                S.stt(x[p, :], mm_[p, :], -TWO_PI, x[p, :], ALU.mult, ALU.add, r=[ab_], w=[ab_])
                S.ts("dve", mm_[p, :], x[p, :], -float(np.pi), ALU.is_lt, r=[ab_], w=[ab_])
                S.stt(x[p, :], mm_[p, :], TWO_PI, x[p, :], ALU.mult, ALU.add, r=[ab_], w=[ab_])
                S.ts("dve", x[p, :], x[p, :], 3.1415925, ALU.min, -3.1415925, ALU.max, r=[ab_], w=[ab_])

            for tb in range(NTB):
                sl = slice(tb * 512, (tb + 1) * 512)
                S.dma("sp", posi[p, :], bass.AP(d["pos"].tensor, tb * 512, [[0, 32], [1, 512]]), w=[ab_])
                S.copy("dve", ang[p, :], posi[p, :], r=[ab_], w=[ab_])
                S.ts("dve", ang[p, :], ang[p, :], rc[p, 0:1], ALU.mult, r=[ab_, rcb], w=[ab_])
                S.ts("dve", kk[p, :], ang[p, :], float(np.float32(1.0 / TWO_PI)), ALU.mult, r=[ab_], w=[ab_])
                S.ts("dve", kk[p, :], kk[p, :], MAGIC, ALU.add, r=[ab_], w=[ab_])
                S.ts("dve", kk[p, :], kk[p, :], MAGIC, ALU.subtract, r=[ab_], w=[ab_])
                S.stt(rr[p, :], kk[p, :], -C1, ang[p, :], ALU.mult, ALU.add, r=[ab_], w=[ab_])
                S.stt(rr[p, :], kk[p, :], -C2, rr[p, :], ALU.mult, ALU.add, r=[ab_], w=[ab_])
                wrap(rr)
                S.act(kk[p, :], rr[p, :], AF.Sin, r=[ab_], w=[ab_])
                S.ts("dve", tab[p, 1, sl], kk[p, :], rc[p, 1:2], ALU.mult, r=[ab_, rcb], w=[tabb])
                S.ts("dve", rr[p, :], rr[p, :], float(np.pi / 2), ALU.add, r=[ab_], w=[ab_])
                wrap(rr)
                S.act(tab[p, 0, sl], rr[p, :], AF.Sin, r=[ab_], w=[tabb])
            for s_ in range(2):
                qk = flat(xin[s_], L1_SBQK, (512, T))
                lin_fm(R, wview(w, 0, 1024, s_ * 512, 512), 8, 512, hT, hT_b,
                       store_fm_chunk(R, qk, 0, lambda ci: 0.125 if ci < 2 else 1.0))

            def epi_v(tt_, pt, pb):
                st, sbuf = R.stage_s.next()
                S.copy("act", st[:], pt[:], r=[pb], w=[sbuf])
                for grp in range(2):
                    vd = flat(xin[grp], L1_SBV, (T, 256))
                    P.store("sp", vd[tt_ * 128:(tt_ + 1) * 128, :], st[:, grp * 256:(grp + 1) * 256], [sbuf])

            lin_tm(R, wview(w, 0, 1024, 1024, 512), 8, 512, hT, hT_b, epi_v)
            if "after_sb" in d:
                d["after_sb"]()
            gq, gqb = load_gain(P, es3, S, d["q_norm"], 3)
            gk, gkb = load_gain(P, es3, S, d["kv_norm"], 2)
            cqblk = P.sb(es3, [128, 3, 512], F32, "cqblk")
            ckvblk = P.sb(es3, [128, 2, 512], F32, "ckvblk")
            cqbb, ckvbb = Buf(), Buf()
            t1r = Ring([(P.sb(es3, [96, 512], F32, "t1"), Buf()) for _ in range(2)])
            kro = P.sb(es3, [96, T], BF16, "kro")
            krob = Buf()
            wq_, wqb = R.load_slab(wview(w, 0, 1024, 1536, 384), 8, 384)
            wk_, wkb = R.load_slab(wview(w, 0, 1024, 1920, 448), 8, 448)
            for tb in range(NTB):
                sl = slice(tb * 512, (tb + 1) * 512)

                def proj(wt, wb, c0, m):
                    pt, pb = R.mmps.next()
                    for kc in range(8):
                        S.mm(pt[0:m, :], wt[:, kc, c0:c0 + m], hT[:, kc, sl], start=(kc == 0), stop=(kc == 7),
                             r=[wb, hT_b], w=[pb])
                    return pt, pb

                for ci in range(3):
                    pt, pb = proj(wq_, wqb, ci * 128, 128)
                    S.copy("act", cqblk[:, ci, :], pt[:], r=[pb], w=[cqbb])

                def out_q(kc, rt, rb, sl=sl):
                    S.stt(cqn[:, kc, sl], cqblk[:, kc, :], gq[:, kc:kc + 1], rt[:], ALU.mult, ALU.mult,
                          r=[cqbb, gqb, rb], w=[cqnb])

                rms_block(R, cqblk, cqbb, 3, gq, gqb, 384, out_q)
                for ci in range(2):
                    pt, pb = proj(wk_, wkb, ci * 128, 128)
                    S.copy("act", ckvblk[:, ci, :], pt[:], r=[pb], w=[ckvbb])

                def out_kv(kc, rt, rb, sl=sl):
                    S.stt(ckvn[:, kc, sl], ckvblk[:, kc, :], gk[:, kc:kc + 1], rt[:], ALU.mult, ALU.mult,
                          r=[ckvbb, gkb, rb], w=[ckvnb])

                rms_block(R, ckvblk, ckvbb, 2, gk, gkb, 256, out_kv)
                pa, pab = proj(wk_, wkb, 256, 96)
                pbt, pbb = proj(wk_, wkb, 352, 96)
                t1, t1b = t1r.next()
                t2, t2b = t1r.next()
                S.tt("dve", t1[p, :], pa[p, :], tab[p, 0, sl], ALU.mult, r=[pab, tabb], w=[t1b])
                S.tt("dve", t2[p, :], pbt[p, :], tab[p, 1, sl], ALU.mult, r=[pbb, tabb], w=[t2b])
                S.tt("dve", kro[p, sl], t1[p, :], t2[p, :], ALU.add, r=[t1b, t2b], w=[krob])
            for grp in range(2):
                P.store("sp", flat(xin[grp], L1_KR, (32, T)), kro[p, :], [krob])
            S.barrier()
        with ExitStack() as es3:
            wq = d["w_uq"]
            qst = Ring([(P.sb(es3, [96, T], BF16, "qst"), Buf()) for _ in range(2)])
            t1r = Ring([(P.sb(es3, [96, 512], F32, "t1"), Buf()) for _ in range(2)])
            for hp in range(4):
                wt, wb = R.load_slab(wview(wq, 0, 384, hp * 384, 384), 3, 384)
                for hh in range(2):
                    h = hp * 2 + hh
                    st, stb = qst.next()
                    for tb in range(NTB):
                        sl = slice(tb * 512, (tb + 1) * 512)
                        pa, pab = R.mmps.next()
                        pbt, pbb = R.mmps.next()
                        for kc in range(3):
                            S.mm(pa[0:96, :], wt[:, kc, hh * 192:hh * 192 + 96], cqn[:, kc, sl],
                                 start=(kc == 0), stop=(kc == 2), r=[wb, cqnb], w=[pab])
                        for kc in range(3):
                            S.mm(pbt[0:96, :], wt[:, kc, hh * 192 + 96:hh * 192 + 192], cqn[:, kc, sl],
                                 start=(kc == 0), stop=(kc == 2), r=[wb, cqnb], w=[pbb])
                        S.act(st[0:64, sl], pa[0:64, :], AF.Copy, scale=MLA_SCALE, r=[pab], w=[stb])
                        t1, t1b = t1r.next()
                        t2, t2b = t1r.next()
                        S.stt(t1[p, :], pa[p, :], MLA_SCALE, tab[p, 0, sl], ALU.mult, ALU.mult, r=[pab, tabb], w=[t1b])
                        S.stt(t2[p, :], pbt[p, :], MLA_SCALE, tab[p, 1, sl], ALU.mult, ALU.mult, r=[pbb, tabb], w=[t2b])
                        S.tt("dve", st[p, sl], t1[p, :], t2[p, :], ALU.add, r=[t1b, t2b], w=[stb])
                    grp, hl = h // 4, h % 4
                    mq = flat(xin[grp], L1_MQ, (4, 96, T))
                    P.store("sp", mq[hl], st[:], [stb])
            wkv = d["w_ukv"]
            for grp in range(2):
                kn = flat(xin[grp], L1_MKN, (256, T))
                lin_fm(R, wview(wkv, 0, 256, grp * 256, 256), 2, 256, ckvn, ckvnb,
                       store_fm_chunk(R, kn, 0, 1.0))

            def epi_mv(tt_, pt, pb):
                st, sbuf = R.stage_s.next()
                S.copy("act", st[:], pt[:], r=[pb], w=[sbuf])
                for grp in range(2):
                    vd = flat(xin[grp], L1_MV, (T, 256))
                    P.store("sp", vd[tt_ * 128:(tt_ + 1) * 128, :], st[:, grp * 256:(grp + 1) * 256], [sbuf])

            lin_tm(R, wview(wkv, 0, 256, 512, 512), 2, 512, ckvn, ckvnb, epi_mv)
            S.barrier()


class AttCtx:
    def __init__(self, P, es, d, ntiles_mask, mask_first):
        S = P.S
        self.P, self.es = P, es
        self.cm = P.sb(es, [128, 5, 128], BF16, "cmat")
        self.cmb = Buf()
        S.dma("sp", self.cm[:], d["cmat"], w=[self.cmb])
        self.masks = P.sb(es, [128, ntiles_mask, 512], BF16, "masks")
        self.maskb = Buf()
        S.dma("sp", self.masks[:], d["cmask"][:, mask_first:mask_first + ntiles_mask, :], w=[self.maskb])
        self.kq = Ring([((P.sb(es, [128, SEQ], BF16, "kt"), P.sb(es, [128, SEQ], BF16, "qt")),
                         [Buf() for _ in range(4)]) for _ in range(2)])
        self.sps = Ring([(P.ps(es), Buf()) for _ in range(4)])
        self.nps = Ring([(P.ps(es), Buf()) for _ in range(2)])
        self.dps = Ring([(P.ps(es), Buf()) for _ in range(1)])
        self.pt = Ring([(P.sb(es, [128, 512], BF16, "pT"), Buf()) for _ in range(3)])
        self.ost = Ring([(P.sb(es, [65, SEQ], BF16, "ost"), Buf()) for _ in range(2)])
        self.rc = Ring([(P.sb(es, [65, 512], F32, "rc"), Buf()) for _ in range(1)])
        self.dn = Ring([(P.sb(es, [1, 512], F32, "dn"), Buf()) for _ in range(1)])
        self.rcpf = Ring([(P.sb(es, [128, 512], F32, "rcpf"), Buf()) for _ in range(1)])
        S.memset("dve", self.rcpf.items[0][0][:], 0.0, w=[self.rcpf.items[0][1]])
        self.row0f = P.sb(es, [128, 128], F32, "row0f")
        self.row0fb = Buf()
        S.memset("dve", self.row0f[:], 0.0, w=[self.row0fb])
        S.memset("dve", self.row0f[0:1, :], 1.0, w=[self.row0fb])


def load_v(A, xout, off, ncols, dep=()):
    P, S = A.P, A.P.S
    nh = ncols // 64
    n = 32 * nh * 66
    vf = P.sb(A.es, [128, n + 64], BF16, "v")
    vb = [Buf() for _ in range(6)]
    S.memset("dve", vf[:, n:n + 64], 0.0, w=[vb[0]])
    v4 = vf[:, 0:n].rearrange("p (j h c) -> p j h c", h=nh, c=66)
    S.memset("dve", v4[:, :, :, 0:1], 1.0, w=[vb[1]])
    i = 0
    for r in range(2):
        for h in range(nh):
            src = flat(xout[r], off, (T, ncols))[:, h * 64:(h + 1) * 64].rearrange("(j p) c -> p j c", p=128)
            S.dma("sp", v4[:, r * 16:(r + 1) * 16, h, 1:65], src, r=list(dep), w=[vb[2 + i % 4]])
            i += 1

    def lhs(kb, h):
        b = (kb * nh + h) * 66
        return vf[:, b:b + 128]

    return lhs, vb


def softmax_head(A, kt, qt, kqb, krows, v, vb, vh, tiles, yin, hrow):
    P, S = A.P, A.P.S
    J = A.cm[:, 0, :]
    ost, ostb = A.ost.next()
    n = len(tiles)
    sts = [None] * n

    def issue_s(i):
        qb, kb, m, mb, first, last, (c0, c1) = tiles[i]
        st, sb_ = A.sps.next()
        sts[i] = (st, sb_)
        S.mm(st[:, c0:c1], kt[:, kb * 128:(kb + 1) * 128], qt[:, qb * 512 + c0:qb * 512 + c1],
             start=True, stop=(m is None), r=list(kqb), w=[sb_])
        if m is not None:
            S.mm(st[:, c0:c1], J, m[:, c0:c1], start=False, stop=True, r=[A.cmb, mb], w=[sb_])

    acc = {}
    pending = []

    def flush(upto):
        while pending and pending[0][0] <= upto:
            pending.pop(0)[1]()

    LA = 3
    for i0 in range(min(LA, n)):
        issue_s(i0)
    for i in range(n):
        if i + LA < n:
            issue_s(i + LA)
        qb, kb, m, mb, first, last, (c0, c1) = tiles[i]
        st, sb_ = sts[i]
        pt, ptb = A.pt.next()
        S.act(pt[:, c0:c1], st[:, c0:c1], AF.Exp, r=[sb_], w=[ptb])
        if first:
            assert (c0, c1) == (0, 512)
            acc["n"] = A.nps.next()
        nt, nb = acc["n"]
        S.mm(nt[:, c0:c1], v(kb, vh), pt[:, c0:c1], start=first, stop=last, r=list(vb) + [ptb], w=[nb],
             skip_group_check=True)
        flush(i)
        if last:
            dn, dnb = A.dn.next()
            S.act(dn[0:1, :], nt[0:1, :], AF.Ln, r=[nb], w=[dnb])
            rf, rfb = A.rcpf.next()
            S.act(rf[0:1, :], dn[0:1, :], AF.Exp, scale=-1.0, r=[dnb], w=[rfb])

            def fin(nt=nt, nb=nb, rf=rf, rfb=rfb, qb=qb):
                bt, bb = A.dps.next()
                S.mm(bt[:], A.row0f[:], rf[:], start=True, stop=True, r=[A.row0fb, rfb], w=[bb])
                rc, rcb = A.rc.next()
                S.copy("dve", rc[:], bt[0:65, :], r=[bb], w=[rcb])
                S.tt("dve", ost[:, qb * 512:(qb + 1) * 512], nt[0:65, :], rc[:], ALU.mult, r=[nb, rcb], w=[ostb])

            pending.append((i + 6, fin))
    flush(n + 100)
    for r in range(2):
        P.store("pool", yin[r, hrow:hrow + 64, :], ost[1:65, r * T:(r + 1) * T], [ostb])


def load_kq_pair(A, xout, offk, nk_total, krow0, offq, nq_total, qrow0, kt, qt, kbuf, qbuf, dep=()):
    S = A.P.S
    for r in range(2):
        srck = flat(xout[r], offk, (nk_total, T))[krow0:krow0 + 64, :]
        S.dma("sp", kt[0:64, r * T:(r + 1) * T], srck, r=list(dep), w=[kbuf])
        srcq = flat(xout[r], offq, (nq_total, T))[qrow0:qrow0 + 64, :]
        S.dma("sp", qt[0:64, r * T:(r + 1) * T], srcq, r=list(dep), w=[qbuf])


def load_kq_rows(A, xout, off, nrows_total, row0, nrows, kt_or_qt, dst_row0, buf, dep=()):
    S = A.P.S
    for r in range(2):
        src = flat(xout[r], off, (nrows_total, T))[row0:row0 + nrows, :]
        S.dma("sp", kt_or_qt[dst_row0:dst_row0 + nrows, r * T:(r + 1) * T], src, r=list(dep), w=[buf])


def phase_attn0(P, d):
    S = P.S
    xout, yin = d["xout0"], d["yin0"]
    cumd = d["cumd"]
    Ed = d["ebuf"]
    cumdb, Edb = Buf(), Buf()
    with ExitStack() as es:
        lf = P.sb(es, [4, SEQ], F32, "lf")
        lfb = Buf()
        for r in range(2):
            S.dma("sp", lf[:, r * T:(r + 1) * T], d["xoutf0"][r], w=[lfb])
        onesf = P.sb(es, [4, SEQ], F32, "onesf")
        S.memset("dve", onesf[:], 1.0, w=[lfb])
        cum = P.sb(es, [4, SEQ], F32, "cum")
        S.add("dve", lambda e: e.tensor_tensor_scan(cum[:], onesf[:], lf[:], 0.0, ALU.mult, ALU.add), r=[lfb], w=[lfb])
        c3 = P.sb(es, [4, 2, 3, SEQ], BF16, "c3")
        c3b = Buf()
        S.copy("dve", c3[:, 1, 0, :], cum[:], r=[lfb], w=[c3b])
        S.tt("dve", cum[:], cum[:], c3[:, 1, 0, :], ALU.subtract, r=[lfb, c3b], w=[lfb])
        S.copy("dve", c3[:, 1, 1, :], cum[:], r=[lfb], w=[c3b])
        S.tt("dve", cum[:], cum[:], c3[:, 1, 1, :], ALU.subtract, r=[lfb, c3b], w=[lfb])
        S.copy("dve", c3[:, 1, 2, :], cum[:], r=[lfb], w=[c3b])
        for i in range(3):
            S.ts("dve", c3[:, 0, i, :], c3[:, 1, i, :], -1.0, ALU.mult, r=[c3b], w=[c3b])
        S.dma("sp", cumd, c3[:], r=[c3b], w=[cumdb])
        rb, rbb = load_small(P, es, S, d["rel_bias"], [4, 320], F32)
        E = P.sb(es, [4, 1536], F32, "E")
        Eb = Buf()
        S.memset("dve", E[:], 0.0, w=[Eb])
        S.ts("dve", E[:, 0:449], E[:, 0:449], rb[:, 0:1], ALU.add, r=[rbb, Eb], w=[Eb])
        S.copy("dve", E[:, 449:767], rb[:, 1:319], r=[rbb, Eb], w=[Eb])
        S.ts("dve", E[:, 767:1536], E[:, 767:1536], rb[:, 319:320], ALU.add, r=[rbb, Eb], w=[Eb])
        S.dma("sp", Ed, E[:], r=[Eb], w=[Edb])
        S.barrier()
    with ExitStack() as es:
        A = AttCtx(P, es, d, 12, 0)
        depa, depb = d.get("xdep0a", []), d.get("xdep0b", [])
        bias = P.sb(es, [128, 32, 512], BF16, "bias")
        biasb = Buf()
        hkw = P.sb(es, [128, 1408], F32, "hkw")
        hkwb = Buf()
        VF = {}

        def build_bias():
            for h in range(4):
                src = bass.AP(Ed.tensor, h * 1536, [[1, 128], [1, 1408]])
                S.dma("sp", hkw[:], src, r=[Edb], w=[hkwb])
                for jb in range(8):
                    o = 896 - 128 * jb
                    S.tt("dve", bias[:, h * 8 + jb, :], hkw[:, o:o + 512], A.masks[:, 4 + jb, :], ALU.add,
                         r=[hkwb, A.maskb], w=[biasb])

        def load_head(j):
            (kt, qt), kqb = A.kq.next()
            kB, kxB, qB, qxB = kqb
            S.memset("dve", kt[64:128, :], 0.0, w=[kxB])
            S.memset("dve", qt[64:128, :], 0.0, w=[qxB])
            if j < 4:
                load_kq_pair(A, xout, L0_QKF, 512, 256 + j * 64, L0_QKF, 512, j * 64, kt, qt, kB, qB,
                             d.get("xdep0k", depa))
                S.memset("dve", kt[64:70, :], 1.0, w=[kxB])
                S.memset("dve", qt[64:70, :], 1.0, w=[qxB])
                S.dma("sp", kt[64:67, :], cumd[j, 0], r=[cumdb], w=[kxB])
                S.dma("sp", qt[67:70, :], cumd[j, 1], r=[cumdb], w=[qxB])
            else:
                jj = j - 4
                load_kq_pair(A, xout, L0_QKC, 512, 256 + jj * 64, L0_QKC, 512, jj * 64, kt, qt, kB, qB, depb)
            return kt, qt, kqb

        def run_head(j, kt, qt, kqb):
            tiles = []
            if j < 4:
                for qb in range(8):
                    nk = 4 * qb + 4
                    for kb in range(nk):
                        jm = kb - 4 * qb
                        m = A.masks[:, jm, :] if jm >= 0 else None
                        tiles.append((qb, kb, m, A.maskb, kb == 0, kb == nk - 1, (128 * max(jm, 0), 512)))
                softmax_head(A, kt, qt, kqb, 70, VF["v"], VF["b"], j, tiles, yin, j * 64)
            else:
                jj = j - 4
                CR = {0: (0, 128), 1: (0, 256), 2: (0, 384), 3: (0, 512), 4: (0, 512), 5: (128, 512),
                      6: (256, 512), 7: (384, 512)}
                for qb in range(8):
                    jbs = [jb for jb in (3, 4, 0, 1, 2, 5, 6, 7) if 4 * qb - 4 + jb >= 0]
                    for jb in jbs:
                        kb = 4 * qb - 4 + jb
                        tiles.append((qb, kb, bias[:, jj * 8 + jb, :], biasb, jb == jbs[0], jb == jbs[-1], CR[jb]))
                softmax_head(A, kt, qt, kqb, 64, VC["v"], VC["b"], jj, tiles, yin, 256 + jj * 64)

        VC = {}
        nxt = load_head(0)
        VF["v"], VF["b"] = load_v(A, xout, L0_VF, 256, depa)
        build_bias()
        for j in range(8):
            cur = nxt
            if j == 2:
                VC["v"], VC["b"] = load_v(A, xout, L0_VC, 256, depb)
            if j + 1 < 8:
                nxt = load_head(j + 1)
            run_head(j, *cur)
        S.barrier()


def sb_head(A, kt, qt, kqb, v, vb, vcol, yin, hrow, X):
    P, S = A.P, A.P.S
    J = A.cm[:, 0, :]
    NTRI = A.cm[:, 2, :]
    NONES = A.cm[:, 3, :]
    ost, ostb = A.ost.next()
    tiles = []
    for qb in range(8):
        kbs = list(range(4 * qb + 3, -1, -1))
        for kb in kbs:
            jm = kb - 4 * qb
            tiles.append((qb, kb, jm, kb == kbs[0], kb == kbs[-1]))
    n = len(tiles)
    zs, sps, rss, es_ = [None] * n, [None] * n, [None] * n, [None] * n
    acc = {}

    def stage1(i):
        qb, kb, jm, first, last = tiles[i]
        c0 = 128 * max(jm, 0)
        zt, zb = A.sps.next()
        zs[i] = (zt, zb)
        S.mm(zt[:, c0:], kt[:, kb * 128:(kb + 1) * 128], qt[:, qb * 512 + c0:(qb + 1) * 512],
             start=True, stop=False, r=list(kqb), w=[zb])
        if jm >= 0:
            S.mm(zt[:, c0:], J, A.masks[:, jm, c0:], start=False, stop=False, r=[A.cmb, A.maskb], w=[zb])
        et, eb = X["e"].next()
        S.act(et[:, c0:], zt[:, c0:], AF.Exp, r=[zb], w=[eb])
        es_[i] = (et, eb)

    def stage1b(i):
        qb, kb, jm, first, last = tiles[i]
        c0 = 128 * max(jm, 0)
        et, eb = es_[i]
        spt, spb = X["sp"].next()
        sps[i] = (spt, spb)
        S.act(spt[:, c0:], et[:, c0:], AF.Ln, bias=1.0, r=[eb], w=[spb])
        rt, rb = X["rs"].next()
        rss[i] = (rt, rb)
        if c0 > 0:
            S.memset("dve", rt[:, 0:c0], 0.0, w=[rb])
        if first:
            S.copy("dve", rt[:, c0:], spt[:, c0:], r=[spb], w=[rb])
        else:
            pr, prb = rss[i - 1]
            S.tt("dve", rt[:, c0:], pr[:, c0:], spt[:, c0:], ALU.add, r=[prb, spb], w=[rb])

    def stage2(i):
        qb, kb, jm, first, last = tiles[i]
        c0 = 128 * max(jm, 0)
        zt, zb = zs[i]
        spt, spb = sps[i]
        S.mm(zt[:, c0:], NTRI, spt[:, c0:], start=False, stop=first, r=[A.cmb, spb], w=[zb])
        if not first:
            pr, prb = rss[i - 1]
            S.mm(zt[:, c0:], NONES, pr[:, c0:], start=False, stop=True, r=[A.cmb, prb], w=[zb])
        pt, ptb = A.pt.next()
        sps[i] = (pt, ptb)
        if first and c0 > 0:
            S.memset("dve", pt[:, 0:c0], 0.0, w=[ptb])
        S.act(pt[:, c0:], zt[:, c0:], AF.Exp, r=[zb], w=[ptb])

    def stage3(i):
        qb, kb, jm, first, last = tiles[i]
        c0 = 0 if first else 128 * max(jm, 0)
        pt, ptb = sps[i]
        if first:
            acc["n"] = A.nps.next()
        nt, nb = acc["n"]
        S.mm(nt[:, c0:], v(kb, vcol), pt[:, c0:], start=first, stop=last, r=list(vb) + [ptb], w=[nb],
             skip_group_check=True)
        if last:
            S.copy("dve", ost[:, qb * 512:(qb + 1) * 512], nt[0:65, :], r=[nb], w=[ostb])

    for step in range(n + 2):
        if step < n:
            stage1(step)
        if 0 <= step - 1 < n:
            stage2(step - 1)
        if step < n:
            stage1b(step)
        if 0 <= step - 2 < n:
            stage3(step - 2)
    for r in range(2):
        P.store("pool", yin[r, hrow:hrow + 64, :], ost[1:65, r * T:(r + 1) * T], [ostb])


def phase_attn1(P, d):
    S = P.S
    with ExitStack() as es:
        A = AttCtx(P, es, d, 8, 12)
        xout, yin = d["xout1"], d["yin1"]
        dep1, dep2 = d.get("xdep1a", []), d.get("xdep1b", [])
        VS = {}
        VM = {}
        X = {
            "e": Ring([(P.ps(es), Buf()) for _ in range(1)]),
            "sp": Ring([(P.sb(es, [128, 512], BF16, "sp"), Buf()) for _ in range(3)]),
            "rs": Ring([(P.sb(es, [128, 512], BF16, "rs"), Buf()) for _ in range(3)]),
        }

        def load_head(j):
            (kt, qt), kqb = A.kq.next()
            kB, kxB, qB, qxB = kqb
            S.memset("dve", kt[64:128, :], 0.0, w=[kxB])
            S.memset("dve", qt[64:128, :], 0.0, w=[qxB])
            if j < 4:
                load_kq_pair(A, xout, L1_SBQK, 512, 256 + j * 64, L1_SBQK, 512, j * 64, kt, qt, kB, qB,
                             d.get("xdep1k", dep1))
            else:
                jj = j - 4
                for r in range(2):
                    srck = flat(xout[r], L1_MKN, (256, T))[jj * 64:jj * 64 + 64, :]
                    S.dma("sp", kt[0:64, r * T:(r + 1) * T], srck, r=list(dep2), w=[kB])
                    src = flat(xout[r], L1_MQ, (4, 96, T))[jj]
                    S.dma("sp", qt[0:96, r * T:(r + 1) * T], src, r=list(dep2), w=[qB, qxB])
                load_kq_rows(A, xout, L1_KR, 32, 0, 32, kt, 64, kxB, dep2)
            return kt, qt, kqb

        def run_head(j, kt, qt, kqb):
            if j < 4:
                sb_head(A, kt, qt, kqb, VS["v"], VS["b"], j, yin, j * 64, X)
            else:
                jj = j - 4
                tiles = []
                for qb in range(8):
                    nk = 4 * qb + 4
                    for kb in range(nk):
                        jm = kb - 4 * qb
                        m = A.masks[:, 4 + jm, :] if jm >= 0 else None
                        tiles.append((qb, kb, m, A.maskb, kb == 0, kb == nk - 1, (128 * max(jm, 0), 512)))
                softmax_head(A, kt, qt, kqb, 96, VM["v"], VM["b"], jj, tiles, yin, 256 + jj * 64)

        nxt = load_head(0)
        VS["v"], VS["b"] = load_v(A, xout, L1_SBV, 256, dep1)
        for j in range(8):
            cur = nxt
            if j == 2:
                VM["v"], VM["b"] = load_v(A, xout, L1_MV, 256, dep2)
            if j + 1 < 8:
                nxt = load_head(j + 1)
            run_head(j, *cur)
        S.barrier()


def _perm_w_in0():
    cols = []
    for grp in range(2):
        hs = range(4 * grp, 4 * grp + 4)
        for base in (0, 512, 1544, 2056):
            for h in hs:
                cols += list(range(base + h * 64, base + h * 64 + 64))
    for grp in range(2):
        hs = range(4 * grp, 4 * grp + 4)
        for base in (1024, 2568):
            for h in hs:
                cols += list(range(base + h * 64, base + h * 64 + 64))
    cols += list(range(1536, 1544))
    return np.array(cols)


def _perm_w_in1():
    cols = []
    for grp in range(2):
        hs = range(4 * grp, 4 * grp + 4)
        for base in (0, 512):
            for h in hs:
                cols += list(range(base + h * 64, base + h * 64 + 64))
    for grp in range(2):
        for h in range(4 * grp, 4 * grp + 4):
            cols += list(range(1024 + h * 64, 1024 + h * 64 + 64))
    cols += list(range(1536, 1920))
    cols += list(range(1920, 2176))
    kr = list(range(2176, 2208))
    krp = kr[16:] + kr[:16]
    cols += list(range(1920, 1984)) + kr
    cols += list(range(1920, 1984)) + krp
    return np.array(cols)


def _perm_w_uq():
    cols = []
    for h in range(8):
        b = h * 96
        nope = list(range(b, b + 64))
        rope = list(range(b + 64, b + 96))
        cols += nope + rope + nope + rope[16:] + rope[:16]
    return np.array(cols)


def _perm_w_ukv():
    cols = []
    for grp in range(2):
        for h in range(4 * grp, 4 * grp + 4):
            cols += list(range(h * 128, h * 128 + 64))
    for grp in range(2):
        for h in range(4 * grp, 4 * grp + 4):
            cols += list(range(h * 128 + 64, h * 128 + 128))
    return np.array(cols)


def _perm_w_out():
    rows = []
    for grp in range(2):
        for base in (0, 512):
            for h in range(4 * grp, 4 * grp + 4):
                rows += list(range(base + h * 64, base + h * 64 + 64))
    return np.array(rows)


def _const_masks():
    kk = np.arange(128)[:, None]
    qq = np.arange(512)[None, :]
    tiles = []
    for j in range(4):
        tiles.append(np.where(128 * j + kk <= qq, 0.0, NEG))
    for jb in range(8):
        v = (qq // 64) + 8 - 2 * jb - (kk // 64)
        tiles.append(np.where((v >= 0) & (v <= 8), 0.0, NEG))
    for j in range(4):
        tiles.append(np.where(128 * j + kk < qq, 0.0, NEG))
    for j in range(4):
        tiles.append(np.where((128 * j + kk) // 64 <= qq // 64, 0.0, NEG))
    m = np.stack(tiles, axis=1).astype(np.float32)
    m = m[::-1].copy()
    return m.astype(ml_dtypes.bfloat16)


def _const_mats():
    i = np.arange(128)
    J = (i[:, None] + i[None, :] == 127).astype(np.float32)
    ones = np.ones((128, 128), np.float32)
    ntri = -(i[:, None] >= i[None, :]).astype(np.float32)
    nones = -ones
    row0 = np.zeros((128, 128), np.float32)
    row0[0, :] = 1.0
    return np.stack([J, ones, ntri, nones, row0], axis=1).astype(ml_dtypes.bfloat16)


def _rope_consts():
    inv = np.array(INV_FREQ_BITS, dtype=np.uint32).view(np.float32)
    c = np.zeros((96, 2), np.float32)
    c[64:96, 0] = np.concatenate([inv, inv])
    c[64:96, 1] = np.concatenate([-np.ones(16, np.float32), np.ones(16, np.float32)])
    return c


def _run(P, in_maps):
    res = run_bass_kernel_spmd(P.nc, in_maps, core_ids=list(range(NCORES)))
    return res.results


def _finish(P):
    P.S.final_wait("sp", P.outbufs)
    P.S.emit()


def _exchange(xin_list):
    out = []
    for c in range(NCORES):
        b, g = c // 2, c % 2
        out.append(np.stack([xin_list[2 * b + r][g] for r in range(2)], axis=0))
    return out


def build_a0():
    P = Prog()
    d = {
        "xT": P.din("xT", [D, T], F32),
        "norm_mix0": P.din("norm_mix0", [128, 8], F32),
        "w_in0": P.din("w_in0", [D, 3080], F32),
        "b_forget": P.din("b_forget", [8, 1], F32),
        "cmat": P.din("cmat", [128, 5, 128], BF16),
        "xin0": P.dout("xin0", [2, L0_SIZE], BF16),
        "xinf0": P.dout("xinf0", [2, 4, T], F32),
    }
    with ExitStack() as es:
        R = setup_row(P, es, d["cmat"])
        xT = P.sb(es, [128, 8, T], F32, "xT")
        hT = P.sb(es, [128, 8, T], BF16, "hT")
        xT_b, hT_b = Buf(), Buf()
        P.S.dma("sp", xT[:], d["xT"].rearrange("(kc p) t -> p kc t", p=128), w=[xT_b])
        phase_a0(P, R, xT, xT_b, hT, hT_b, d)
        _finish(P)
    return P


def build_attn0():
    P = Prog()
    d = {
        "xout0": P.din("xout0", [2, L0_SIZE], BF16),
        "xoutf0": P.din("xoutf0", [2, 4, T], F32),
        "rel_bias": P.din("rel_bias", [4, 320], F32),
        "cmat": P.din("cmat", [128, 5, 128], BF16),
        "cmask": P.din("cmask", [128, 20, 512], BF16),
        "ebuf": P.dint("ebuf", [4, 1536], F32),
        "cumd": P.dint("cumd", [4, 2, 3, SEQ], BF16),
        "yin0": P.dout("yin0", [2, 512, T], BF16),
    }
    phase_attn0(P, d)
    _finish(P)
    return P


def build_attn1():
    P = Prog()
    d = {
        "xout1": P.din("xout1", [2, L1_SIZE], BF16),
        "cmat": P.din("cmat", [128, 5, 128], BF16),
        "cmask": P.din("cmask", [128, 20, 512], BF16),
        "yin1": P.dout("yin1", [2, 512, T], BF16),
    }
    phase_attn1(P, d)
    _finish(P)
    return P


def build_b(L, last):
    P = Prog()
    d = {
        "xT": P.din("xT", [D, T], F32),
        "yout%d" % L: P.din("yout%d" % L, [2, 512, T], BF16),
        "w_out%d" % L: P.din("w_out%d" % L, [D, D], F32),
        "norm_mlp%d" % L: P.din("norm_mlp%d" % L, [128, 8], F32),
        "w_up%d" % L: P.din("w_up%d" % L, [D, 4096], F32),
        "w_down%d" % L: P.din("w_down%d" % L, [4096, D], F32),
        "cmat": P.din("cmat", [128, 5, 128], BF16),
    }
    if not last:
        d.update({
            "norm_mix1": P.din("norm_mix1", [128, 8], F32),
            "w_in1": P.din("w_in1", [D, 2368], F32),
            "q_norm": P.din("q_norm", [128, 3], F32),
            "kv_norm": P.din("kv_norm", [128, 2], F32),
            "w_uq": P.din("w_uq", [384, 1536], F32),
            "w_ukv": P.din("w_ukv", [256, 1024], F32),
            "ropec": P.din("ropec", [96, 2], F32),
            "pos": P.din("pos", [1, T], I32),
            "xin1": P.dout("xin1", [2, L1_SIZE], BF16),
            "xT1": P.dout("xT1", [D, T], F32),
        })
    else:
        d.update({
            "norm_final": P.din("norm_final", [128, 8], F32),
            "outT": P.dout("outT", [D, T], F32),
        })
    with ExitStack() as es:
        S = P.S
        R = setup_row(P, es, d["cmat"])
        xT = P.sb(es, [128, 8, T], F32, "xT")
        xT_b = Buf()
        S.dma("sp", xT[:], d["xT"].rearrange("(kc p) t -> p kc t", p=128), w=[xT_b])
        with ExitStack() as esh:
            hT = P.sb(esh, [128, 8, T], BF16, "hT")
            hT_b = Buf()
            phase_b(P, R, esh, xT, xT_b, hT, hT_b, d, L)
            S.barrier()
        if not last:
            P.store("sp", d["xT1"].rearrange("(kc p) t -> p kc t", p=128), xT[:], [xT_b])
            phase_a1(P, R, xT, xT_b, d)
        else:
            g, gb = load_gain(P, es, S, d["norm_final"], 8)
            R.ostage = Ring([(P.sb(es, [128, 512], F32, "ostage"), Buf()) for _ in range(3)])
            rms_fm(R, xT, xT_b, 8, g, gb, None, None, D, out_dram=d["outT"])
        _finish(P)
    return P


PAIRS = [[0, 1], [2, 3], [4, 5], [6, 7]]


_XCNT = [0]


def own_copy(P, regs, src, dst, o, sz, pre_deps, nobar):
    bo = Buf()
    w_all = sz // 128
    P.S.add("pool", lambda e: e.dma_start(
        out=dst[regs["g"], o:o + sz].rearrange("(p w) -> p w", w=w_all),
        in_=src[regs["g"], o:o + sz].rearrange("(p w) -> p w", w=w_all)),
        r=pre_deps, w=[bo], dma=True, nobar=nobar)
    return bo


def exchange_start(P, regs, src, dst, ranges, dt, pre_deps, nobar, do_own=True):
    S = P.S
    CH = 128 * 8192
    chunks = []
    for (o, sz) in ranges:
        off = o
        while off < o + sz:
            n = min(CH, o + sz - off)
            chunks.append((off, n, n // 128))
            off += n
    st = []
    own = []
    for (off, n, w) in chunks:
        _XCNT[0] += 1
        bnc = P.dint("xb%d" % _XCNT[0], [128, w], dt)
        gat = P.dint("xg%d" % _XCNT[0], [256, w], dt)
        b1, b2 = Buf(), Buf()
        S.add("pool", lambda e, bnc=bnc, off=off, n=n, w=w: e.dma_start(
            out=bnc, in_=src[regs["ng"], off:off + n].rearrange("(p w) -> p w", w=w)),
            r=pre_deps, w=[b1], dma=True, nobar=nobar)
        st.append((bnc, gat, b1, b2, off, n, w))
    for (bnc, gat, b1, b2, off, n, w) in st:
        S.collective(lambda e, bnc=bnc, gat=gat: e.collective_compute(
            "AllGather", ALU.bypass, replica_groups=PAIRS, ins=[bnc.opt()], outs=[gat.opt()]),
            r=[b1], w=[b2], nobar=nobar)
    if do_own:
        for (o, sz) in ranges:
            own.append(own_copy(P, regs, src, dst, o, sz, pre_deps, nobar))
    return (st, own, dst, nobar)


def exchange_finish(P, regs, state):
    S = P.S
    st, own, dst, nobar = state
    done = list(own)
    for (bnc, gat, b1, b2, off, n, w) in st:
        gv = gat.rearrange("(r p) w -> r p w", r=2)
        b3 = Buf()
        S.add("pool", lambda e, gv=gv, off=off, n=n, w=w: e.dma_start(
            out=dst[regs["ng"], off:off + n].rearrange("(p w) -> p w", w=w), in_=gv[regs["ng"]]),
            r=[b2], w=[b3], dma=True, nobar=nobar)
        done.append(b3)
    return done


def exchange(P, regs, src, dst, size, dt, name):
    S = P.S
    S.barrier()
    P.outbufs = []
    exchange_finish(P, regs, exchange_start(P, regs, src, dst, [(0, size)], dt, [], False))
    S.barrier()
    return dst


def build_fused(upto=5, nof=False):
    P = Prog()
    S = P.S
    n0, n1 = L0_SIZE // 128, L1_SIZE // 128
    d = {
        "xT": P.din("xT", [D, T], F32),
        "gsel": P.din("gsel", [1, 2], I32),
        "cmat": P.din("cmat", [128, 5, 128], BF16),
        "cmask": P.din("cmask", [128, 20, 512], BF16),
        "norm_mix0": P.din("norm_mix0", [128, 8], F32),
        "w_in0": P.din("w_in0", [D, 3080], F32),
        "b_forget": P.din("b_forget", [8, 1], F32),
        "rel_bias": P.din("rel_bias", [4, 320], F32),
        "norm_mix1": P.din("norm_mix1", [128, 8], F32),
        "w_in1": P.din("w_in1", [D, 2368], F32),
        "q_norm": P.din("q_norm", [128, 3], F32),
        "kv_norm": P.din("kv_norm", [128, 2], F32),
        "w_uq": P.din("w_uq", [384, 1536], F32),
        "w_ukv": P.din("w_ukv", [256, 1024], F32),
        "ropec": P.din("ropec", [96, 2], F32),
        "pos": P.din("pos", [1, T], I32),
        "norm_final": P.din("norm_final", [128, 8], F32),
        "outT": P.dout("outT", [D, T], F32),
        "ebuf": P.dint("ebuf", [4, 1536], F32),
        "cumd": P.dint("cumd", [4, 2, 3, SEQ], BF16),
    }
    for L in range(2):
        d["w_out%d" % L] = P.din("w_out%d" % L, [D, D], F32)
        d["norm_mlp%d" % L] = P.din("norm_mlp%d" % L, [128, 8], F32)
        d["w_up%d" % L] = P.din("w_up%d" % L, [D, 4096], F32)
        d["w_down%d" % L] = P.din("w_down%d" % L, [4096, D], F32)
    x0 = P.dint("x_in0", [2, L0_SIZE], BF16)
    x0o = P.dint("x_out0", [2, L0_SIZE], BF16)
    f0 = P.dint("f_in0", [2, 4 * T], F32)
    f0o = P.dint("f_out0", [2, 4 * T], F32)
    x1 = P.dint("x_in1", [2, L1_SIZE], BF16)
    x1o = P.dint("x_out1", [2, L1_SIZE], BF16)
    ys = [P.dint("y_in%d" % L, [2, 512 * T], BF16) for L in range(2)]
    yos = [P.dint("y_out%d" % L, [2, 512 * T], BF16) for L in range(2)]
    d["xin0"] = x0
    d["xinf0"] = f0.rearrange("r (h t) -> r h t", h=4)
    d["xin1"] = x1
    d["yin0"] = ys[0].rearrange("r (f t) -> r f t", f=512)
    d["yin1"] = ys[1].rearrange("r (f t) -> r f t", f=512)

    regs = {}

    def setup(e):
        r0, r1 = e.alloc_register("g"), e.alloc_register("ng")
        e.reg_load(r0, d["gsel"][0:1, 0:1])
        e.reg_load(r1, d["gsel"][0:1, 1:2])
        regs["g"] = e.snap(r0, min_val=0, max_val=1)
        regs["ng"] = e.snap(r1, min_val=0, max_val=1)

    S.add("pool", setup).aux = True
    with ExitStack() as es:
        xT = P.sb(es, [128, 8, T], F32, "xT")
        xT_b = Buf()
        S.dma("sp", xT[:], d["xT"].rearrange("(kc p) t -> p kc t", p=128), w=[xT_b])
        with ExitStack() as es1:
            R = setup_row(P, es1, d["cmat"])
            hT = P.sb(es1, [128, 8, T], BF16, "hT")
            hT_b = Buf()
            phase_a0(P, R, xT, xT_b, hT, hT_b, d)
            S.barrier()
        def early_out():
            P.outbufs = []
            P.store("sp", d["outT"].rearrange("(kc p) t -> p kc t", p=128), xT[:], [xT_b])
            _finish(P)
            return P

        d["xoutf0"] = exchange(P, regs, f0, f0o, 4 * T, F32, "f0").rearrange("r (h t) -> r h t", h=4)
        d["xout0"] = x0o
        st1 = exchange_start(P, regs, x0, x0o, [(0, L0_VF)], BF16, [], True, do_own=False)
        st2 = exchange_start(P, regs, x0, x0o, [(L0_VF, L0_QKC - L0_VF)], BF16, [], True, do_own=False)
        ownb = own_copy(P, regs, x0, x0o, 0, L0_QKC, [], True)
        d["xdep0k"] = exchange_finish(P, regs, st1) + [ownb]
        d["xdep0a"] = d["xdep0k"] + exchange_finish(P, regs, st2)
        d["xdep0b"] = exchange_finish(P, regs, exchange_start(P, regs, x0, x0o, [(L0_QKC, L0_SIZE - L0_QKC)],
                                                             BF16, [], True))
        if upto == 1:
            return early_out()
        phase_attn0(P, d)
        d["yout0"] = exchange(P, regs, ys[0], yos[0], 512 * T, BF16, "y0").rearrange("r (f t) -> r f t", f=512)
        if upto == 2:
            return early_out()
        with ExitStack() as es1:
            R = setup_row(P, es1, d["cmat"])
            with ExitStack() as esh:
                hT = P.sb(esh, [128, 8, T], BF16, "hT")
                hT_b = Buf()
                phase_b(P, R, esh, xT, xT_b, hT, hT_b, d, 0)
                S.barrier()
            phase_a1(P, R, xT, xT_b, d)
            S.barrier()
            P.outbufs = []
            st1 = exchange_start(P, regs, x1, x1o, [(0, L1_SBV)], BF16, [], True, do_own=False)
            st2 = exchange_start(P, regs, x1, x1o, [(L1_SBV, L1_MQ - L1_SBV)], BF16, [], True, do_own=False)
            ownb = own_copy(P, regs, x1, x1o, 0, L1_MQ, [], True)
            d["xdep1k"] = exchange_finish(P, regs, st1) + [ownb]
            d["xdep1a"] = d["xdep1k"] + exchange_finish(P, regs, st2)
            d["xdep1b"] = exchange_finish(P, regs, exchange_start(P, regs, x1, x1o, [(L1_MQ, L1_SIZE - L1_MQ)],
                                                                 BF16, [], True))
        d["xout1"] = x1o
        phase_attn1(P, d)
        d["yout1"] = exchange(P, regs, ys[1], yos[1], 512 * T, BF16, "y1").rearrange("r (f t) -> r f t", f=512)
        with ExitStack() as es1:
            R = setup_row(P, es1, d["cmat"])
            with ExitStack() as esh:
                hT = P.sb(esh, [128, 8, T], BF16, "hT")
                hT_b = Buf()
                phase_b(P, R, esh, xT, xT_b, hT, hT_b, d, 1)
                S.barrier()
            P.outbufs = []
            gf, gfb = load_gain(P, es1, S, d["norm_final"], 8)
            R.ostage = Ring([(P.sb(es1, [128, 512], F32, "ostage"), Buf()) for _ in range(3)])
            rms_fm(R, xT, xT_b, 8, gf, gfb, None, None, D, out_dram=d["outT"])
            _finish(P)
    return P


_CACHE = {}


def _prog(key, fn):
    if key not in _CACHE:
        _CACHE[key] = fn()
    return _CACHE[key]


def kernel(x, positions, norm_mix, norm_mlp, norm_final, w_in_ab, b_forget, rel_bias, w_out_ab,
           w_in_cd, q_norm, kv_norm, w_uq, w_ukv, w_out_cd, w_up, w_down):
    f32 = lambda a: np.ascontiguousarray(np.asarray(a, dtype=np.float32))
    x = f32(x)
    cmat = _const_mats()
    cmask = _const_masks()
    ropec = _rope_consts()
    w_in0 = f32(np.asarray(w_in_ab)[0][:, _perm_w_in0()])
    w_in1 = f32(np.asarray(w_in_cd)[0][:, _perm_w_in1()])
    w_uq_p = f32(np.asarray(w_uq)[0][:, _perm_w_uq()])
    w_ukv_p = f32(np.asarray(w_ukv)[0][:, _perm_w_ukv()])
    w_out0 = f32(np.asarray(w_out_ab)[0][_perm_w_out(), :])
    w_out1 = f32(np.asarray(w_out_cd)[0][_perm_w_out(), :])
    pos = np.asarray(positions).astype(np.int32)
    gl = lambda v: f32(np.asarray(v, dtype=np.float32).reshape(-1, 128).T)
    xTs = [f32(x[c // 2, (c % 2) * T:(c % 2 + 1) * T, :].T) for c in range(NCORES)]
    if FUSED:
        P = _prog("fused", build_fused)
        rbias = np.asarray(rel_bias, dtype=np.float32)[0]
        common = {
            "cmat": cmat, "cmask": cmask, "norm_mix0": gl(norm_mix[0]), "w_in0": w_in0,
            "b_forget": f32(np.asarray(b_forget)[0].reshape(8, 1)),
            "norm_mix1": gl(norm_mix[1]), "w_in1": w_in1, "q_norm": gl(np.asarray(q_norm)[0]),
            "kv_norm": gl(np.asarray(kv_norm)[0]), "w_uq": w_uq_p, "w_ukv": w_ukv_p, "ropec": ropec,
            "norm_final": gl(norm_final), "w_out0": w_out0, "w_out1": w_out1,
            "norm_mlp0": gl(norm_mlp[0]), "norm_mlp1": gl(norm_mlp[1]),
            "w_up0": f32(np.asarray(w_up)[0]), "w_up1": f32(np.asarray(w_up)[1]),
            "w_down0": f32(np.asarray(w_down)[0]), "w_down1": f32(np.asarray(w_down)[1]),
        }
        maps = []
        for c in range(NCORES):
            g = c % 2
            m = dict(common)
            m.update({"xT": xTs[c], "gsel": np.array([[g, 1 - g]], np.int32),
                      "rel_bias": f32(rbias[4 * g:4 * g + 4]),
                      "pos": np.ascontiguousarray(pos[c // 2, g * T:(g + 1) * T].reshape(1, T))})
            maps.append(m)
        rr = _run(P, maps)
        out = np.empty((4, SEQ, D), np.float32)
        for c in range(NCORES):
            out[c // 2, (c % 2) * T:(c % 2 + 1) * T, :] = rr[c]["outT"].T
        return out

    P = _prog("a0", build_a0)
    maps = [{"xT": xTs[c], "norm_mix0": gl(norm_mix[0]), "w_in0": w_in0,
             "b_forget": f32(np.asarray(b_forget)[0].reshape(8, 1)), "cmat": cmat} for c in range(NCORES)]
    r1 = _run(P, maps)
    xout0 = _exchange([r["xin0"] for r in r1])
    xoutf0 = _exchange([r["xinf0"] for r in r1])
    P = _prog("attn0", build_attn0)
    rbias = np.asarray(rel_bias, dtype=np.float32)[0]
    maps = [{"xout0": xout0[c], "xoutf0": xoutf0[c], "rel_bias": f32(rbias[4 * (c % 2):4 * (c % 2) + 4]),
             "cmat": cmat, "cmask": cmask} for c in range(NCORES)]
    r2 = _run(P, maps)
    yout0 = _exchange([r["yin0"] for r in r2])
    P = _prog("b0", lambda: build_b(0, False))
    maps = [{"xT": xTs[c], "yout0": yout0[c], "w_out0": w_out0, "norm_mlp0": gl(norm_mlp[0]),
             "w_up0": f32(np.asarray(w_up)[0]), "w_down0": f32(np.asarray(w_down)[0]), "cmat": cmat,
             "norm_mix1": gl(norm_mix[1]), "w_in1": w_in1, "q_norm": gl(np.asarray(q_norm)[0]),
             "kv_norm": gl(np.asarray(kv_norm)[0]), "w_uq": w_uq_p, "w_ukv": w_ukv_p, "ropec": ropec,
             "pos": np.ascontiguousarray(pos[c // 2, (c % 2) * T:(c % 2 + 1) * T].reshape(1, T))}
            for c in range(NCORES)]
    r3 = _run(P, maps)
    xout1 = _exchange([r["xin1"] for r in r3])
    P = _prog("attn1", build_attn1)
    maps = [{"xout1": xout1[c], "cmat": cmat, "cmask": cmask} for c in range(NCORES)]
    r4 = _run(P, maps)
    yout1 = _exchange([r["yin1"] for r in r4])
    P = _prog("b1", lambda: build_b(1, True))
    maps = [{"xT": r3[c]["xT1"], "yout1": yout1[c], "w_out1": w_out1, "norm_mlp1": gl(norm_mlp[1]),
             "w_up1": f32(np.asarray(w_up)[1]), "w_down1": f32(np.asarray(w_down)[1]), "cmat": cmat,
             "norm_final": gl(norm_final)} for c in range(NCORES)]
    r5 = _run(P, maps)
    out = np.empty((4, SEQ, D), np.float32)
    for c in range(NCORES):
        out[c // 2, (c % 2) * T:(c % 2 + 1) * T, :] = r5[c]["outT"].T
    return out
```

```python
import numpy as np
import ml_dtypes
import concourse.bass as bass
import concourse.mybir as mybir
from concourse.bass_utils import run_bass_kernel_spmd
from contextlib import ExitStack

F32 = mybir.dt.float32
BF16 = mybir.dt.bfloat16
I32 = mybir.dt.int32
AF = mybir.ActivationFunctionType
ALU = mybir.AluOpType

NCORES = 8
FUSED = True
D = 1024
T = 2048
SEQ = 4096
NTB = T // 512
EPS = 1e-6
NEG = -30000.0
MLA_SCALE = float(96 ** -0.5)
INV_FREQ_BITS = [0x3f800000, 0x3f0ff59a, 0x3ea1e89b, 0x3e361887, 0x3dcccccd, 0x3d6655c3, 0x3d0186e2, 0x3c91ad39,
                 0x3c23d70a, 0x3bb8449c, 0x3b4f3e37, 0x3ae91528, 0x3a83126f, 0x3a136a16, 0x39a5cb5f, 0x393a7753]

L0_QKF = 0
L0_VF = L0_QKF + 512 * T
L0_QKC = L0_VF + T * 256
L0_VC = L0_QKC + 512 * T
L0_SIZE = L0_VC + T * 256
L1_SBQK = 0
L1_SBV = L1_SBQK + 512 * T
L1_MQ = L1_SBV + T * 256
L1_MKN = L1_MQ + 4 * 96 * T
L1_KR = L1_MKN + 256 * T
L1_MV = L1_KR + 32 * T
L1_SIZE = L1_MV + T * 256


class Buf:
    __slots__ = ("w", "r")

    def __init__(self):
        self.w = None
        self.r = []


class Op:
    __slots__ = ("eng", "fn", "deps", "dma", "sig", "sem", "val", "prev_use", "cc", "aux")

    def __init__(self, eng, fn, dma):
        self.eng = eng
        self.fn = fn
        self.dma = dma
        self.cc = False
        self.aux = False
        self.deps = []
        self.sig = False
        self.sem = None
        self.val = 0
        self.prev_use = None


class Sched:
    ENGS = ("pe", "act", "dve", "pool", "sp")
    NDMASEM = {"sp": 16, "pool": 8, "act": 4}

    def __init__(self, nc):
        self.nc = nc
        self.ops = {e: [] for e in self.ENGS}
        self.since_bar = []

    def add(self, eng, fn, r=(), w=(), dma=False, nobar=False):
        op = Op(eng, fn, dma)
        deps = {}
        for b in r:
            if b.w is not None:
                deps[id(b.w)] = b.w
        for b in w:
            if b.w is not None:
                deps[id(b.w)] = b.w
            for x in b.r:
                deps[id(x)] = x
        for b in r:
            b.r.append(op)
        for b in w:
            b.w = op
            b.r = []
        for d in deps.values():
            if d is op:
                continue
            if (not d.dma) and (not dma) and d.eng == "pe" and eng == "pe":
                continue
            op.deps.append(d)
            d.sig = True
        if dma:
            op.sig = True
            if not nobar:
                self.since_bar.append(op)
        self.ops[eng].append(op)
        return op

    def barrier(self):
        lasts = []
        for e in self.ENGS:
            for op in reversed(self.ops[e]):
                if (not op.dma) and op.fn is not None and not op.aux:
                    lasts.append(op)
                    break
        dmas = self.since_bar
        self.since_bar = []
        for e in self.ENGS:
            op = Op(e, None, False)
            for d in lasts:
                if d.eng != e:
                    op.deps.append(d)
                    d.sig = True
            for d in reversed(dmas):
                op.deps.append(d)
            self.ops[e].append(op)

    def mm(self, out, lhsT, rhs, start=True, stop=True, r=(), w=(), **kw):
        return self.add("pe", lambda e: e.matmul(out, lhsT, rhs, start=start, stop=stop, **kw), r, w)

    def act(self, out, in_, func, bias=None, scale=None, r=(), w=()):
        kw = {}
        if bias is not None:
            kw["bias"] = bias
        if scale is not None:
            kw["scale"] = scale
        return self.add("act", lambda e: e.activation(out, in_, func, **kw), r, w)

    def tt(self, eng, out, in0, in1, op, r=(), w=()):
        return self.add(eng, lambda e: e.tensor_tensor(out, in0, in1, op), r, w)

    def ts(self, eng, out, in0, s1, op0, s2=None, op1=None, r=(), w=()):
        if op1 is None:
            return self.add(eng, lambda e: e.tensor_scalar(out, in0, s1, None, op0), r, w)
        return self.add(eng, lambda e: e.tensor_scalar(out, in0, s1, s2, op0, op1), r, w)

    def stt(self, out, in0, scalar, in1, op0, op1, r=(), w=()):
        return self.add("dve", lambda e: e.scalar_tensor_tensor(out, in0, scalar, in1, op0, op1), r, w)

    def copy(self, eng, out, in_, r=(), w=()):
        if eng == "act":
            return self.add("act", lambda e: e.copy(out, in_), r, w)
        return self.add(eng, lambda e: e.tensor_copy(out, in_), r, w)

    def memset(self, eng, ap, val, r=(), w=()):
        return self.add(eng, lambda e: e.memset(ap, val), r, w)

    def recip(self, out, in_, r=(), w=()):
        return self.add("dve", lambda e: e.reciprocal(out, in_), r, w)

    def dma(self, q, out, in_, r=(), w=()):
        return self.add(q, lambda e: e.dma_start(out=out, in_=in_), r, w, dma=True)

    def collective(self, fn, r=(), w=(), nobar=False):
        op = self.add("pool", fn, r, w, dma=True, nobar=nobar)
        op.cc = True
        return op

    def final_wait(self, eng, bufs):
        return self.add(eng, None, r=bufs)

    def emit(self):
        nc = self.nc
        with ExitStack() as es:
            block = es.enter_context(nc.Block())
            csem = {}
            for e in ("pe", "act", "dve", "pool"):
                csem[e] = es.enter_context(nc.semaphore("c_" + e))
            ccsem = es.enter_context(nc.semaphore("cc_sem"))
            dsem = {}
            for q, n in self.NDMASEM.items():
                dsem[q] = [es.enter_context(nc.semaphore("d_%s%d" % (q, i))) for i in range(n)]
            for e in self.ENGS:
                cnt = 0
                dcnt = 0
                uses = {}
                ccnt = 0
                for op in self.ops[e]:
                    if op.cc:
                        ccnt += 1
                        op.sem = ccsem
                        op.val = ccnt
                    elif op.dma:
                        pool = dsem[e]
                        k = dcnt % len(pool)
                        dcnt += 1
                        op.sem = pool[k]
                        prev = uses.get(k)
                        op.prev_use = prev
                        op.val = (prev.val if prev is not None else 0) + 16
                        uses[k] = op
                    elif op.sig:
                        cnt += 1
                        op.sem = csem[e]
                        op.val = cnt
            sched = self

            def run(eng_name):
                def body(e):
                    waited = {}

                    def wait(sem, val):
                        key = id(sem)
                        if waited.get(key, 0) >= val:
                            return
                        waited[key] = val
                        e.wait_ge(sem, val)

                    for op in sched.ops[eng_name]:
                        for d in op.deps:
                            wait(d.sem, d.val)
                        if op.dma and (not op.cc) and op.prev_use is not None:
                            wait(op.prev_use.sem, op.prev_use.val)
                        if op.fn is None:
                            continue
                        inst = op.fn(e)
                        if op.sig:
                            if op.cc:
                                inst.then_inc(op.sem)
                            else:
                                inst.then_inc(op.sem, 16 if op.dma else 1)

                return body

            block.tensor(run("pe"))
            block.scalar(run("act"))
            block.vector(run("dve"))
            block.gpsimd(run("pool"))
            block.sync(run("sp"))
        return nc


class Ring:
    def __init__(self, items):
        self.items = items
        self.i = 0

    def next(self):
        it = self.items[self.i % len(self.items)]
        self.i += 1
        return it


class Prog:
    def __init__(self):
        self.nc = bass.Bass("TRN2", target_bir_lowering=False)
        self.S = Sched(self.nc)
        self.es = ExitStack()
        self.outbufs = []
        self.nname = 0

    def name(self, p):
        self.nname += 1
        return "%s_%d" % (p, self.nname)

    def din(self, name, shape, dt):
        return self.nc.dram_tensor(name, list(shape), dt, kind="ExternalInput").ap()

    def dout(self, name, shape, dt):
        return self.nc.dram_tensor(name, list(shape), dt, kind="ExternalOutput").ap()

    def dint(self, name, shape, dt):
        return self.nc.dram_tensor(name, list(shape), dt).ap()

    def sb(self, es, shape, dt, name="t"):
        return es.enter_context(self.nc.sbuf_tensor(self.name(name), list(shape), dt))

    def ps(self, es, shape=(128, 512), dt=F32, name="p"):
        return es.enter_context(self.nc.psum_tensor(self.name(name), list(shape), dt))

    def store(self, q, dram_ap, sb_ap, r):
        b = Buf()
        self.S.dma(q, dram_ap, sb_ap, r=r, w=[b])
        self.outbufs.append(b)
        return b


def flat(ap2, off, shape):
    n = int(np.prod(shape))
    v = ap2[off:off + n]
    if len(shape) == 2:
        return v.rearrange("(a b) -> a b", b=shape[1])
    if len(shape) == 3:
        return v.rearrange("(a b c) -> a b c", b=shape[1], c=shape[2])
    return v


class RowCtx:
    def __init__(self, P, es, cmat):
        self.P = P
        self.es = es
        self.xsq = Ring([(P.sb(es, [128, 512], BF16, "xsq"), Buf()) for _ in range(3)])
        self.rstd = Ring([(P.sb(es, [128, 512], F32, "rstd"), Buf()) for _ in range(2)])
        self.wslab = Ring([(P.sb(es, [128, 4096], BF16, "wslab"), Buf()) for _ in range(3)])
        self.mmps = Ring([(P.ps(es), Buf()) for _ in range(4)])
        self.ssps = Ring([(P.ps(es), Buf()) for _ in range(2)])
        self.stage = Ring([(P.sb(es, [128, 2048], BF16, "stg"), Buf()) for _ in range(2)])
        self.stage_s = Ring([(P.sb(es, [128, 512], BF16, "stgs"), Buf()) for _ in range(3)])
        self.cmat = cmat

    def load_slab(self, src3, nk, ncols):
        t, b = self.wslab.next()
        v = t[:, 0:nk * ncols].rearrange("p (k c) -> p k c", c=ncols)
        self.P.S.dma("pool", v, src3, w=[b])
        return v, b


def rms_block(R, src3, src_b, nk, gain, gain_b, dim, emit_out):
    S = R.P.S
    ones = R.cmat[0][:, 1, :]
    pst, psb = R.ssps.next()
    for kc in range(nk):
        xq, xqb = R.xsq.next()
        S.act(xq[:], src3[:, kc, :], AF.Square, r=[src_b], w=[xqb])
        S.mm(pst[:], ones, xq[:], start=(kc == 0), stop=(kc == nk - 1), r=[xqb, R.cmat[1]], w=[psb])
    rt, rb = R.rstd.next()
    S.act(rt[:], pst[:], AF.Ln, bias=R.eps[:, 0:1], scale=1.0 / dim, r=[psb, R.eps_b], w=[rb])
    S.act(rt[:], rt[:], AF.Exp, scale=-0.5, r=[rb], w=[rb])
    for kc in range(nk):
        emit_out(kc, rt, rb)


def rms_fm(R, src, src_b, nk, gain, gain_b, dst, dst_b, dim, out_dram=None):
    P, S = R.P, R.P.S
    for tb in range(NTB):
        ts_ = slice(tb * 512, (tb + 1) * 512)

        def emit_out(kc, rt, rb, ts_=ts_):
            if out_dram is None:
                S.stt(dst[:, kc, ts_], src[:, kc, ts_], gain[:, kc:kc + 1], rt[:], ALU.mult, ALU.mult,
                      r=[src_b, gain_b, rb], w=[dst_b])
            else:
                ot, ob = R.ostage.next()
                S.stt(ot[:], src[:, kc, ts_], gain[:, kc:kc + 1], rt[:], ALU.mult, ALU.mult,
                      r=[src_b, gain_b, rb], w=[ob])
                P.store("sp", out_dram[kc * 128:(kc + 1) * 128, ts_], ot[:], [ob])

        rms_block(R, src[:, :, ts_], src_b, nk, gain, gain_b, dim, emit_out)


def lin_fm(R, wsrc, nk, ncols, rhs, rhs_b, epilogue, mrows=None):
    S = R.P.S
    wt, wb = R.load_slab(wsrc, nk, ncols)
    chunks = []
    c0 = 0
    while c0 < ncols:
        m = min(128, ncols - c0) if mrows is None else mrows
        chunks.append((c0, m))
        c0 += m
    for ci, (c0, m) in enumerate(chunks):
        for tb in range(NTB):
            pt, pb = R.mmps.next()
            for kc in range(nk):
                S.mm(pt[0:m, :], wt[:, kc, c0:c0 + m], rhs[:, kc, tb * 512:(tb + 1) * 512],
                     start=(kc == 0), stop=(kc == nk - 1), r=[wb, rhs_b], w=[pb])
            epilogue(ci, tb, pt, pb)


def lin_tm(R, wsrc, nk, ncols, lhs, lhs_b, epilogue):
    S = R.P.S
    wt, wb = R.load_slab(wsrc, nk, ncols)
    for tt_ in range(T // 128):
        pt, pb = R.mmps.next()
        for kc in range(nk):
            S.mm(pt[:, 0:ncols], lhs[:, kc, tt_ * 128:(tt_ + 1) * 128], wt[:, kc, 0:ncols],
                 start=(kc == 0), stop=(kc == nk - 1), r=[wb, lhs_b], w=[pb])
        epilogue(tt_, pt, pb)


def wview(w2, r0, nrows, c0, ncols):
    return w2[r0:r0 + nrows, c0:c0 + ncols].rearrange("(kc p) c -> p kc c", p=128)


def store_fm_chunk(R, dram2, row0, scale):
    S = R.P.S
    state = {}

    def epi(ci, tb, pt, pb):
        if tb == 0:
            state["st"] = R.stage.next()
        st, sbuf = state["st"]
        S.act(st[:, tb * 512:(tb + 1) * 512], pt[:], AF.Copy, scale=scale(ci) if callable(scale) else scale,
              r=[pb], w=[sbuf])
        if tb == NTB - 1:
            R.P.store("sp", dram2[row0 + ci * 128: row0 + (ci + 1) * 128, :], st[:], [sbuf])

    return epi


def store_tm(R, dram2, ncols, col0=0):
    S = R.P.S

    def epi(tt_, pt, pb):
        st, sbuf = R.stage_s.next()
        S.copy("dve", st[:, 0:ncols], pt[:, 0:ncols], r=[pb], w=[sbuf])
        R.P.store("sp", dram2[tt_ * 128:(tt_ + 1) * 128, col0:col0 + ncols], st[:, 0:ncols], [sbuf])

    return epi


def load_small(P, es, S, dram, shape, dt, q="sp"):
    t = P.sb(es, shape, dt, "sm")
    b = Buf()
    S.dma(q, t[:], dram, w=[b])
    return t, b


def load_gain(P, es, S, dram2, nk):
    t = P.sb(es, [128, nk], F32, "gain")
    b = Buf()
    S.dma("sp", t[:], dram2, w=[b])
    return t, b


def setup_row(P, es, cmat_d):
    S = P.S
    cm = P.sb(es, [128, 5, 128], BF16, "cmat")
    cmb = Buf()
    S.dma("sp", cm[:], cmat_d, w=[cmb])
    R = RowCtx(P, es, (cm, cmb))
    R.eps = P.sb(es, [128, 1], F32, "eps")
    R.eps_b = Buf()
    S.memset("dve", R.eps[:], EPS, w=[R.eps_b])
    return R


def phase_a0(P, R, xT, xT_b, hT, hT_b, d):
    S, es = P.S, R.es
    g, gb = load_gain(P, es, S, d["norm_mix0"], 8)
    rms_fm(R, xT, xT_b, 8, g, gb, hT, hT_b, D)
    xin = d["xin0"]
    w = d["w_in0"]
    for s in range(4):
        grp = s // 2
        qk = flat(xin[grp], L0_QKF if s % 2 == 0 else L0_QKC, (512, T))
        lin_fm(R, wview(w, 0, 1024, s * 512, 512), 8, 512, hT, hT_b,
               store_fm_chunk(R, qk, 0, lambda ci: 0.125 if ci < 2 else 1.0))
    for grp in range(2):
        vf = flat(xin[grp], L0_VF, (T, 256))
        vc = flat(xin[grp], L0_VC, (T, 256))

        def epi_v0(tt_, pt, pb, vf=vf, vc=vc):
            st, sbuf = R.stage_s.next()
            S.copy("act", st[:], pt[:], r=[pb], w=[sbuf])
            P.store("sp", vf[tt_ * 128:(tt_ + 1) * 128, :], st[:, 0:256], [sbuf])
            P.store("sp", vc[tt_ * 128:(tt_ + 1) * 128, :], st[:, 256:512], [sbuf])

        lin_tm(R, wview(w, 0, 1024, 2048 + grp * 512, 512), 8, 512, hT, hT_b, epi_v0)
    nb, nbb = load_small(P, es, S, d["b_forget"], [8, 1], F32)
    S.ts("dve", nb[:], nb[:], -1.0, ALU.mult, r=[nbb], w=[nbb])
    lf = P.sb(es, [8, T], F32, "lf")
    lfb = Buf()

    def epi_f(ci, tb, pt, pb):
        sl = slice(tb * 512, (tb + 1) * 512)
        S.act(lf[:, sl], pt[0:8, :], AF.Exp, bias=nb[:, 0:1], scale=-1.0, r=[pb, nbb], w=[lfb])
        S.act(lf[:, sl], lf[:, sl], AF.Ln, bias=1.0, r=[lfb], w=[lfb])
        S.ts("dve", lf[:, sl], lf[:, sl], -1.0, ALU.mult, r=[lfb], w=[lfb])

    lin_fm(R, wview(w, 0, 1024, 3072, 8), 8, 8, hT, hT_b, epi_f)
    for grp in range(2):
        P.store("sp", d["xinf0"][grp], lf[grp * 4:(grp + 1) * 4, :], [lfb])


def phase_b(P, R, es, xT, xT_b, hT, hT_b, d, L):
    S = P.S
    yout = d["yout%d" % L]
    S.dma("sp", hT[:], yout.rearrange("r (c p) t -> p (r c) t", p=128), w=[hT_b])
    wo = d["w_out%d" % L]
    for s in range(2):
        def epi(ci, tb, pt, pb, s=s):
            dc = s * 4 + ci
            sl = slice(tb * 512, (tb + 1) * 512)
            S.tt("dve", xT[:, dc, sl], pt[:], xT[:, dc, sl], ALU.add, r=[pb, xT_b], w=[xT_b])

        lin_fm(R, wview(wo, 0, 1024, s * 512, 512), 8, 512, hT, hT_b, epi)
    g, gb = load_gain(P, es, S, d["norm_mlp%d" % L], 8)
    rms_fm(R, xT, xT_b, 8, g, gb, hT, hT_b, D)
    wu, wd = d["w_up%d" % L], d["w_down%d" % L]
    with ExitStack() as es2:
        acts = Ring([(P.sb(es2, [128, 4, T], BF16, "act"), Buf()) for _ in range(2)])
        relu = Ring([(P.sb(es2, [128, 512], F32, "relu"), Buf()) for _ in range(2)])

        def up(s):
            at, ab = acts.next()

            def epi(ci, tb, pt, pb):
                rt, rb = relu.next()
                S.act(rt[:], pt[:], AF.Relu, r=[pb], w=[rb])
                S.act(at[:, ci, tb * 512:(tb + 1) * 512], rt[:], AF.Square, r=[rb], w=[ab])

            lin_fm(R, wview(wu, 0, 1024, s * 512, 512), 8, 512, hT, hT_b, epi)
            return at, ab

        def down(s, at, ab):
            wv, wb = R.load_slab(wd[s * 512:(s + 1) * 512, :].rearrange("(kc p) c -> p kc c", p=128), 4, 1024)
            for dc in range(8):
                for tb in range(NTB):
                    pt, pb = R.mmps.next()
                    sl = slice(tb * 512, (tb + 1) * 512)
                    for kc in range(4):
                        S.mm(pt[:], wv[:, kc, dc * 128:(dc + 1) * 128], at[:, kc, sl],
                             start=(kc == 0), stop=(kc == 3), r=[wb, ab], w=[pb])
                    S.tt("dve", xT[:, dc, sl], pt[:], xT[:, dc, sl], ALU.add, r=[pb, xT_b], w=[xT_b])

        prev = None
        for s in range(8):
            cur = up(s)
            if prev is not None:
                down(s - 1, *prev)
            prev = cur
        down(7, *prev)
        S.barrier()


def phase_a1(P, R, xT, xT_b, d):
    S = P.S
    xin = d["xin1"]
    w = d["w_in1"]
    p = slice(64, 96)
    with ExitStack() as es2:
        cqn = P.sb(es2, [128, 3, T], BF16, "cqn")
        ckvn = P.sb(es2, [128, 2, T], BF16, "ckvn")
        tab = P.sb(es2, [96, 2, T], BF16, "ropetab")
        cqnb, ckvnb, tabb = Buf(), Buf(), Buf()
        with ExitStack() as es3:
            hT = P.sb(es3, [128, 8, T], BF16, "hT")
            hT_b = Buf()
            g, gb = load_gain(P, es3, S, d["norm_mix1"], 8)
            rms_fm(R, xT, xT_b, 8, g, gb, hT, hT_b, D)
            rc, rcb = load_small(P, es3, S, d["ropec"], [96, 2], F32)
            posi = P.sb(es3, [96, 512], I32, "posi")
            ang = P.sb(es3, [96, 512], F32, "ang")
            kk = P.sb(es3, [96, 512], F32, "kk")
            rr = P.sb(es3, [96, 512], F32, "rr")
            mm_ = P.sb(es3, [96, 512], F32, "mm")
            ab_ = Buf()
            TWO_PI = 2.0 * np.pi
            C1 = 6.28125
            C2 = float(np.float32(TWO_PI - C1))
            MAGIC = 12582912.0

            def wrap(x):
                S.ts("dve", mm_[p, :], x[p, :], float(np.pi), ALU.is_gt, r=[ab_], w=[ab_])
                S.stt(x[p, :], mm_[p, :], -TWO_PI, x[p, :], ALU.mult, ALU.add, r=[ab_], w=[ab_])
                S.ts("dve", mm_[p, :], x[p, :], -float(np.pi), ALU.is_lt, r=[ab_], w=[ab_])
                S.stt(x[p, :], mm_[p, :], TWO_PI, x[p, :], ALU.mult, ALU.add, r=[ab_], w=[ab_])
                S.ts("dve", x[p, :], x[p, :], 3.1415925, ALU.min, -3.1415925, ALU.max, r=[ab_], w=[ab_])

            for tb in range(NTB):
                sl = slice(tb * 512, (tb + 1) * 512)
                S.dma("sp", posi[p, :], bass.AP(d["pos"].tensor, tb * 512, [[0, 32], [1, 512]]), w=[ab_])
                S.copy("dve", ang[p, :], posi[p, :], r=[ab_], w=[ab_])
                S.ts("dve", ang[p, :], ang[p, :], rc[p, 0:1], ALU.mult, r=[ab_, rcb], w=[ab_])
                S.ts("dve", kk[p, :], ang[p, :], float(np.float32(1.0 / TWO_PI)), ALU.mult, r=[ab_], w=[ab_])
                S.ts("dve", kk[p, :], kk[p, :], MAGIC, ALU.add, r=[ab_], w=[ab_])
                S.ts("dve", kk[p, :], kk[p, :], MAGIC, ALU.subtract, r=[ab_], w=[ab_])
                S.stt(rr[p, :], kk[p, :], -C1, ang[p, :], ALU.mult, ALU.add, r=[ab_], w=[ab_])
                S.stt(rr[p, :], kk[p, :], -C2, rr[p, :], ALU.mult, ALU.add, r=[ab_], w=[ab_])
                wrap(rr)
                S.act(kk[p, :], rr[p, :], AF.Sin, r=[ab_], w=[ab_])
                S.ts("dve", tab[p, 1, sl], kk[p, :], rc[p, 1:2], ALU.mult, r=[ab_, rcb], w=[tabb])
                S.ts("dve", rr[p, :], rr[p, :], float(np.pi / 2), ALU.add, r=[ab_], w=[ab_])
                wrap(rr)
                S.act(tab[p, 0, sl], rr[p, :], AF.Sin, r=[ab_], w=[tabb])
            for s_ in range(2):
                qk = flat(xin[s_], L1_SBQK, (512, T))
                lin_fm(R, wview(w, 0, 1024, s_ * 512, 512), 8, 512, hT, hT_b,
                       store_fm_chunk(R, qk, 0, lambda ci: 0.125 if ci < 2 else 1.0))

            def epi_v(tt_, pt, pb):
                st, sbuf = R.stage_s.next()
                S.copy("act", st[:], pt[:], r=[pb], w=[sbuf])
                for grp in range(2):
                    vd = flat(xin[grp], L1_SBV, (T, 256))
                    P.store("sp", vd[tt_ * 128:(tt_ + 1) * 128, :], st[:, grp * 256:(grp + 1) * 256], [sbuf])

            lin_tm(R, wview(w, 0, 1024, 1024, 512), 8, 512, hT, hT_b, epi_v)
            if "after_sb" in d:
                d["after_sb"]()
            gq, gqb = load_gain(P, es3, S, d["q_norm"], 3)
            gk, gkb = load_gain(P, es3, S, d["kv_norm"], 2)
            cqblk = P.sb(es3, [128, 3, 512], F32, "cqblk")
            ckvblk = P.sb(es3, [128, 2, 512], F32, "ckvblk")
            cqbb, ckvbb = Buf(), Buf()
            t1r = Ring([(P.sb(es3, [96, 512], F32, "t1"), Buf()) for _ in range(2)])
            kro = P.sb(es3, [96, T], BF16, "kro")
            krob = Buf()
            wq_, wqb = R.load_slab(wview(w, 0, 1024, 1536, 384), 8, 384)
            wk_, wkb = R.load_slab(wview(w, 0, 1024, 1920, 448), 8, 448)
            for tb in range(NTB):
                sl = slice(tb * 512, (tb + 1) * 512)

                def proj(wt, wb, c0, m):
                    pt, pb = R.mmps.next()
                    for kc in range(8):
                        S.mm(pt[0:m, :], wt[:, kc, c0:c0 + m], hT[:, kc, sl], start=(kc == 0), stop=(kc == 7),
                             r=[wb, hT_b], w=[pb])
                    return pt, pb

                for ci in range(3):
                    pt, pb = proj(wq_, wqb, ci * 128, 128)
                    S.copy("act", cqblk[:, ci, :], pt[:], r=[pb], w=[cqbb])

                def out_q(kc, rt, rb, sl=sl):
                    S.stt(cqn[:, kc, sl], cqblk[:, kc, :], gq[:, kc:kc + 1], rt[:], ALU.mult, ALU.mult,
                          r=[cqbb, gqb, rb], w=[cqnb])

                rms_block(R, cqblk, cqbb, 3, gq, gqb, 384, out_q)
                for ci in range(2):
                    pt, pb = proj(wk_, wkb, ci * 128, 128)
                    S.copy("act", ckvblk[:, ci, :], pt[:], r=[pb], w=[ckvbb])

                def out_kv(kc, rt, rb, sl=sl):
                    S.stt(ckvn[:, kc, sl], ckvblk[:, kc, :], gk[:, kc:kc + 1], rt[:], ALU.mult, ALU.mult,
                          r=[ckvbb, gkb, rb], w=[ckvnb])

                rms_block(R, ckvblk, ckvbb, 2, gk, gkb, 256, out_kv)
                pa, pab = proj(wk_, wkb, 256, 96)
                pbt, pbb = proj(wk_, wkb, 352, 96)
                t1, t1b = t1r.next()
                t2, t2b = t1r.next()
                S.tt("dve", t1[p, :], pa[p, :], tab[p, 0, sl], ALU.mult, r=[pab, tabb], w=[t1b])
                S.tt("dve", t2[p, :], pbt[p, :], tab[p, 1, sl], ALU.mult, r=[pbb, tabb], w=[t2b])
                S.tt("dve", kro[p, sl], t1[p, :], t2[p, :], ALU.add, r=[t1b, t2b], w=[krob])
            for grp in range(2):
                P.store("sp", flat(xin[grp], L1_KR, (32, T)), kro[p, :], [krob])
            S.barrier()
        with ExitStack() as es3:
            wq = d["w_uq"]
            qst = Ring([(P.sb(es3, [96, T], BF16, "qst"), Buf()) for _ in range(2)])
            t1r = Ring([(P.sb(es3, [96, 512], F32, "t1"), Buf()) for _ in range(2)])
            for hp in range(4):
                wt, wb = R.load_slab(wview(wq, 0, 384, hp * 384, 384), 3, 384)
                for hh in range(2):
                    h = hp * 2 + hh
                    st, stb = qst.next()
                    for tb in range(NTB):
                        sl = slice(tb * 512, (tb + 1) * 512)
                        pa, pab = R.mmps.next()
                        pbt, pbb = R.mmps.next()
                        for kc in range(3):
                            S.mm(pa[0:96, :], wt[:, kc, hh * 192:hh * 192 + 96], cqn[:, kc, sl],
                                 start=(kc == 0), stop=(kc == 2), r=[wb, cqnb], w=[pab])
                        for kc in range(3):
                            S.mm(pbt[0:96, :], wt[:, kc, hh * 192 + 96:hh * 192 + 192], cqn[:, kc, sl],
                                 start=(kc == 0), stop=(kc == 2), r=[wb, cqnb], w=[pbb])
                        S.act(st[0:64, sl], pa[0:64, :], AF.Copy, scale=MLA_SCALE, r=[pab], w=[stb])
                        t1, t1b = t1r.next()
                        t2, t2b = t1r.next()
                        S.stt(t1[p, :], pa[p, :], MLA_SCALE, tab[p, 0, sl], ALU.mult, ALU.mult, r=[pab, tabb], w=[t1b])
                        S.stt(t2[p, :], pbt[p, :], MLA_SCALE, tab[p, 1, sl], ALU.mult, ALU.mult, r=[pbb, tabb], w=[t2b])
                        S.tt("dve", st[p, sl], t1[p, :], t2[p, :], ALU.add, r=[t1b, t2b], w=[stb])
                    grp, hl = h // 4, h % 4
                    mq = flat(xin[grp], L1_MQ, (4, 96, T))
                    P.store("sp", mq[hl], st[:], [stb])
            wkv = d["w_ukv"]
            for grp in range(2):
                kn = flat(xin[grp], L1_MKN, (256, T))
                lin_fm(R, wview(wkv, 0, 256, grp * 256, 256), 2, 256, ckvn, ckvnb,
                       store_fm_chunk(R, kn, 0, 1.0))

            def epi_mv(tt_, pt, pb):
                st, sbuf = R.stage_s.next()
                S.copy("act", st[:], pt[:], r=[pb], w=[sbuf])
                for grp in range(2):
                    vd = flat(xin[grp], L1_MV, (T, 256))
                    P.store("sp", vd[tt_ * 128:(tt_ + 1) * 128, :], st[:, grp * 256:(grp + 1) * 256], [sbuf])

            lin_tm(R, wview(wkv, 0, 256, 512, 512), 2, 512, ckvn, ckvnb, epi_mv)
            S.barrier()


class AttCtx:
    def __init__(self, P, es, d, ntiles_mask, mask_first):
        S = P.S
        self.P, self.es = P, es
        self.cm = P.sb(es, [128, 5, 128], BF16, "cmat")
        self.cmb = Buf()
        S.dma("sp", self.cm[:], d["cmat"], w=[self.cmb])
        self.masks = P.sb(es, [128, ntiles_mask, 512], BF16, "masks")
        self.maskb = Buf()
        S.dma("sp", self.masks[:], d["cmask"][:, mask_first:mask_first + ntiles_mask, :], w=[self.maskb])
        self.kq = Ring([((P.sb(es, [128, SEQ], BF16, "kt"), P.sb(es, [128, SEQ], BF16, "qt")),
                         [Buf() for _ in range(4)]) for _ in range(2)])
        self.sps = Ring([(P.ps(es), Buf()) for _ in range(4)])
        self.nps = Ring([(P.ps(es), Buf()) for _ in range(2)])
        self.dps = Ring([(P.ps(es), Buf()) for _ in range(1)])
        self.pt = Ring([(P.sb(es, [128, 512], BF16, "pT"), Buf()) for _ in range(3)])
        self.ost = Ring([(P.sb(es, [65, SEQ], BF16, "ost"), Buf()) for _ in range(2)])
        self.rc = Ring([(P.sb(es, [65, 512], F32, "rc"), Buf()) for _ in range(1)])
        self.dn = Ring([(P.sb(es, [1, 512], F32, "dn"), Buf()) for _ in range(1)])
        self.rcpf = Ring([(P.sb(es, [128, 512], F32, "rcpf"), Buf()) for _ in range(1)])
        S.memset("dve", self.rcpf.items[0][0][:], 0.0, w=[self.rcpf.items[0][1]])
        self.row0f = P.sb(es, [128, 128], F32, "row0f")
        self.row0fb = Buf()
        S.memset("dve", self.row0f[:], 0.0, w=[self.row0fb])
        S.memset("dve", self.row0f[0:1, :], 1.0, w=[self.row0fb])


def load_v(A, xout, off, ncols, dep=()):
    P, S = A.P, A.P.S
    nh = ncols // 64
    n = 32 * nh * 66
    vf = P.sb(A.es, [128, n + 64], BF16, "v")
    vb = [Buf() for _ in range(6)]
    S.memset("dve", vf[:, n:n + 64], 0.0, w=[vb[0]])
    v4 = vf[:, 0:n].rearrange("p (j h c) -> p j h c", h=nh, c=66)
    S.memset("dve", v4[:, :, :, 0:1], 1.0, w=[vb[1]])
    i = 0
    for r in range(2):
        for h in range(nh):
            src = flat(xout[r], off, (T, ncols))[:, h * 64:(h + 1) * 64].rearrange("(j p) c -> p j c", p=128)
            S.dma("sp", v4[:, r * 16:(r + 1) * 16, h, 1:65], src, r=list(dep), w=[vb[2 + i % 4]])
            i += 1

    def lhs(kb, h):
        b = (kb * nh + h) * 66
        return vf[:, b:b + 128]

    return lhs, vb


def softmax_head(A, kt, qt, kqb, krows, v, vb, vh, tiles, yin, hrow):
    P, S = A.P, A.P.S
    J = A.cm[:, 0, :]
    ost, ostb = A.ost.next()
    n = len(tiles)
    sts = [None] * n

    def issue_s(i):
        qb, kb, m, mb, first, last, (c0, c1) = tiles[i]
        st, sb_ = A.sps.next()
        sts[i] = (st, sb_)
        S.mm(st[:, c0:c1], kt[:, kb * 128:(kb + 1) * 128], qt[:, qb * 512 + c0:qb * 512 + c1],
             start=True, stop=(m is None), r=list(kqb), w=[sb_])
        if m is not None:
            S.mm(st[:, c0:c1], J, m[:, c0:c1], start=False, stop=True, r=[A.cmb, mb], w=[sb_])

    acc = {}
    pending = []

    def flush(upto):
        while pending and pending[0][0] <= upto:
            pending.pop(0)[1]()

    LA = 3
    for i0 in range(min(LA, n)):
        issue_s(i0)
    for i in range(n):
        if i + LA < n:
            issue_s(i + LA)
        qb, kb, m, mb, first, last, (c0, c1) = tiles[i]
        st, sb_ = sts[i]
        pt, ptb = A.pt.next()
        S.act(pt[:, c0:c1], st[:, c0:c1], AF.Exp, r=[sb_], w=[ptb])
        if first:
            assert (c0, c1) == (0, 512)
            acc["n"] = A.nps.next()
        nt, nb = acc["n"]
        S.mm(nt[:, c0:c1], v(kb, vh), pt[:, c0:c1], start=first, stop=last, r=list(vb) + [ptb], w=[nb],
             skip_group_check=True)
        flush(i)
        if last:
            dn, dnb = A.dn.next()
            S.act(dn[0:1, :], nt[0:1, :], AF.Ln, r=[nb], w=[dnb])
            rf, rfb = A.rcpf.next()
            S.act(rf[0:1, :], dn[0:1, :], AF.Exp, scale=-1.0, r=[dnb], w=[rfb])

            def fin(nt=nt, nb=nb, rf=rf, rfb=rfb, qb=qb):
                bt, bb = A.dps.next()
                S.mm(bt[:], A.row0f[:], rf[:], start=True, stop=True, r=[A.row0fb, rfb], w=[bb])
                rc, rcb = A.rc.next()
                S.copy("dve", rc[:], bt[0:65, :], r=[bb], w=[rcb])
                S.tt("dve", ost[:, qb * 512:(qb + 1) * 512], nt[0:65, :], rc[:], ALU.mult, r=[nb, rcb], w=[ostb])

            pending.append((i + 6, fin))
    flush(n + 100)
    for r in range(2):
        P.store("pool", yin[r, hrow:hrow + 64, :], ost[1:65, r * T:(r + 1) * T], [ostb])


def load_kq_pair(A, xout, offk, nk_total, krow0, offq, nq_total, qrow0, kt, qt, kbuf, qbuf, dep=()):
    S = A.P.S
    for r in range(2):
        srck = flat(xout[r], offk, (nk_total, T))[krow0:krow0 + 64, :]
        S.dma("sp", kt[0:64, r * T:(r + 1) * T], srck, r=list(dep), w=[kbuf])
        srcq = flat(xout[r], offq, (nq_total, T))[qrow0:qrow0 + 64, :]
        S.dma("sp", qt[0:64, r * T:(r + 1) * T], srcq, r=list(dep), w=[qbuf])


def load_kq_rows(A, xout, off, nrows_total, row0, nrows, kt_or_qt, dst_row0, buf, dep=()):
    S = A.P.S
    for r in range(2):
        src = flat(xout[r], off, (nrows_total, T))[row0:row0 + nrows, :]
        S.dma("sp", kt_or_qt[dst_row0:dst_row0 + nrows, r * T:(r + 1) * T], src, r=list(dep), w=[buf])


def phase_attn0(P, d):
    S = P.S
    xout, yin = d["xout0"], d["yin0"]
    cumd = d["cumd"]
    Ed = d["ebuf"]
    cumdb, Edb = Buf(), Buf()
    with ExitStack() as es:
        lf = P.sb(es, [4, SEQ], F32, "lf")
        lfb = Buf()
        for r in range(2):
            S.dma("sp", lf[:, r * T:(r + 1) * T], d["xoutf0"][r], w=[lfb])
        onesf = P.sb(es, [4, SEQ], F32, "onesf")
        S.memset("dve", onesf[:], 1.0, w=[lfb])
        cum = P.sb(es, [4, SEQ], F32, "cum")
        S.add("dve", lambda e: e.tensor_tensor_scan(cum[:], onesf[:], lf[:], 0.0, ALU.mult, ALU.add), r=[lfb], w=[lfb])
        c3 = P.sb(es, [4, 2, 3, SEQ], BF16, "c3")
        c3b = Buf()
        S.copy("dve", c3[:, 1, 0, :], cum[:], r=[lfb], w=[c3b])
        S.tt("dve", cum[:], cum[:], c3[:, 1, 0, :], ALU.subtract, r=[lfb, c3b], w=[lfb])
        S.copy("dve", c3[:, 1, 1, :], cum[:], r=[lfb], w=[c3b])
        S.tt("dve", cum[:], cum[:], c3[:, 1, 1, :], ALU.subtract, r=[lfb, c3b], w=[lfb])
        S.copy("dve", c3[:, 1, 2, :], cum[:], r=[lfb], w=[c3b])
        for i in range(3):
            S.ts("dve", c3[:, 0, i, :], c3[:, 1, i, :], -1.0, ALU.mult, r=[c3b], w=[c3b])
        S.dma("sp", cumd, c3[:], r=[c3b], w=[cumdb])
        rb, rbb = load_small(P, es, S, d["rel_bias"], [4, 320], F32)
        E = P.sb(es, [4, 1536], F32, "E")
        Eb = Buf()
        S.memset("dve", E[:], 0.0, w=[Eb])
        S.ts("dve", E[:, 0:449], E[:, 0:449], rb[:, 0:1], ALU.add, r=[rbb, Eb], w=[Eb])
        S.copy("dve", E[:, 449:767], rb[:, 1:319], r=[rbb, Eb], w=[Eb])
        S.ts("dve", E[:, 767:1536], E[:, 767:1536], rb[:, 319:320], ALU.add, r=[rbb, Eb], w=[Eb])
        S.dma("sp", Ed, E[:], r=[Eb], w=[Edb])
        S.barrier()
    with ExitStack() as es:
        A = AttCtx(P, es, d, 12, 0)
        depa, depb = d.get("xdep0a", []), d.get("xdep0b", [])
        bias = P.sb(es, [128, 32, 512], BF16, "bias")
        biasb = Buf()
        hkw = P.sb(es, [128, 1408], F32, "hkw")
        hkwb = Buf()
        VF = {}

        def build_bias():
            for h in range(4):
                src = bass.AP(Ed.tensor, h * 1536, [[1, 128], [1, 1408]])
                S.dma("sp", hkw[:], src, r=[Edb], w=[hkwb])
                for jb in range(8):
                    o = 896 - 128 * jb
                    S.tt("dve", bias[:, h * 8 + jb, :], hkw[:, o:o + 512], A.masks[:, 4 + jb, :], ALU.add,
                         r=[hkwb, A.maskb], w=[biasb])

        def load_head(j):
            (kt, qt), kqb = A.kq.next()
            kB, kxB, qB, qxB = kqb
            S.memset("dve", kt[64:128, :], 0.0, w=[kxB])
            S.memset("dve", qt[64:128, :], 0.0, w=[qxB])
            if j < 4:
                load_kq_pair(A, xout, L0_QKF, 512, 256 + j * 64, L0_QKF, 512, j * 64, kt, qt, kB, qB,
                             d.get("xdep0k", depa))
                S.memset("dve", kt[64:70, :], 1.0, w=[kxB])
                S.memset("dve", qt[64:70, :], 1.0, w=[qxB])
                S.dma("sp", kt[64:67, :], cumd[j, 0], r=[cumdb], w=[kxB])
                S.dma("sp", qt[67:70, :], cumd[j, 1], r=[cumdb], w=[qxB])
            else:
                jj = j - 4
                load_kq_pair(A, xout, L0_QKC, 512, 256 + jj * 64, L0_QKC, 512, jj * 64, kt, qt, kB, qB, depb)
            return kt, qt, kqb

        def run_head(j, kt, qt, kqb):
            tiles = []
            if j < 4:
                for qb in range(8):
                    nk = 4 * qb + 4
                    for kb in range(nk):
                        jm = kb - 4 * qb
                        m = A.masks[:, jm, :] if jm >= 0 else None
                        tiles.append((qb, kb, m, A.maskb, kb == 0, kb == nk - 1, (128 * max(jm, 0), 512)))
                softmax_head(A, kt, qt, kqb, 70, VF["v"], VF["b"], j, tiles, yin, j * 64)
            else:
                jj = j - 4
                CR = {0: (0, 128), 1: (0, 256), 2: (0, 384), 3: (0, 512), 4: (0, 512), 5: (128, 512),
                      6: (256, 512), 7: (384, 512)}
                for qb in range(8):
                    jbs = [jb for jb in (3, 4, 0, 1, 2, 5, 6, 7) if 4 * qb - 4 + jb >= 0]
                    for jb in jbs:
                        kb = 4 * qb - 4 + jb
                        tiles.append((qb, kb, bias[:, jj * 8 + jb, :], biasb, jb == jbs[0], jb == jbs[-1], CR[jb]))
                softmax_head(A, kt, qt, kqb, 64, VC["v"], VC["b"], jj, tiles, yin, 256 + jj * 64)

        VC = {}
        nxt = load_head(0)
        VF["v"], VF["b"] = load_v(A, xout, L0_VF, 256, depa)
        build_bias()
        for j in range(8):
            cur = nxt
            if j == 2:
                VC["v"], VC["b"] = load_v(A, xout, L0_VC, 256, depb)
            if j + 1 < 8:
                nxt = load_head(j + 1)
            run_head(j, *cur)
        S.barrier()


def sb_head(A, kt, qt, kqb, v, vb, vcol, yin, hrow, X):
    P, S = A.P, A.P.S
    J = A.cm[:, 0, :]
    NTRI = A.cm[:, 2, :]
    NONES = A.cm[:, 3, :]
    ost, ostb = A.ost.next()
    tiles = []
    for qb in range(8):
        kbs = list(range(4 * qb + 3, -1, -1))
        for kb in kbs:
            jm = kb - 4 * qb
            tiles.append((qb, kb, jm, kb == kbs[0], kb == kbs[-1]))
    n = len(tiles)
    zs, sps, rss, es_ = [None] * n, [None] * n, [None] * n, [None] * n
    acc = {}

    def stage1(i):
        qb, kb, jm, first, last = tiles[i]
        c0 = 128 * max(jm, 0)
        zt, zb = A.sps.next()
        zs[i] = (zt, zb)
        S.mm(zt[:, c0:], kt[:, kb * 128:(kb + 1) * 128], qt[:, qb * 512 + c0:(qb + 1) * 512],
             start=True, stop=False, r=list(kqb), w=[zb])
        if jm >= 0:
            S.mm(zt[:, c0:], J, A.masks[:, jm, c0:], start=False, stop=False, r=[A.cmb, A.maskb], w=[zb])
        et, eb = X["e"].next()
        S.act(et[:, c0:], zt[:, c0:], AF.Exp, r=[zb], w=[eb])
        es_[i] = (et, eb)

    def stage1b(i):
        qb, kb, jm, first, last = tiles[i]
        c0 = 128 * max(jm, 0)
        et, eb = es_[i]
        spt, spb = X["sp"].next()
        sps[i] = (spt, spb)
        S.act(spt[:, c0:], et[:, c0:], AF.Ln, bias=1.0, r=[eb], w=[spb])
        rt, rb = X["rs"].next()
        rss[i] = (rt, rb)
        if c0 > 0:
            S.memset("dve", rt[:, 0:c0], 0.0, w=[rb])
        if first:
            S.copy("dve", rt[:, c0:], spt[:, c0:], r=[spb], w=[rb])
        else:
            pr, prb = rss[i - 1]
            S.tt("dve", rt[:, c0:], pr[:, c0:], spt[:, c0:], ALU.add, r=[prb, spb], w=[rb])

    def stage2(i):
        qb, kb, jm, first, last = tiles[i]
        c0 = 128 * max(jm, 0)
        zt, zb = zs[i]
        spt, spb = sps[i]
        S.mm(zt[:, c0:], NTRI, spt[:, c0:], start=False, stop=first, r=[A.cmb, spb], w=[zb])
        if not first:
            pr, prb = rss[i - 1]
            S.mm(zt[:, c0:], NONES, pr[:, c0:], start=False, stop=True, r=[A.cmb, prb], w=[zb])
        pt, ptb = A.pt.next()
        sps[i] = (pt, ptb)
        if first and c0 > 0:
            S.memset("dve", pt[:, 0:c0], 0.0, w=[ptb])
        S.act(pt[:, c0:], zt[:, c0:], AF.Exp, r=[zb], w=[ptb])

    def stage3(i):
        qb, kb, jm, first, last = tiles[i]
        c0 = 0 if first else 128 * max(jm, 0)
        pt, ptb = sps[i]
        if first:
            acc["n"] = A.nps.next()
        nt, nb = acc["n"]
        S.mm(nt[:, c0:], v(kb, vcol), pt[:, c0:], start=first, stop=last, r=list(vb) + [ptb], w=[nb],
             skip_group_check=True)
        if last:
            S.copy("dve", ost[:, qb * 512:(qb + 1) * 512], nt[0:65, :], r=[nb], w=[ostb])

    for step in range(-1, n + 2):
        if 0 <= step + 1 < n:
            stage1(step + 1)
        if 0 <= step < n:
            stage1b(step)
        if 0 <= step - 1 < n:
            stage2(step - 1)
        if 0 <= step - 2 < n:
            stage3(step - 2)
    for r in range(2):
        P.store("pool", yin[r, hrow:hrow + 64, :], ost[1:65, r * T:(r + 1) * T], [ostb])


def phase_attn1(P, d):
    S = P.S
    with ExitStack() as es:
        A = AttCtx(P, es, d, 8, 12)
        xout, yin = d["xout1"], d["yin1"]
        dep1, dep2 = d.get("xdep1a", []), d.get("xdep1b", [])
        VS = {}
        VM = {}
        X = {
            "e": Ring([A.dps.items[0], (P.ps(es), Buf())]),
            "sp": Ring([(P.sb(es, [128, 512], BF16, "sp"), Buf()) for _ in range(3)]),
            "rs": Ring([(P.sb(es, [128, 512], BF16, "rs"), Buf()) for _ in range(4)]),
        }

        def load_head(j):
            (kt, qt), kqb = A.kq.next()
            kB, kxB, qB, qxB = kqb
            S.memset("dve", kt[64:128, :], 0.0, w=[kxB])
            S.memset("dve", qt[64:128, :], 0.0, w=[qxB])
            if j < 4:
                load_kq_pair(A, xout, L1_SBQK, 512, 256 + j * 64, L1_SBQK, 512, j * 64, kt, qt, kB, qB,
                             d.get("xdep1k", dep1))
            else:
                jj = j - 4
                for r in range(2):
                    srck = flat(xout[r], L1_MKN, (256, T))[jj * 64:jj * 64 + 64, :]
                    S.dma("sp", kt[0:64, r * T:(r + 1) * T], srck, r=list(dep2), w=[kB])
                    src = flat(xout[r], L1_MQ, (4, 96, T))[jj]
                    S.dma("sp", qt[0:96, r * T:(r + 1) * T], src, r=list(dep2), w=[qB, qxB])
                load_kq_rows(A, xout, L1_KR, 32, 0, 32, kt, 64, kxB, dep2)
            return kt, qt, kqb

        def run_head(j, kt, qt, kqb):
            if j < 4:
                sb_head(A, kt, qt, kqb, VS["v"], VS["b"], j, yin, j * 64, X)
            else:
                jj = j - 4
                tiles = []
                for qb in range(8):
                    nk = 4 * qb + 4
                    for kb in range(nk):
                        jm = kb - 4 * qb
                        m = A.masks[:, 4 + jm, :] if jm >= 0 else None
                        tiles.append((qb, kb, m, A.maskb, kb == 0, kb == nk - 1, (128 * max(jm, 0), 512)))
                softmax_head(A, kt, qt, kqb, 96, VM["v"], VM["b"], jj, tiles, yin, 256 + jj * 64)

        nxt = load_head(0)
        VS["v"], VS["b"] = load_v(A, xout, L1_SBV, 256, dep1)
        for j in range(8):
            cur = nxt
            if j == 2:
                VM["v"], VM["b"] = load_v(A, xout, L1_MV, 256, dep2)
            if j + 1 < 8:
                nxt = load_head(j + 1)
            run_head(j, *cur)
        S.barrier()


def _perm_w_in0():
    cols = []
    for grp in range(2):
        hs = range(4 * grp, 4 * grp + 4)
        for base in (0, 512, 1544, 2056):
            for h in hs:
                cols += list(range(base + h * 64, base + h * 64 + 64))
    for grp in range(2):
        hs = range(4 * grp, 4 * grp + 4)
        for base in (1024, 2568):
            for h in hs:
                cols += list(range(base + h * 64, base + h * 64 + 64))
    cols += list(range(1536, 1544))
    return np.array(cols)


def _perm_w_in1():
    cols = []
    for grp in range(2):
        hs = range(4 * grp, 4 * grp + 4)
        for base in (0, 512):
            for h in hs:
                cols += list(range(base + h * 64, base + h * 64 + 64))
    for grp in range(2):
        for h in range(4 * grp, 4 * grp + 4):
            cols += list(range(1024 + h * 64, 1024 + h * 64 + 64))
    cols += list(range(1536, 1920))
    cols += list(range(1920, 2176))
    kr = list(range(2176, 2208))
    krp = kr[16:] + kr[:16]
    cols += list(range(1920, 1984)) + kr
    cols += list(range(1920, 1984)) + krp
    return np.array(cols)


def _perm_w_uq():
    cols = []
    for h in range(8):
        b = h * 96
        nope = list(range(b, b + 64))
        rope = list(range(b + 64, b + 96))
        cols += nope + rope + nope + rope[16:] + rope[:16]
    return np.array(cols)


def _perm_w_ukv():
    cols = []
    for grp in range(2):
        for h in range(4 * grp, 4 * grp + 4):
            cols += list(range(h * 128, h * 128 + 64))
    for grp in range(2):
        for h in range(4 * grp, 4 * grp + 4):
            cols += list(range(h * 128 + 64, h * 128 + 128))
    return np.array(cols)


def _perm_w_out():
    rows = []
    for grp in range(2):
        for base in (0, 512):
            for h in range(4 * grp, 4 * grp + 4):
                rows += list(range(base + h * 64, base + h * 64 + 64))
    return np.array(rows)


def _const_masks():
    kk = np.arange(128)[:, None]
    qq = np.arange(512)[None, :]
    tiles = []
    for j in range(4):
        tiles.append(np.where(128 * j + kk <= qq, 0.0, NEG))
    for jb in range(8):
        v = (qq // 64) + 8 - 2 * jb - (kk // 64)
        tiles.append(np.where((v >= 0) & (v <= 8), 0.0, NEG))
    for j in range(4):
        tiles.append(np.where(128 * j + kk < qq, 0.0, NEG))
    for j in range(4):
        tiles.append(np.where((128 * j + kk) // 64 <= qq // 64, 0.0, NEG))
    m = np.stack(tiles, axis=1).astype(np.float32)
    m = m[::-1].copy()
    return m.astype(ml_dtypes.bfloat16)


def _const_mats():
    i = np.arange(128)
    J = (i[:, None] + i[None, :] == 127).astype(np.float32)
    ones = np.ones((128, 128), np.float32)
    ntri = -(i[:, None] >= i[None, :]).astype(np.float32)
    nones = -ones
    row0 = np.zeros((128, 128), np.float32)
    row0[0, :] = 1.0
    return np.stack([J, ones, ntri, nones, row0], axis=1).astype(ml_dtypes.bfloat16)


def _rope_consts():
    inv = np.array(INV_FREQ_BITS, dtype=np.uint32).view(np.float32)
    c = np.zeros((96, 2), np.float32)
    c[64:96, 0] = np.concatenate([inv, inv])
    c[64:96, 1] = np.concatenate([-np.ones(16, np.float32), np.ones(16, np.float32)])
    return c


def _run(P, in_maps):
    res = run_bass_kernel_spmd(P.nc, in_maps, core_ids=list(range(NCORES)))
    return res.results


def _finish(P):
    P.S.final_wait("sp", P.outbufs)
    P.S.emit()


def _exchange(xin_list):
    out = []
    for c in range(NCORES):
        b, g = c // 2, c % 2
        out.append(np.stack([xin_list[2 * b + r][g] for r in range(2)], axis=0))
    return out


def build_a0():
    P = Prog()
    d = {
        "xT": P.din("xT", [D, T], F32),
        "norm_mix0": P.din("norm_mix0", [128, 8], F32),
        "w_in0": P.din("w_in0", [D, 3080], F32),
        "b_forget": P.din("b_forget", [8, 1], F32),
        "cmat": P.din("cmat", [128, 5, 128], BF16),
        "xin0": P.dout("xin0", [2, L0_SIZE], BF16),
        "xinf0": P.dout("xinf0", [2, 4, T], F32),
    }
    with ExitStack() as es:
        R = setup_row(P, es, d["cmat"])
        xT = P.sb(es, [128, 8, T], F32, "xT")
        hT = P.sb(es, [128, 8, T], BF16, "hT")
        xT_b, hT_b = Buf(), Buf()
        P.S.dma("sp", xT[:], d["xT"].rearrange("(kc p) t -> p kc t", p=128), w=[xT_b])
        phase_a0(P, R, xT, xT_b, hT, hT_b, d)
        _finish(P)
    return P


def build_attn0():
    P = Prog()
    d = {
        "xout0": P.din("xout0", [2, L0_SIZE], BF16),
        "xoutf0": P.din("xoutf0", [2, 4, T], F32),
        "rel_bias": P.din("rel_bias", [4, 320], F32),
        "cmat": P.din("cmat", [128, 5, 128], BF16),
        "cmask": P.din("cmask", [128, 20, 512], BF16),
        "ebuf": P.dint("ebuf", [4, 1536], F32),
        "cumd": P.dint("cumd", [4, 2, 3, SEQ], BF16),
        "yin0": P.dout("yin0", [2, 512, T], BF16),
    }
    phase_attn0(P, d)
    _finish(P)
    return P


def build_attn1():
    P = Prog()
    d = {
        "xout1": P.din("xout1", [2, L1_SIZE], BF16),
        "cmat": P.din("cmat", [128, 5, 128], BF16),
        "cmask": P.din("cmask", [128, 20, 512], BF16),
        "yin1": P.dout("yin1", [2, 512, T], BF16),
    }
    phase_attn1(P, d)
    _finish(P)
    return P


def build_b(L, last):
    P = Prog()
    d = {
        "xT": P.din("xT", [D, T], F32),
        "yout%d" % L: P.din("yout%d" % L, [2, 512, T], BF16),
        "w_out%d" % L: P.din("w_out%d" % L, [D, D], F32),
        "norm_mlp%d" % L: P.din("norm_mlp%d" % L, [128, 8], F32),
        "w_up%d" % L: P.din("w_up%d" % L, [D, 4096], F32),
        "w_down%d" % L: P.din("w_down%d" % L, [4096, D], F32),
        "cmat": P.din("cmat", [128, 5, 128], BF16),
    }
    if not last:
        d.update({
            "norm_mix1": P.din("norm_mix1", [128, 8], F32),
            "w_in1": P.din("w_in1", [D, 2368], F32),
            "q_norm": P.din("q_norm", [128, 3], F32),
            "kv_norm": P.din("kv_norm", [128, 2], F32),
            "w_uq": P.din("w_uq", [384, 1536], F32),
            "w_ukv": P.din("w_ukv", [256, 1024], F32),
            "ropec": P.din("ropec", [96, 2], F32),
            "pos": P.din("pos", [1, T], I32),
            "xin1": P.dout("xin1", [2, L1_SIZE], BF16),
            "xT1": P.dout("xT1", [D, T], F32),
        })
    else:
        d.update({
            "norm_final": P.din("norm_final", [128, 8], F32),
            "outT": P.dout("outT", [D, T], F32),
        })
    with ExitStack() as es:
        S = P.S
        R = setup_row(P, es, d["cmat"])
        xT = P.sb(es, [128, 8, T], F32, "xT")
        xT_b = Buf()
        S.dma("sp", xT[:], d["xT"].rearrange("(kc p) t -> p kc t", p=128), w=[xT_b])
        with ExitStack() as esh:
            hT = P.sb(esh, [128, 8, T], BF16, "hT")
            hT_b = Buf()
            phase_b(P, R, esh, xT, xT_b, hT, hT_b, d, L)
            S.barrier()
        if not last:
            P.store("sp", d["xT1"].rearrange("(kc p) t -> p kc t", p=128), xT[:], [xT_b])
            phase_a1(P, R, xT, xT_b, d)
        else:
            g, gb = load_gain(P, es, S, d["norm_final"], 8)
            R.ostage = Ring([(P.sb(es, [128, 512], F32, "ostage"), Buf()) for _ in range(3)])
            rms_fm(R, xT, xT_b, 8, g, gb, None, None, D, out_dram=d["outT"])
        _finish(P)
    return P


PAIRS = [[0, 1], [2, 3], [4, 5], [6, 7]]


_XCNT = [0]


def own_copy(P, regs, src, dst, o, sz, pre_deps, nobar):
    bo = Buf()
    w_all = sz // 128
    P.S.add("pool", lambda e: e.dma_start(
        out=dst[regs["g"], o:o + sz].rearrange("(p w) -> p w", w=w_all),
        in_=src[regs["g"], o:o + sz].rearrange("(p w) -> p w", w=w_all)),
        r=pre_deps, w=[bo], dma=True, nobar=nobar)
    return bo


def exchange_start(P, regs, src, dst, ranges, dt, pre_deps, nobar, do_own=True):
    S = P.S
    CH = 128 * 8192
    chunks = []
    for (o, sz) in ranges:
        off = o
        while off < o + sz:
            n = min(CH, o + sz - off)
            chunks.append((off, n, n // 128))
            off += n
    st = []
    own = []
    for (off, n, w) in chunks:
        _XCNT[0] += 1
        bnc = P.dint("xb%d" % _XCNT[0], [128, w], dt)
        gat = P.dint("xg%d" % _XCNT[0], [256, w], dt)
        b1, b2 = Buf(), Buf()
        S.add("pool", lambda e, bnc=bnc, off=off, n=n, w=w: e.dma_start(
            out=bnc, in_=src[regs["ng"], off:off + n].rearrange("(p w) -> p w", w=w)),
            r=pre_deps, w=[b1], dma=True, nobar=nobar)
        st.append((bnc, gat, b1, b2, off, n, w))
    for (bnc, gat, b1, b2, off, n, w) in st:
        S.collective(lambda e, bnc=bnc, gat=gat: e.collective_compute(
            "AllGather", ALU.bypass, replica_groups=PAIRS, ins=[bnc.opt()], outs=[gat.opt()]),
            r=[b1], w=[b2], nobar=nobar)
    if do_own:
        for (o, sz) in ranges:
            own.append(own_copy(P, regs, src, dst, o, sz, pre_deps, nobar))
    return (st, own, dst, nobar)


def exchange_finish(P, regs, state):
    S = P.S
    st, own, dst, nobar = state
    done = list(own)
    for (bnc, gat, b1, b2, off, n, w) in st:
        gv = gat.rearrange("(r p) w -> r p w", r=2)
        b3 = Buf()
        S.add("pool", lambda e, gv=gv, off=off, n=n, w=w: e.dma_start(
            out=dst[regs["ng"], off:off + n].rearrange("(p w) -> p w", w=w), in_=gv[regs["ng"]]),
            r=[b2], w=[b3], dma=True, nobar=nobar)
        done.append(b3)
    return done


def exchange(P, regs, src, dst, size, dt, name):
    S = P.S
    S.barrier()
    P.outbufs = []
    exchange_finish(P, regs, exchange_start(P, regs, src, dst, [(0, size)], dt, [], False))
    S.barrier()
    return dst


def build_fused(upto=5, nof=False):
    P = Prog()
    S = P.S
    n0, n1 = L0_SIZE // 128, L1_SIZE // 128
    d = {
        "xT": P.din("xT", [D, T], F32),
        "gsel": P.din("gsel", [1, 2], I32),
        "cmat": P.din("cmat", [128, 5, 128], BF16),
        "cmask": P.din("cmask", [128, 20, 512], BF16),
        "norm_mix0": P.din("norm_mix0", [128, 8], F32),
        "w_in0": P.din("w_in0", [D, 3080], F32),
        "b_forget": P.din("b_forget", [8, 1], F32),
        "rel_bias": P.din("rel_bias", [4, 320], F32),
        "norm_mix1": P.din("norm_mix1", [128, 8], F32),
        "w_in1": P.din("w_in1", [D, 2368], F32),
        "q_norm": P.din("q_norm", [128, 3], F32),
        "kv_norm": P.din("kv_norm", [128, 2], F32),
        "w_uq": P.din("w_uq", [384, 1536], F32),
        "w_ukv": P.din("w_ukv", [256, 1024], F32),
        "ropec": P.din("ropec", [96, 2], F32),
        "pos": P.din("pos", [1, T], I32),
        "norm_final": P.din("norm_final", [128, 8], F32),
        "outT": P.dout("outT", [D, T], F32),
        "ebuf": P.dint("ebuf", [4, 1536], F32),
        "cumd": P.dint("cumd", [4, 2, 3, SEQ], BF16),
    }
    for L in range(2):
        d["w_out%d" % L] = P.din("w_out%d" % L, [D, D], F32)
        d["norm_mlp%d" % L] = P.din("norm_mlp%d" % L, [128, 8], F32)
        d["w_up%d" % L] = P.din("w_up%d" % L, [D, 4096], F32)
        d["w_down%d" % L] = P.din("w_down%d" % L, [4096, D], F32)
    x0 = P.dint("x_in0", [2, L0_SIZE], BF16)
    x0o = P.dint("x_out0", [2, L0_SIZE], BF16)
    f0 = P.dint("f_in0", [2, 4 * T], F32)
    f0o = P.dint("f_out0", [2, 4 * T], F32)
    x1 = P.dint("x_in1", [2, L1_SIZE], BF16)
    x1o = P.dint("x_out1", [2, L1_SIZE], BF16)
    ys = [P.dint("y_in%d" % L, [2, 512 * T], BF16) for L in range(2)]
    yos = [P.dint("y_out%d" % L, [2, 512 * T], BF16) for L in range(2)]
    d["xin0"] = x0
    d["xinf0"] = f0.rearrange("r (h t) -> r h t", h=4)
    d["xin1"] = x1
    d["yin0"] = ys[0].rearrange("r (f t) -> r f t", f=512)
    d["yin1"] = ys[1].rearrange("r (f t) -> r f t", f=512)

    regs = {}

    def setup(e):
        r0, r1 = e.alloc_register("g"), e.alloc_register("ng")
        e.reg_load(r0, d["gsel"][0:1, 0:1])
        e.reg_load(r1, d["gsel"][0:1, 1:2])
        regs["g"] = e.snap(r0, min_val=0, max_val=1)
        regs["ng"] = e.snap(r1, min_val=0, max_val=1)

    S.add("pool", setup).aux = True
    with ExitStack() as es:
        xT = P.sb(es, [128, 8, T], F32, "xT")
        xT_b = Buf()
        S.dma("sp", xT[:], d["xT"].rearrange("(kc p) t -> p kc t", p=128), w=[xT_b])
        with ExitStack() as es1:
            R = setup_row(P, es1, d["cmat"])
            hT = P.sb(es1, [128, 8, T], BF16, "hT")
            hT_b = Buf()
            phase_a0(P, R, xT, xT_b, hT, hT_b, d)
            S.barrier()
        def early_out():
            P.outbufs = []
            P.store("sp", d["outT"].rearrange("(kc p) t -> p kc t", p=128), xT[:], [xT_b])
            _finish(P)
            return P

        d["xoutf0"] = exchange(P, regs, f0, f0o, 4 * T, F32, "f0").rearrange("r (h t) -> r h t", h=4)
        d["xout0"] = x0o
        st1 = exchange_start(P, regs, x0, x0o, [(0, L0_VF)], BF16, [], True, do_own=False)
        st2 = exchange_start(P, regs, x0, x0o, [(L0_VF, L0_QKC - L0_VF)], BF16, [], True, do_own=False)
        ownb = own_copy(P, regs, x0, x0o, 0, L0_QKC, [], True)
        d["xdep0k"] = exchange_finish(P, regs, st1) + [ownb]
        d["xdep0a"] = d["xdep0k"] + exchange_finish(P, regs, st2)
        d["xdep0b"] = exchange_finish(P, regs, exchange_start(P, regs, x0, x0o, [(L0_QKC, L0_SIZE - L0_QKC)],
                                                             BF16, [], True))
        if upto == 1:
            return early_out()
        phase_attn0(P, d)
        d["yout0"] = exchange(P, regs, ys[0], yos[0], 512 * T, BF16, "y0").rearrange("r (f t) -> r f t", f=512)
        if upto == 2:
            return early_out()
        with ExitStack() as es1:
            R = setup_row(P, es1, d["cmat"])
            with ExitStack() as esh:
                hT = P.sb(esh, [128, 8, T], BF16, "hT")
                hT_b = Buf()
                phase_b(P, R, esh, xT, xT_b, hT, hT_b, d, 0)
                S.barrier()
            phase_a1(P, R, xT, xT_b, d)
            S.barrier()
            P.outbufs = []
            st1 = exchange_start(P, regs, x1, x1o, [(0, L1_SBV)], BF16, [], True, do_own=False)
            st2 = exchange_start(P, regs, x1, x1o, [(L1_SBV, L1_MQ - L1_SBV)], BF16, [], True, do_own=False)
            ownb = own_copy(P, regs, x1, x1o, 0, L1_MQ, [], True)
            d["xdep1k"] = exchange_finish(P, regs, st1) + [ownb]
            d["xdep1a"] = d["xdep1k"] + exchange_finish(P, regs, st2)
            d["xdep1b"] = exchange_finish(P, regs, exchange_start(P, regs, x1, x1o, [(L1_MQ, L1_SIZE - L1_MQ)],
                                                                 BF16, [], True))
        d["xout1"] = x1o
        phase_attn1(P, d)
        d["yout1"] = exchange(P, regs, ys[1], yos[1], 512 * T, BF16, "y1").rearrange("r (f t) -> r f t", f=512)
        with ExitStack() as es1:
            R = setup_row(P, es1, d["cmat"])
            with ExitStack() as esh:
                hT = P.sb(esh, [128, 8, T], BF16, "hT")
                hT_b = Buf()
                phase_b(P, R, esh, xT, xT_b, hT, hT_b, d, 1)
                S.barrier()
            P.outbufs = []
            gf, gfb = load_gain(P, es1, S, d["norm_final"], 8)
            R.ostage = Ring([(P.sb(es1, [128, 512], F32, "ostage"), Buf()) for _ in range(3)])
            rms_fm(R, xT, xT_b, 8, gf, gfb, None, None, D, out_dram=d["outT"])
            _finish(P)
    return P


_CACHE = {}


def _prog(key, fn):
    if key not in _CACHE:
        _CACHE[key] = fn()
    return _CACHE[key]


def kernel(x, positions, norm_mix, norm_mlp, norm_final, w_in_ab, b_forget, rel_bias, w_out_ab,
           w_in_cd, q_norm, kv_norm, w_uq, w_ukv, w_out_cd, w_up, w_down):
    f32 = lambda a: np.ascontiguousarray(np.asarray(a, dtype=np.float32))
    x = f32(x)
    cmat = _const_mats()
    cmask = _const_masks()
    ropec = _rope_consts()
    w_in0 = f32(np.asarray(w_in_ab)[0][:, _perm_w_in0()])
    w_in1 = f32(np.asarray(w_in_cd)[0][:, _perm_w_in1()])
    w_uq_p = f32(np.asarray(w_uq)[0][:, _perm_w_uq()])
    w_ukv_p = f32(np.asarray(w_ukv)[0][:, _perm_w_ukv()])
    w_out0 = f32(np.asarray(w_out_ab)[0][_perm_w_out(), :])
    w_out1 = f32(np.asarray(w_out_cd)[0][_perm_w_out(), :])
    pos = np.asarray(positions).astype(np.int32)
    gl = lambda v: f32(np.asarray(v, dtype=np.float32).reshape(-1, 128).T)
    xTs = [f32(x[c // 2, (c % 2) * T:(c % 2 + 1) * T, :].T) for c in range(NCORES)]
    if FUSED:
        P = _prog("fused", build_fused)
        rbias = np.asarray(rel_bias, dtype=np.float32)[0]
        common = {
            "cmat": cmat, "cmask": cmask, "norm_mix0": gl(norm_mix[0]), "w_in0": w_in0,
            "b_forget": f32(np.asarray(b_forget)[0].reshape(8, 1)),
            "norm_mix1": gl(norm_mix[1]), "w_in1": w_in1, "q_norm": gl(np.asarray(q_norm)[0]),
            "kv_norm": gl(np.asarray(kv_norm)[0]), "w_uq": w_uq_p, "w_ukv": w_ukv_p, "ropec": ropec,
            "norm_final": gl(norm_final), "w_out0": w_out0, "w_out1": w_out1,
            "norm_mlp0": gl(norm_mlp[0]), "norm_mlp1": gl(norm_mlp[1]),
            "w_up0": f32(np.asarray(w_up)[0]), "w_up1": f32(np.asarray(w_up)[1]),
            "w_down0": f32(np.asarray(w_down)[0]), "w_down1": f32(np.asarray(w_down)[1]),
        }
        maps = []
        for c in range(NCORES):
            g = c % 2
            m = dict(common)
            m.update({"xT": xTs[c], "gsel": np.array([[g, 1 - g]], np.int32),
                      "rel_bias": f32(rbias[4 * g:4 * g + 4]),
                      "pos": np.ascontiguousarray(pos[c // 2, g * T:(g + 1) * T].reshape(1, T))})
            maps.append(m)
        rr = _run(P, maps)
        out = np.empty((4, SEQ, D), np.float32)
        for c in range(NCORES):
            out[c // 2, (c % 2) * T:(c % 2 + 1) * T, :] = rr[c]["outT"].T
        return out

    P = _prog("a0", build_a0)
    maps = [{"xT": xTs[c], "norm_mix0": gl(norm_mix[0]), "w_in0": w_in0,
             "b_forget": f32(np.asarray(b_forget)[0].reshape(8, 1)), "cmat": cmat} for c in range(NCORES)]
    r1 = _run(P, maps)
    xout0 = _exchange([r["xin0"] for r in r1])
    xoutf0 = _exchange([r["xinf0"] for r in r1])
    P = _prog("attn0", build_attn0)
    rbias = np.asarray(rel_bias, dtype=np.float32)[0]
    maps = [{"xout0": xout0[c], "xoutf0": xoutf0[c], "rel_bias": f32(rbias[4 * (c % 2):4 * (c % 2) + 4]),
             "cmat": cmat, "cmask": cmask} for c in range(NCORES)]
    r2 = _run(P, maps)
    yout0 = _exchange([r["yin0"] for r in r2])
    P = _prog("b0", lambda: build_b(0, False))
    maps = [{"xT": xTs[c], "yout0": yout0[c], "w_out0": w_out0, "norm_mlp0": gl(norm_mlp[0]),
             "w_up0": f32(np.asarray(w_up)[0]), "w_down0": f32(np.asarray(w_down)[0]), "cmat": cmat,
             "norm_mix1": gl(norm_mix[1]), "w_in1": w_in1, "q_norm": gl(np.asarray(q_norm)[0]),
             "kv_norm": gl(np.asarray(kv_norm)[0]), "w_uq": w_uq_p, "w_ukv": w_ukv_p, "ropec": ropec,
             "pos": np.ascontiguousarray(pos[c // 2, (c % 2) * T:(c % 2 + 1) * T].reshape(1, T))}
            for c in range(NCORES)]
    r3 = _run(P, maps)
    xout1 = _exchange([r["xin1"] for r in r3])
    P = _prog("attn1", build_attn1)
    maps = [{"xout1": xout1[c], "cmat": cmat, "cmask": cmask} for c in range(NCORES)]
    r4 = _run(P, maps)
    yout1 = _exchange([r["yin1"] for r in r4])
    P = _prog("b1", lambda: build_b(1, True))
    maps = [{"xT": r3[c]["xT1"], "yout1": yout1[c], "w_out1": w_out1, "norm_mlp1": gl(norm_mlp[1]),
             "w_up1": f32(np.asarray(w_up)[1]), "w_down1": f32(np.asarray(w_down)[1]), "cmat": cmat,
             "norm_final": gl(norm_final)} for c in range(NCORES)]
    r5 = _run(P, maps)
    out = np.empty((4, SEQ, D), np.float32)
    for c in range(NCORES):
        out[c // 2, (c % 2) * T:(c % 2 + 1) * T, :] = r5[c]["outT"].T
    return out
```

```python
import numpy as np
import ml_dtypes
import concourse.bass as bass
import concourse.mybir as mybir
from concourse.bass_utils import run_bass_kernel_spmd
from contextlib import ExitStack

F32 = mybir.dt.float32
BF16 = mybir.dt.bfloat16
I32 = mybir.dt.int32
AF = mybir.ActivationFunctionType
ALU = mybir.AluOpType

NCORES = 8
FUSED = True
D = 1024
T = 2048
SEQ = 4096
NTB = T // 512
EPS = 1e-6
NEG = -30000.0
MLA_SCALE = float(96 ** -0.5)
INV_FREQ_BITS = [0x3f800000, 0x3f0ff59a, 0x3ea1e89b, 0x3e361887, 0x3dcccccd, 0x3d6655c3, 0x3d0186e2, 0x3c91ad39,
                 0x3c23d70a, 0x3bb8449c, 0x3b4f3e37, 0x3ae91528, 0x3a83126f, 0x3a136a16, 0x39a5cb5f, 0x393a7753]

L0_QKF = 0
L0_VF = L0_QKF + 512 * T
L0_QKC = L0_VF + T * 256
L0_VC = L0_QKC + 512 * T
L0_SIZE = L0_VC + T * 256
L1_SBQK = 0
L1_SBV = L1_SBQK + 512 * T
L1_MQ = L1_SBV + T * 256
L1_MKN = L1_MQ + 4 * 96 * T
L1_KR = L1_MKN + 256 * T
L1_MV = L1_KR + 32 * T
L1_SIZE = L1_MV + T * 256


class Buf:
    __slots__ = ("w", "r")

    def __init__(self):
        self.w = None
        self.r = []


class Op:
    __slots__ = ("eng", "fn", "deps", "dma", "sig", "sem", "val", "prev_use", "cc", "aux")

    def __init__(self, eng, fn, dma):
        self.eng = eng
        self.fn = fn
        self.dma = dma
        self.cc = False
        self.aux = False
        self.deps = []
        self.sig = False
        self.sem = None
        self.val = 0
        self.prev_use = None


class Sched:
    ENGS = ("pe", "act", "dve", "pool", "sp")
    NDMASEM = {"sp": 16, "pool": 8, "act": 4}

    def __init__(self, nc):
        self.nc = nc
        self.ops = {e: [] for e in self.ENGS}
        self.since_bar = []

    def add(self, eng, fn, r=(), w=(), dma=False, nobar=False):
        op = Op(eng, fn, dma)
        deps = {}
        for b in r:
            if b.w is not None:
                deps[id(b.w)] = b.w
        for b in w:
            if b.w is not None:
                deps[id(b.w)] = b.w
            for x in b.r:
                deps[id(x)] = x
        for b in r:
            b.r.append(op)
        for b in w:
            b.w = op
            b.r = []
        for d in deps.values():
            if d is op:
                continue
            if (not d.dma) and (not dma) and d.eng == "pe" and eng == "pe":
                continue
            op.deps.append(d)
            d.sig = True
        if dma:
            op.sig = True
            if not nobar:
                self.since_bar.append(op)
        self.ops[eng].append(op)
        return op

    def barrier(self):
        lasts = []
        for e in self.ENGS:
            for op in reversed(self.ops[e]):
                if (not op.dma) and op.fn is not None and not op.aux:
                    lasts.append(op)
                    break
        dmas = self.since_bar
        self.since_bar = []
        for e in self.ENGS:
            op = Op(e, None, False)
            for d in lasts:
                if d.eng != e:
                    op.deps.append(d)
                    d.sig = True
            for d in reversed(dmas):
                op.deps.append(d)
            self.ops[e].append(op)

    def mm(self, out, lhsT, rhs, start=True, stop=True, r=(), w=(), **kw):
        return self.add("pe", lambda e: e.matmul(out, lhsT, rhs, start=start, stop=stop, **kw), r, w)

    def act(self, out, in_, func, bias=None, scale=None, r=(), w=()):
        kw = {}
        if bias is not None:
            kw["bias"] = bias
        if scale is not None:
            kw["scale"] = scale
        return self.add("act", lambda e: e.activation(out, in_, func, **kw), r, w)

    def tt(self, eng, out, in0, in1, op, r=(), w=()):
        return self.add(eng, lambda e: e.tensor_tensor(out, in0, in1, op), r, w)

    def ts(self, eng, out, in0, s1, op0, s2=None, op1=None, r=(), w=()):
        if op1 is None:
            return self.add(eng, lambda e: e.tensor_scalar(out, in0, s1, None, op0), r, w)
        return self.add(eng, lambda e: e.tensor_scalar(out, in0, s1, s2, op0, op1), r, w)

    def stt(self, out, in0, scalar, in1, op0, op1, r=(), w=()):
        return self.add("dve", lambda e: e.scalar_tensor_tensor(out, in0, scalar, in1, op0, op1), r, w)

    def copy(self, eng, out, in_, r=(), w=()):
        if eng == "act":
            return self.add("act", lambda e: e.copy(out, in_), r, w)
        return self.add(eng, lambda e: e.tensor_copy(out, in_), r, w)

    def memset(self, eng, ap, val, r=(), w=()):
        return self.add(eng, lambda e: e.memset(ap, val), r, w)

    def recip(self, out, in_, r=(), w=()):
        return self.add("dve", lambda e: e.reciprocal(out, in_), r, w)

    def dma(self, q, out, in_, r=(), w=()):
        return self.add(q, lambda e: e.dma_start(out=out, in_=in_), r, w, dma=True)

    def collective(self, fn, r=(), w=(), nobar=False):
        op = self.add("pool", fn, r, w, dma=True, nobar=nobar)
        op.cc = True
        return op

    def final_wait(self, eng, bufs):
        return self.add(eng, None, r=bufs)

    def emit(self):
        nc = self.nc
        with ExitStack() as es:
            block = es.enter_context(nc.Block())
            csem = {}
            for e in ("pe", "act", "dve", "pool"):
                csem[e] = es.enter_context(nc.semaphore("c_" + e))
            ccsem = es.enter_context(nc.semaphore("cc_sem"))
            dsem = {}
            for q, n in self.NDMASEM.items():
                dsem[q] = [es.enter_context(nc.semaphore("d_%s%d" % (q, i))) for i in range(n)]
            for e in self.ENGS:
                cnt = 0
                dcnt = 0
                uses = {}
                ccnt = 0
                for op in self.ops[e]:
                    if op.cc:
                        ccnt += 1
                        op.sem = ccsem
                        op.val = ccnt
                    elif op.dma:
                        pool = dsem[e]
                        k = dcnt % len(pool)
                        dcnt += 1
                        op.sem = pool[k]
                        prev = uses.get(k)
                        op.prev_use = prev
                        op.val = (prev.val if prev is not None else 0) + 16
                        uses[k] = op
                    elif op.sig:
                        cnt += 1
                        op.sem = csem[e]
                        op.val = cnt
            sched = self

            def run(eng_name):
                def body(e):
                    waited = {}

                    def wait(sem, val):
                        key = id(sem)
                        if waited.get(key, 0) >= val:
                            return
                        waited[key] = val
                        e.wait_ge(sem, val)

                    for op in sched.ops[eng_name]:
                        for d in op.deps:
                            wait(d.sem, d.val)
                        if op.dma and (not op.cc) and op.prev_use is not None:
                            wait(op.prev_use.sem, op.prev_use.val)
                        if op.fn is None:
                            continue
                        inst = op.fn(e)
                        if op.sig:
                            if op.cc:
                                inst.then_inc(op.sem)
                            else:
                                inst.then_inc(op.sem, 16 if op.dma else 1)

                return body

            block.tensor(run("pe"))
            block.scalar(run("act"))
            block.vector(run("dve"))
            block.gpsimd(run("pool"))
            block.sync(run("sp"))
        return nc


class Ring:
    def __init__(self, items):
        self.items = items
        self.i = 0

    def next(self):
        it = self.items[self.i % len(self.items)]
        self.i += 1
        return it


class Prog:
    def __init__(self):
        self.nc = bass.Bass("TRN2", target_bir_lowering=False)
        self.S = Sched(self.nc)
        self.es = ExitStack()
        self.outbufs = []
        self.nname = 0

    def name(self, p):
        self.nname += 1
        return "%s_%d" % (p, self.nname)

    def din(self, name, shape, dt):
        return self.nc.dram_tensor(name, list(shape), dt, kind="ExternalInput").ap()

    def dout(self, name, shape, dt):
        return self.nc.dram_tensor(name, list(shape), dt, kind="ExternalOutput").ap()

    def dint(self, name, shape, dt):
        return self.nc.dram_tensor(name, list(shape), dt).ap()

    def sb(self, es, shape, dt, name="t"):
        return es.enter_context(self.nc.sbuf_tensor(self.name(name), list(shape), dt))

    def ps(self, es, shape=(128, 512), dt=F32, name="p"):
        return es.enter_context(self.nc.psum_tensor(self.name(name), list(shape), dt))

    def store(self, q, dram_ap, sb_ap, r):
        b = Buf()
        self.S.dma(q, dram_ap, sb_ap, r=r, w=[b])
        self.outbufs.append(b)
        return b


def flat(ap2, off, shape):
    n = int(np.prod(shape))
    v = ap2[off:off + n]
    if len(shape) == 2:
        return v.rearrange("(a b) -> a b", b=shape[1])
    if len(shape) == 3:
        return v.rearrange("(a b c) -> a b c", b=shape[1], c=shape[2])
    return v


class RowCtx:
    def __init__(self, P, es, cmat):
        self.P = P
        self.es = es
        self.xsq = Ring([(P.sb(es, [128, 512], BF16, "xsq"), Buf()) for _ in range(3)])
        self.rstd = Ring([(P.sb(es, [128, 512], F32, "rstd"), Buf()) for _ in range(2)])
        self.wslab = Ring([(P.sb(es, [128, 4096], BF16, "wslab"), Buf()) for _ in range(3)])
        self.mmps = Ring([(P.ps(es), Buf()) for _ in range(4)])
        self.ssps = Ring([(P.ps(es), Buf()) for _ in range(2)])
        self.stage = Ring([(P.sb(es, [128, 2048], BF16, "stg"), Buf()) for _ in range(2)])
        self.stage_s = Ring([(P.sb(es, [128, 512], BF16, "stgs"), Buf()) for _ in range(3)])
        self.cmat = cmat

    def load_slab(self, src3, nk, ncols):
        t, b = self.wslab.next()
        v = t[:, 0:nk * ncols].rearrange("p (k c) -> p k c", c=ncols)
        self.P.S.dma("pool", v, src3, w=[b])
        return v, b


def rms_block(R, src3, src_b, nk, gain, gain_b, dim, emit_out):
    S = R.P.S
    ones = R.cmat[0][:, 1, :]
    pst, psb = R.ssps.next()
    for kc in range(nk):
        xq, xqb = R.xsq.next()
        S.act(xq[:], src3[:, kc, :], AF.Square, r=[src_b], w=[xqb])
        S.mm(pst[:], ones, xq[:], start=(kc == 0), stop=(kc == nk - 1), r=[xqb, R.cmat[1]], w=[psb])
    rt, rb = R.rstd.next()
    S.act(rt[:], pst[:], AF.Ln, bias=R.eps[:, 0:1], scale=1.0 / dim, r=[psb, R.eps_b], w=[rb])
    S.act(rt[:], rt[:], AF.Exp, scale=-0.5, r=[rb], w=[rb])
    for kc in range(nk):
        emit_out(kc, rt, rb)


def rms_fm(R, src, src_b, nk, gain, gain_b, dst, dst_b, dim, out_dram=None):
    P, S = R.P, R.P.S
    for tb in range(NTB):
        ts_ = slice(tb * 512, (tb + 1) * 512)

        def emit_out(kc, rt, rb, ts_=ts_):
            if out_dram is None:
                S.stt(dst[:, kc, ts_], src[:, kc, ts_], gain[:, kc:kc + 1], rt[:], ALU.mult, ALU.mult,
                      r=[src_b, gain_b, rb], w=[dst_b])
            else:
                ot, ob = R.ostage.next()
                S.stt(ot[:], src[:, kc, ts_], gain[:, kc:kc + 1], rt[:], ALU.mult, ALU.mult,
                      r=[src_b, gain_b, rb], w=[ob])
                P.store("sp", out_dram[kc * 128:(kc + 1) * 128, ts_], ot[:], [ob])

        rms_block(R, src[:, :, ts_], src_b, nk, gain, gain_b, dim, emit_out)


def lin_fm(R, wsrc, nk, ncols, rhs, rhs_b, epilogue, mrows=None):
    S = R.P.S
    wt, wb = R.load_slab(wsrc, nk, ncols)
    chunks = []
    c0 = 0
    while c0 < ncols:
        m = min(128, ncols - c0) if mrows is None else mrows
        chunks.append((c0, m))
        c0 += m
    for ci, (c0, m) in enumerate(chunks):
        for tb in range(NTB):
            pt, pb = R.mmps.next()
            for kc in range(nk):
                S.mm(pt[0:m, :], wt[:, kc, c0:c0 + m], rhs[:, kc, tb * 512:(tb + 1) * 512],
                     start=(kc == 0), stop=(kc == nk - 1), r=[wb, rhs_b], w=[pb])
            epilogue(ci, tb, pt, pb)


def lin_tm(R, wsrc, nk, ncols, lhs, lhs_b, epilogue):
    S = R.P.S
    wt, wb = R.load_slab(wsrc, nk, ncols)
    for tt_ in range(T // 128):
        pt, pb = R.mmps.next()
        for kc in range(nk):
            S.mm(pt[:, 0:ncols], lhs[:, kc, tt_ * 128:(tt_ + 1) * 128], wt[:, kc, 0:ncols],
                 start=(kc == 0), stop=(kc == nk - 1), r=[wb, lhs_b], w=[pb])
        epilogue(tt_, pt, pb)


def wview(w2, r0, nrows, c0, ncols):
    return w2[r0:r0 + nrows, c0:c0 + ncols].rearrange("(kc p) c -> p kc c", p=128)


def store_fm_chunk(R, dram2, row0, scale):
    S = R.P.S
    state = {}

    def epi(ci, tb, pt, pb):
        if tb == 0:
            state["st"] = R.stage.next()
        st, sbuf = state["st"]
        S.act(st[:, tb * 512:(tb + 1) * 512], pt[:], AF.Copy, scale=scale(ci) if callable(scale) else scale,
              r=[pb], w=[sbuf])
        if tb == NTB - 1:
            R.P.store("sp", dram2[row0 + ci * 128: row0 + (ci + 1) * 128, :], st[:], [sbuf])

    return epi


def store_tm(R, dram2, ncols, col0=0):
    S = R.P.S

    def epi(tt_, pt, pb):
        st, sbuf = R.stage_s.next()
        S.copy("dve", st[:, 0:ncols], pt[:, 0:ncols], r=[pb], w=[sbuf])
        R.P.store("sp", dram2[tt_ * 128:(tt_ + 1) * 128, col0:col0 + ncols], st[:, 0:ncols], [sbuf])

    return epi


def load_small(P, es, S, dram, shape, dt, q="sp"):
    t = P.sb(es, shape, dt, "sm")
    b = Buf()
    S.dma(q, t[:], dram, w=[b])
    return t, b


def load_gain(P, es, S, dram2, nk):
    t = P.sb(es, [128, nk], F32, "gain")
    b = Buf()
    S.dma("sp", t[:], dram2, w=[b])
    return t, b


def setup_row(P, es, cmat_d):
    S = P.S
    cm = P.sb(es, [128, 5, 128], BF16, "cmat")
    cmb = Buf()
    S.dma("sp", cm[:], cmat_d, w=[cmb])
    R = RowCtx(P, es, (cm, cmb))
    R.eps = P.sb(es, [128, 1], F32, "eps")
    R.eps_b = Buf()
    S.memset("dve", R.eps[:], EPS, w=[R.eps_b])
    return R


def phase_a0(P, R, xT, xT_b, hT, hT_b, d):
    S, es = P.S, R.es
    g, gb = load_gain(P, es, S, d["norm_mix0"], 8)
    rms_fm(R, xT, xT_b, 8, g, gb, hT, hT_b, D)
    xin = d["xin0"]
    w = d["w_in0"]
    for s in range(4):
        grp = s // 2
        qk = flat(xin[grp], L0_QKF if s % 2 == 0 else L0_QKC, (512, T))
        lin_fm(R, wview(w, 0, 1024, s * 512, 512), 8, 512, hT, hT_b,
               store_fm_chunk(R, qk, 0, lambda ci: 0.125 if ci < 2 else 1.0))
    for grp in range(2):
        vf = flat(xin[grp], L0_VF, (T, 256))
        vc = flat(xin[grp], L0_VC, (T, 256))

        def epi_v0(tt_, pt, pb, vf=vf, vc=vc):
            st, sbuf = R.stage_s.next()
            S.copy("act", st[:], pt[:], r=[pb], w=[sbuf])
            P.store("sp", vf[tt_ * 128:(tt_ + 1) * 128, :], st[:, 0:256], [sbuf])
            P.store("sp", vc[tt_ * 128:(tt_ + 1) * 128, :], st[:, 256:512], [sbuf])

        lin_tm(R, wview(w, 0, 1024, 2048 + grp * 512, 512), 8, 512, hT, hT_b, epi_v0)
    nb, nbb = load_small(P, es, S, d["b_forget"], [8, 1], F32)
    S.ts("dve", nb[:], nb[:], -1.0, ALU.mult, r=[nbb], w=[nbb])
    lf = P.sb(es, [8, T], F32, "lf")
    lfb = Buf()

    def epi_f(ci, tb, pt, pb):
        sl = slice(tb * 512, (tb + 1) * 512)
        S.act(lf[:, sl], pt[0:8, :], AF.Exp, bias=nb[:, 0:1], scale=-1.0, r=[pb, nbb], w=[lfb])
        S.act(lf[:, sl], lf[:, sl], AF.Ln, bias=1.0, r=[lfb], w=[lfb])
        S.ts("dve", lf[:, sl], lf[:, sl], -1.0, ALU.mult, r=[lfb], w=[lfb])

    lin_fm(R, wview(w, 0, 1024, 3072, 8), 8, 8, hT, hT_b, epi_f)
    for grp in range(2):
        P.store("sp", d["xinf0"][grp], lf[grp * 4:(grp + 1) * 4, :], [lfb])


def phase_b(P, R, es, xT, xT_b, hT, hT_b, d, L):
    S = P.S
    yout = d["yout%d" % L]
    S.dma("sp", hT[:], yout.rearrange("r (c p) t -> p (r c) t", p=128), w=[hT_b])
    wo = d["w_out%d" % L]
    for s in range(2):
        def epi(ci, tb, pt, pb, s=s):
            dc = s * 4 + ci
            sl = slice(tb * 512, (tb + 1) * 512)
            S.tt("dve", xT[:, dc, sl], pt[:], xT[:, dc, sl], ALU.add, r=[pb, xT_b], w=[xT_b])

        lin_fm(R, wview(wo, 0, 1024, s * 512, 512), 8, 512, hT, hT_b, epi)
    g, gb = load_gain(P, es, S, d["norm_mlp%d" % L], 8)
    rms_fm(R, xT, xT_b, 8, g, gb, hT, hT_b, D)
    wu, wd = d["w_up%d" % L], d["w_down%d" % L]
    with ExitStack() as es2:
        acts = Ring([(P.sb(es2, [128, 4, T], BF16, "act"), Buf()) for _ in range(2)])
        relu = Ring([(P.sb(es2, [128, 512], F32, "relu"), Buf()) for _ in range(2)])

        def up(s):
            at, ab = acts.next()

            def epi(ci, tb, pt, pb):
                rt, rb = relu.next()
                S.act(rt[:], pt[:], AF.Relu, r=[pb], w=[rb])
                S.act(at[:, ci, tb * 512:(tb + 1) * 512], rt[:], AF.Square, r=[rb], w=[ab])

            lin_fm(R, wview(wu, 0, 1024, s * 512, 512), 8, 512, hT, hT_b, epi)
            return at, ab

        def down(s, at, ab):
            wv, wb = R.load_slab(wd[s * 512:(s + 1) * 512, :].rearrange("(kc p) c -> p kc c", p=128), 4, 1024)
            for dc in range(8):
                for tb in range(NTB):
                    pt, pb = R.mmps.next()
                    sl = slice(tb * 512, (tb + 1) * 512)
                    for kc in range(4):
                        S.mm(pt[:], wv[:, kc, dc * 128:(dc + 1) * 128], at[:, kc, sl],
                             start=(kc == 0), stop=(kc == 3), r=[wb, ab], w=[pb])
                    S.tt("dve", xT[:, dc, sl], pt[:], xT[:, dc, sl], ALU.add, r=[pb, xT_b], w=[xT_b])

        prev = None
        for s in range(8):
            cur = up(s)
            if prev is not None:
                down(s - 1, *prev)
            prev = cur
        down(7, *prev)
        S.barrier()


def phase_a1(P, R, xT, xT_b, d):
    S = P.S
    xin = d["xin1"]
    w = d["w_in1"]
    p = slice(64, 96)
    with ExitStack() as es2:
        cqn = P.sb(es2, [128, 3, T], BF16, "cqn")
        ckvn = P.sb(es2, [128, 2, T], BF16, "ckvn")
        tab = P.sb(es2, [96, 2, T], BF16, "ropetab")
        cqnb, ckvnb, tabb = Buf(), Buf(), Buf()
        with ExitStack() as es3:
            hT = P.sb(es3, [128, 8, T], BF16, "hT")
            hT_b = Buf()
            g, gb = load_gain(P, es3, S, d["norm_mix1"], 8)
            rms_fm(R, xT, xT_b, 8, g, gb, hT, hT_b, D)
            rc, rcb = load_small(P, es3, S, d["ropec"], [96, 2], F32)
            posi = P.sb(es3, [96, 512], I32, "posi")
            ang = P.sb(es3, [96, 512], F32, "ang")
            kk = P.sb(es3, [96, 512], F32, "kk")
            rr = P.sb(es3, [96, 512], F32, "rr")
            mm_ = P.sb(es3, [96, 512], F32, "mm")
            ab_ = Buf()
            TWO_PI = 2.0 * np.pi
            C1 = 6.28125
            C2 = float(np.float32(TWO_PI - C1))
            MAGIC = 12582912.0

            def wrap(x):
                S.ts("dve", mm_[p, :], x[p, :], float(np.pi), ALU.is_gt, r=[ab_], w=[ab_])
                S.stt(x[p, :], mm_[p, :], -TWO_PI, x[p, :], ALU.mult, ALU.add, r=[ab_], w=[ab_])
                S.ts("dve", mm_[p, :], x[p, :], -float(np.pi), ALU.is_lt, r=[ab_], w=[ab_])
                S.stt(x[p, :], mm_[p, :], TWO_PI, x[p, :], ALU.mult, ALU.add, r=[ab_], w=[ab_])
                S.ts("dve", x[p, :], x[p, :], 3.1415925, ALU.min, -3.1415925, ALU.max, r=[ab_], w=[ab_])

            for tb in range(NTB):
                sl = slice(tb * 512, (tb + 1) * 512)
                S.dma("sp", posi[p, :], bass.AP(d["pos"].tensor, tb * 512, [[0, 32], [1, 512]]), w=[ab_])
                S.copy("dve", ang[p, :], posi[p, :], r=[ab_], w=[ab_])
                S.ts("dve", ang[p, :], ang[p, :], rc[p, 0:1], ALU.mult, r=[ab_, rcb], w=[ab_])
                S.ts("dve", kk[p, :], ang[p, :], float(np.float32(1.0 / TWO_PI)), ALU.mult, r=[ab_], w=[ab_])
                S.ts("dve", kk[p, :], kk[p, :], MAGIC, ALU.add, r=[ab_], w=[ab_])
                S.ts("dve", kk[p, :], kk[p, :], MAGIC, ALU.subtract, r=[ab_], w=[ab_])
                S.stt(rr[p, :], kk[p, :], -C1, ang[p, :], ALU.mult, ALU.add, r=[ab_], w=[ab_])
                S.stt(rr[p, :], kk[p, :], -C2, rr[p, :], ALU.mult, ALU.add, r=[ab_], w=[ab_])
                wrap(rr)
                S.act(kk[p, :], rr[p, :], AF.Sin, r=[ab_], w=[ab_])
                S.ts("dve", tab[p, 1, sl], kk[p, :], rc[p, 1:2], ALU.mult, r=[ab_, rcb], w=[tabb])
                S.ts("dve", rr[p, :], rr[p, :], float(np.pi / 2), ALU.add, r=[ab_], w=[ab_])
                wrap(rr)
                S.act(tab[p, 0, sl], rr[p, :], AF.Sin, r=[ab_], w=[tabb])
            for s_ in range(2):
                qk = flat(xin[s_], L1_SBQK, (512, T))
                lin_fm(R, wview(w, 0, 1024, s_ * 512, 512), 8, 512, hT, hT_b,
                       store_fm_chunk(R, qk, 0, lambda ci: 0.125 if ci < 2 else 1.0))

            def epi_v(tt_, pt, pb):
                st, sbuf = R.stage_s.next()
                S.copy("act", st[:], pt[:], r=[pb], w=[sbuf])
                for grp in range(2):
                    vd = flat(xin[grp], L1_SBV, (T, 256))
                    P.store("sp", vd[tt_ * 128:(tt_ + 1) * 128, :], st[:, grp * 256:(grp + 1) * 256], [sbuf])

            lin_tm(R, wview(w, 0, 1024, 1024, 512), 8, 512, hT, hT_b, epi_v)
            if "after_sb" in d:
                d["after_sb"]()
            gq, gqb = load_gain(P, es3, S, d["q_norm"], 3)
            gk, gkb = load_gain(P, es3, S, d["kv_norm"], 2)
            cqblk = P.sb(es3, [128, 3, 512], F32, "cqblk")
            ckvblk = P.sb(es3, [128, 2, 512], F32, "ckvblk")
            cqbb, ckvbb = Buf(), Buf()
            t1r = Ring([(P.sb(es3, [96, 512], F32, "t1"), Buf()) for _ in range(2)])
            kro = P.sb(es3, [96, T], BF16, "kro")
            krob = Buf()
            wq_, wqb = R.load_slab(wview(w, 0, 1024, 1536, 384), 8, 384)
            wk_, wkb = R.load_slab(wview(w, 0, 1024, 1920, 448), 8, 448)
            for tb in range(NTB):
                sl = slice(tb * 512, (tb + 1) * 512)

                def proj(wt, wb, c0, m):
                    pt, pb = R.mmps.next()
                    for kc in range(8):
                        S.mm(pt[0:m, :], wt[:, kc, c0:c0 + m], hT[:, kc, sl], start=(kc == 0), stop=(kc == 7),
                             r=[wb, hT_b], w=[pb])
                    return pt, pb

                for ci in range(3):
                    pt, pb = proj(wq_, wqb, ci * 128, 128)
                    S.copy("act", cqblk[:, ci, :], pt[:], r=[pb], w=[cqbb])

                def out_q(kc, rt, rb, sl=sl):
                    S.stt(cqn[:, kc, sl], cqblk[:, kc, :], gq[:, kc:kc + 1], rt[:], ALU.mult, ALU.mult,
                          r=[cqbb, gqb, rb], w=[cqnb])

                rms_block(R, cqblk, cqbb, 3, gq, gqb, 384, out_q)
                for ci in range(2):
                    pt, pb = proj(wk_, wkb, ci * 128, 128)
                    S.copy("act", ckvblk[:, ci, :], pt[:], r=[pb], w=[ckvbb])

                def out_kv(kc, rt, rb, sl=sl):
                    S.stt(ckvn[:, kc, sl], ckvblk[:, kc, :], gk[:, kc:kc + 1], rt[:], ALU.mult, ALU.mult,
                          r=[ckvbb, gkb, rb], w=[ckvnb])

                rms_block(R, ckvblk, ckvbb, 2, gk, gkb, 256, out_kv)
                pa, pab = proj(wk_, wkb, 256, 96)
                pbt, pbb = proj(wk_, wkb, 352, 96)
                t1, t1b = t1r.next()
                t2, t2b = t1r.next()
                S.tt("dve", t1[p, :], pa[p, :], tab[p, 0, sl], ALU.mult, r=[pab, tabb], w=[t1b])
                S.tt("dve", t2[p, :], pbt[p, :], tab[p, 1, sl], ALU.mult, r=[pbb, tabb], w=[t2b])
                S.tt("dve", kro[p, sl], t1[p, :], t2[p, :], ALU.add, r=[t1b, t2b], w=[krob])
            for grp in range(2):
                P.store("sp", flat(xin[grp], L1_KR, (32, T)), kro[p, :], [krob])
            S.barrier()
        with ExitStack() as es3:
            wq = d["w_uq"]
            qst = Ring([(P.sb(es3, [96, T], BF16, "qst"), Buf()) for _ in range(2)])
            t1r = Ring([(P.sb(es3, [96, 512], F32, "t1"), Buf()) for _ in range(2)])
            for hp in range(4):
                wt, wb = R.load_slab(wview(wq, 0, 384, hp * 384, 384), 3, 384)
                for hh in range(2):
                    h = hp * 2 + hh
                    st, stb = qst.next()
                    for tb in range(NTB):
                        sl = slice(tb * 512, (tb + 1) * 512)
                        pa, pab = R.mmps.next()
                        pbt, pbb = R.mmps.next()
                        for kc in range(3):
                            S.mm(pa[0:96, :], wt[:, kc, hh * 192:hh * 192 + 96], cqn[:, kc, sl],
                                 start=(kc == 0), stop=(kc == 2), r=[wb, cqnb], w=[pab])
                        for kc in range(3):
                            S.mm(pbt[0:96, :], wt[:, kc, hh * 192 + 96:hh * 192 + 192], cqn[:, kc, sl],
                                 start=(kc == 0), stop=(kc == 2), r=[wb, cqnb], w=[pbb])
                        S.act(st[0:64, sl], pa[0:64, :], AF.Copy, scale=MLA_SCALE, r=[pab], w=[stb])
                        t1, t1b = t1r.next()
                        t2, t2b = t1r.next()
                        S.stt(t1[p, :], pa[p, :], MLA_SCALE, tab[p, 0, sl], ALU.mult, ALU.mult, r=[pab, tabb], w=[t1b])
                        S.stt(t2[p, :], pbt[p, :], MLA_SCALE, tab[p, 1, sl], ALU.mult, ALU.mult, r=[pbb, tabb], w=[t2b])
                        S.tt("dve", st[p, sl], t1[p, :], t2[p, :], ALU.add, r=[t1b, t2b], w=[stb])
                    grp, hl = h // 4, h % 4
                    mq = flat(xin[grp], L1_MQ, (4, 96, T))
                    P.store("sp", mq[hl], st[:], [stb])
            wkv = d["w_ukv"]
            for grp in range(2):
                kn = flat(xin[grp], L1_MKN, (256, T))
                lin_fm(R, wview(wkv, 0, 256, grp * 256, 256), 2, 256, ckvn, ckvnb,
                       store_fm_chunk(R, kn, 0, 1.0))

            def epi_mv(tt_, pt, pb):
                st, sbuf = R.stage_s.next()
                S.copy("act", st[:], pt[:], r=[pb], w=[sbuf])
                for grp in range(2):
                    vd = flat(xin[grp], L1_MV, (T, 256))
                    P.store("sp", vd[tt_ * 128:(tt_ + 1) * 128, :], st[:, grp * 256:(grp + 1) * 256], [sbuf])

            lin_tm(R, wview(wkv, 0, 256, 512, 512), 2, 512, ckvn, ckvnb, epi_mv)
            S.barrier()


class AttCtx:
    def __init__(self, P, es, d, ntiles_mask, mask_first):
        S = P.S
        self.P, self.es = P, es
        self.cm = P.sb(es, [128, 5, 128], BF16, "cmat")
        self.cmb = Buf()
        S.dma("sp", self.cm[:], d["cmat"], w=[self.cmb])
        self.masks = P.sb(es, [128, ntiles_mask, 512], BF16, "masks")
        self.maskb = Buf()
        S.dma("sp", self.masks[:], d["cmask"][:, mask_first:mask_first + ntiles_mask, :], w=[self.maskb])
        self.kq = Ring([((P.sb(es, [128, SEQ], BF16, "kt"), P.sb(es, [128, SEQ], BF16, "qt")),
                         [Buf() for _ in range(4)]) for _ in range(2)])
        self.sps = Ring([(P.ps(es), Buf()) for _ in range(4)])
        self.nps = Ring([(P.ps(es), Buf()) for _ in range(2)])
        self.dps = Ring([(P.ps(es), Buf()) for _ in range(1)])
        self.pt = Ring([(P.sb(es, [128, 512], BF16, "pT"), Buf()) for _ in range(3)])
        self.ost = Ring([(P.sb(es, [65, SEQ], BF16, "ost"), Buf()) for _ in range(2)])
        self.rc = Ring([(P.sb(es, [65, 512], F32, "rc"), Buf()) for _ in range(1)])
        self.dn = Ring([(P.sb(es, [1, 512], F32, "dn"), Buf()) for _ in range(1)])
        self.rcpf = Ring([(P.sb(es, [128, 512], F32, "rcpf"), Buf()) for _ in range(1)])
        S.memset("dve", self.rcpf.items[0][0][:], 0.0, w=[self.rcpf.items[0][1]])
        self.row0f = P.sb(es, [128, 128], F32, "row0f")
        self.row0fb = Buf()
        S.memset("dve", self.row0f[:], 0.0, w=[self.row0fb])
        S.memset("dve", self.row0f[0:1, :], 1.0, w=[self.row0fb])


def load_v(A, xout, off, ncols, dep=()):
    P, S = A.P, A.P.S
    nh = ncols // 64
    n = 32 * nh * 66
    vf = P.sb(A.es, [128, n + 64], BF16, "v")
    vb = [Buf() for _ in range(6)]
    S.memset("dve", vf[:, n:n + 64], 0.0, w=[vb[0]])
    v4 = vf[:, 0:n].rearrange("p (j h c) -> p j h c", h=nh, c=66)
    S.memset("dve", v4[:, :, :, 0:1], 1.0, w=[vb[1]])
    i = 0
    for r in range(2):
        for h in range(nh):
            src = flat(xout[r], off, (T, ncols))[:, h * 64:(h + 1) * 64].rearrange("(j p) c -> p j c", p=128)
            S.dma("sp", v4[:, r * 16:(r + 1) * 16, h, 1:65], src, r=list(dep), w=[vb[2 + i % 4]])
            i += 1

    def lhs(kb, h):
        b = (kb * nh + h) * 66
        return vf[:, b:b + 128]

    return lhs, vb


def softmax_head(A, kt, qt, kqb, krows, v, vb, vh, tiles, yin, hrow):
    P, S = A.P, A.P.S
    J = A.cm[:, 0, :]
    ost, ostb = A.ost.next()
    n = len(tiles)
    sts = [None] * n

    def issue_s(i):
        qb, kb, m, mb, first, last, (c0, c1) = tiles[i]
        st, sb_ = A.sps.next()
        sts[i] = (st, sb_)
        S.mm(st[:, c0:c1], kt[:, kb * 128:(kb + 1) * 128], qt[:, qb * 512 + c0:qb * 512 + c1],
             start=True, stop=(m is None), r=list(kqb), w=[sb_])
        if m is not None:
            S.mm(st[:, c0:c1], J, m[:, c0:c1], start=False, stop=True, r=[A.cmb, mb], w=[sb_])

    acc = {}
    pending = []

    def flush(upto):
        while pending and pending[0][0] <= upto:
            pending.pop(0)[1]()

    LA = 3
    pts = [None] * n

    def do_pv(j):
        qb, kb, m, mb, first, last, (c0, c1) = tiles[j]
        pt, ptb = pts[j]
        if first:
            assert (c0, c1) == (0, 512)
            acc["n"] = A.nps.next()
        nt, nb = acc["n"]
        S.mm(nt[:, c0:c1], v(kb, vh), pt[:, c0:c1], start=first, stop=last, r=list(vb) + [ptb], w=[nb],
             skip_group_check=True)
        flush(j)
        if last:
            dn, dnb = A.dn.next()
            rf, rfb = A.rcpf.next()

            def a1(nt=nt, nb=nb, dn=dn, dnb=dnb):
                S.act(dn[0:1, :], nt[0:1, :], AF.Ln, r=[nb], w=[dnb])

            def a2(dn=dn, dnb=dnb, rf=rf, rfb=rfb):
                S.act(rf[0:1, :], dn[0:1, :], AF.Exp, scale=-1.0, r=[dnb], w=[rfb])

            def fin(nt=nt, nb=nb, rf=rf, rfb=rfb, qb=qb):
                bt, bb = A.dps.next()
                S.mm(bt[:], A.row0f[:], rf[:], start=True, stop=True, r=[A.row0fb, rfb], w=[bb])
                rc, rcb = A.rc.next()
                S.copy("dve", rc[:], bt[0:65, :], r=[bb], w=[rcb])
                S.tt("dve", ost[:, qb * 512:(qb + 1) * 512], nt[0:65, :], rc[:], ALU.mult, r=[nb, rcb], w=[ostb])

            pending.append((j + 2, a1))
            pending.append((j + 3, a2))
            pending.append((j + 8, fin))

    for i0 in range(min(LA, n)):
        issue_s(i0)
    for i in range(n + 1):
        if i < n:
            if i + LA < n:
                issue_s(i + LA)
            qb, kb, m, mb, first, last, (c0, c1) = tiles[i]
            st, sb_ = sts[i]
            pts[i] = A.pt.next()
            S.act(pts[i][0][:, c0:c1], st[:, c0:c1], AF.Exp, r=[sb_], w=[pts[i][1]])
        if i >= 1:
            do_pv(i - 1)
    flush(n + 100)
    for r in range(2):
        P.store("pool", yin[r, hrow:hrow + 64, :], ost[1:65, r * T:(r + 1) * T], [ostb])


def load_kq_pair(A, xout, offk, nk_total, krow0, offq, nq_total, qrow0, kt, qt, kbuf, qbuf, dep=()):
    S = A.P.S
    for r in range(2):
        srck = flat(xout[r], offk, (nk_total, T))[krow0:krow0 + 64, :]
        S.dma("sp", kt[0:64, r * T:(r + 1) * T], srck, r=list(dep), w=[kbuf])
        srcq = flat(xout[r], offq, (nq_total, T))[qrow0:qrow0 + 64, :]
        S.dma("sp", qt[0:64, r * T:(r + 1) * T], srcq, r=list(dep), w=[qbuf])


def load_kq_rows(A, xout, off, nrows_total, row0, nrows, kt_or_qt, dst_row0, buf, dep=()):
    S = A.P.S
    for r in range(2):
        src = flat(xout[r], off, (nrows_total, T))[row0:row0 + nrows, :]
        S.dma("sp", kt_or_qt[dst_row0:dst_row0 + nrows, r * T:(r + 1) * T], src, r=list(dep), w=[buf])


def phase_attn0(P, d):
    S = P.S
    xout, yin = d["xout0"], d["yin0"]
    cumd = d["cumd"]
    Ed = d["ebuf"]
    cumdb, Edb = Buf(), Buf()
    with ExitStack() as es:
        lf = P.sb(es, [4, SEQ], F32, "lf")
        lfb = Buf()
        for r in range(2):
            S.dma("sp", lf[:, r * T:(r + 1) * T], d["xoutf0"][r], w=[lfb])
        onesf = P.sb(es, [4, SEQ], F32, "onesf")
        S.memset("dve", onesf[:], 1.0, w=[lfb])
        cum = P.sb(es, [4, SEQ], F32, "cum")
        S.add("dve", lambda e: e.tensor_tensor_scan(cum[:], onesf[:], lf[:], 0.0, ALU.mult, ALU.add), r=[lfb], w=[lfb])
        c3 = P.sb(es, [4, 2, 3, SEQ], BF16, "c3")
        c3b = Buf()
        S.copy("dve", c3[:, 1, 0, :], cum[:], r=[lfb], w=[c3b])
        S.tt("dve", cum[:], cum[:], c3[:, 1, 0, :], ALU.subtract, r=[lfb, c3b], w=[lfb])
        S.copy("dve", c3[:, 1, 1, :], cum[:], r=[lfb], w=[c3b])
        S.tt("dve", cum[:], cum[:], c3[:, 1, 1, :], ALU.subtract, r=[lfb, c3b], w=[lfb])
        S.copy("dve", c3[:, 1, 2, :], cum[:], r=[lfb], w=[c3b])
        for i in range(3):
            S.ts("dve", c3[:, 0, i, :], c3[:, 1, i, :], -1.0, ALU.mult, r=[c3b], w=[c3b])
        S.dma("sp", cumd, c3[:], r=[c3b], w=[cumdb])
        rb, rbb = load_small(P, es, S, d["rel_bias"], [4, 320], F32)
        E = P.sb(es, [4, 1536], F32, "E")
        Eb = Buf()
        S.memset("dve", E[:], 0.0, w=[Eb])
        S.ts("dve", E[:, 0:449], E[:, 0:449], rb[:, 0:1], ALU.add, r=[rbb, Eb], w=[Eb])
        S.copy("dve", E[:, 449:767], rb[:, 1:319], r=[rbb, Eb], w=[Eb])
        S.ts("dve", E[:, 767:1536], E[:, 767:1536], rb[:, 319:320], ALU.add, r=[rbb, Eb], w=[Eb])
        S.dma("sp", Ed, E[:], r=[Eb], w=[Edb])
        S.barrier()
    with ExitStack() as es:
        A = AttCtx(P, es, d, 12, 0)
        depa, depb = d.get("xdep0a", []), d.get("xdep0b", [])
        bias = P.sb(es, [128, 32, 512], BF16, "bias")
        biasb = Buf()
        hkw = P.sb(es, [128, 1408], F32, "hkw")
        hkwb = Buf()
        VF = {}

        def build_bias():
            for h in range(4):
                src = bass.AP(Ed.tensor, h * 1536, [[1, 128], [1, 1408]])
                S.dma("sp", hkw[:], src, r=[Edb], w=[hkwb])
                for jb in range(8):
                    o = 896 - 128 * jb
                    S.tt("dve", bias[:, h * 8 + jb, :], hkw[:, o:o + 512], A.masks[:, 4 + jb, :], ALU.add,
                         r=[hkwb, A.maskb], w=[biasb])

        def load_head(j):
            (kt, qt), kqb = A.kq.next()
            kB, kxB, qB, qxB = kqb
            S.memset("dve", kt[64:128, :], 0.0, w=[kxB])
            S.memset("dve", qt[64:128, :], 0.0, w=[qxB])
            if j < 4:
                load_kq_pair(A, xout, L0_QKF, 512, 256 + j * 64, L0_QKF, 512, j * 64, kt, qt, kB, qB,
                             d.get("xdep0k", depa))
                S.memset("dve", kt[64:70, :], 1.0, w=[kxB])
                S.memset("dve", qt[64:70, :], 1.0, w=[qxB])
                S.dma("sp", kt[64:67, :], cumd[j, 0], r=[cumdb], w=[kxB])
                S.dma("sp", qt[67:70, :], cumd[j, 1], r=[cumdb], w=[qxB])
            else:
                jj = j - 4
                load_kq_pair(A, xout, L0_QKC, 512, 256 + jj * 64, L0_QKC, 512, jj * 64, kt, qt, kB, qB, depb)
            return kt, qt, kqb

        def run_head(j, kt, qt, kqb):
            tiles = []
            if j < 4:
                for qb in range(8):
                    nk = 4 * qb + 4
                    for kb in range(nk):
                        jm = kb - 4 * qb
                        m = A.masks[:, jm, :] if jm >= 0 else None
                        tiles.append((qb, kb, m, A.maskb, kb == 0, kb == nk - 1, (128 * max(jm, 0), 512)))
                softmax_head(A, kt, qt, kqb, 70, VF["v"], VF["b"], j, tiles, yin, j * 64)
            else:
                jj = j - 4
                CR = {0: (0, 128), 1: (0, 256), 2: (0, 384), 3: (0, 512), 4: (0, 512), 5: (128, 512),
                      6: (256, 512), 7: (384, 512)}
                for qb in range(8):
                    jbs = [jb for jb in (3, 4, 0, 1, 2, 5, 6, 7) if 4 * qb - 4 + jb >= 0]
                    for jb in jbs:
                        kb = 4 * qb - 4 + jb
                        tiles.append((qb, kb, bias[:, jj * 8 + jb, :], biasb, jb == jbs[0], jb == jbs[-1], CR[jb]))
                softmax_head(A, kt, qt, kqb, 64, VC["v"], VC["b"], jj, tiles, yin, 256 + jj * 64)

        VC = {}
        nxt = load_head(0)
        VF["v"], VF["b"] = load_v(A, xout, L0_VF, 256, depa)
        build_bias()
        for j in range(8):
            cur = nxt
            if j == 2:
                VC["v"], VC["b"] = load_v(A, xout, L0_VC, 256, depb)
            if j + 1 < 8:
                nxt = load_head(j + 1)
            run_head(j, *cur)
        S.barrier()


def sb_head(A, kt, qt, kqb, v, vb, vcol, yin, hrow, X):
    P, S = A.P, A.P.S
    J = A.cm[:, 0, :]
    NTRI = A.cm[:, 2, :]
    NONES = A.cm[:, 3, :]
    ost, ostb = A.ost.next()
    tiles = []
    for qb in range(8):
        kbs = list(range(4 * qb + 3, -1, -1))
        for kb in kbs:
            jm = kb - 4 * qb
            tiles.append((qb, kb, jm, kb == kbs[0], kb == kbs[-1]))
    n = len(tiles)
    zs, sps, rss, es_ = [None] * n, [None] * n, [None] * n, [None] * n
    acc = {}

    def stage1(i):
        qb, kb, jm, first, last = tiles[i]
        c0 = 128 * max(jm, 0)
        zt, zb = A.sps.next()
        zs[i] = (zt, zb)
        S.mm(zt[:, c0:], kt[:, kb * 128:(kb + 1) * 128], qt[:, qb * 512 + c0:(qb + 1) * 512],
             start=True, stop=False, r=list(kqb), w=[zb])
        if jm >= 0:
            S.mm(zt[:, c0:], J, A.masks[:, jm, c0:], start=False, stop=False, r=[A.cmb, A.maskb], w=[zb])
        et, eb = X["e"].next()
        S.act(et[:, c0:], zt[:, c0:], AF.Exp, r=[zb], w=[eb])
        es_[i] = (et, eb)

    def stage1b(i):
        qb, kb, jm, first, last = tiles[i]
        c0 = 128 * max(jm, 0)
        et, eb = es_[i]
        spt, spb = X["sp"].next()
        sps[i] = (spt, spb)
        S.act(spt[:, c0:], et[:, c0:], AF.Ln, bias=1.0, r=[eb], w=[spb])
        rt, rb = X["rs"].next()
        rss[i] = (rt, rb)
        if c0 > 0:
            S.memset("dve", rt[:, 0:c0], 0.0, w=[rb])
        if first:
            S.copy("dve", rt[:, c0:], spt[:, c0:], r=[spb], w=[rb])
        else:
            pr, prb = rss[i - 1]
            S.tt("dve", rt[:, c0:], pr[:, c0:], spt[:, c0:], ALU.add, r=[prb, spb], w=[rb])

    def stage2(i):
        qb, kb, jm, first, last = tiles[i]
        c0 = 128 * max(jm, 0)
        zt, zb = zs[i]
        spt, spb = sps[i]
        S.mm(zt[:, c0:], NTRI, spt[:, c0:], start=False, stop=first, r=[A.cmb, spb], w=[zb])
        if not first:
            pr, prb = rss[i - 1]
            S.mm(zt[:, c0:], NONES, pr[:, c0:], start=False, stop=True, r=[A.cmb, prb], w=[zb])
        pt, ptb = A.pt.next()
        sps[i] = (pt, ptb)
        if first and c0 > 0:
            S.memset("dve", pt[:, 0:c0], 0.0, w=[ptb])
        S.act(pt[:, c0:], zt[:, c0:], AF.Exp, r=[zb], w=[ptb])

    def stage3(i):
        qb, kb, jm, first, last = tiles[i]
        c0 = 0 if first else 128 * max(jm, 0)
        pt, ptb = sps[i]
        if first:
            acc["n"] = A.nps.next()
        nt, nb = acc["n"]
        S.mm(nt[:, c0:], v(kb, vcol), pt[:, c0:], start=first, stop=last, r=list(vb) + [ptb], w=[nb],
             skip_group_check=True)
        if last:
            S.copy("dve", ost[:, qb * 512:(qb + 1) * 512], nt[0:65, :], r=[nb], w=[ostb])

    for step in range(-1, n + 2):
        if 0 <= step + 1 < n:
            stage1(step + 1)
        if 0 <= step < n:
            stage1b(step)
        if 0 <= step - 1 < n:
            stage2(step - 1)
        if 0 <= step - 2 < n:
            stage3(step - 2)
    for r in range(2):
        P.store("pool", yin[r, hrow:hrow + 64, :], ost[1:65, r * T:(r + 1) * T], [ostb])


def phase_attn1(P, d):
    S = P.S
    with ExitStack() as es:
        A = AttCtx(P, es, d, 8, 12)
        xout, yin = d["xout1"], d["yin1"]
        dep1, dep2 = d.get("xdep1a", []), d.get("xdep1b", [])
        VS = {}
        VM = {}
        X = {
            "e": Ring([A.dps.items[0], (P.ps(es), Buf())]),
            "sp": Ring([(P.sb(es, [128, 512], BF16, "sp"), Buf()) for _ in range(3)]),
            "rs": Ring([(P.sb(es, [128, 512], BF16, "rs"), Buf()) for _ in range(4)]),
        }

        def load_head(j):
            (kt, qt), kqb = A.kq.next()
            kB, kxB, qB, qxB = kqb
            S.memset("dve", kt[64:128, :], 0.0, w=[kxB])
            S.memset("dve", qt[64:128, :], 0.0, w=[qxB])
            if j < 4:
                load_kq_pair(A, xout, L1_SBQK, 512, 256 + j * 64, L1_SBQK, 512, j * 64, kt, qt, kB, qB,
                             d.get("xdep1k", dep1))
            else:
                jj = j - 4
                for r in range(2):
                    srck = flat(xout[r], L1_MKN, (256, T))[jj * 64:jj * 64 + 64, :]
                    S.dma("sp", kt[0:64, r * T:(r + 1) * T], srck, r=list(dep2), w=[kB])
                    src = flat(xout[r], L1_MQ, (4, 96, T))[jj]
                    S.dma("sp", qt[0:96, r * T:(r + 1) * T], src, r=list(dep2), w=[qB, qxB])
                load_kq_rows(A, xout, L1_KR, 32, 0, 32, kt, 64, kxB, dep2)
            return kt, qt, kqb

        def run_head(j, kt, qt, kqb):
            if j < 4:
                sb_head(A, kt, qt, kqb, VS["v"], VS["b"], j, yin, j * 64, X)
            else:
                jj = j - 4
                tiles = []
                for qb in range(8):
                    nk = 4 * qb + 4
                    for kb in range(nk):
                        jm = kb - 4 * qb
                        m = A.masks[:, 4 + jm, :] if jm >= 0 else None
                        tiles.append((qb, kb, m, A.maskb, kb == 0, kb == nk - 1, (128 * max(jm, 0), 512)))
                softmax_head(A, kt, qt, kqb, 96, VM["v"], VM["b"], jj, tiles, yin, 256 + jj * 64)

        nxt = load_head(0)
        VS["v"], VS["b"] = load_v(A, xout, L1_SBV, 256, dep1)
        for j in range(8):
            cur = nxt
            if j == 2:
                VM["v"], VM["b"] = load_v(A, xout, L1_MV, 256, dep2)
            if j + 1 < 8:
                nxt = load_head(j + 1)
            run_head(j, *cur)
        S.barrier()


def _perm_w_in0():
    cols = []
    for grp in range(2):
        hs = range(4 * grp, 4 * grp + 4)
        for base in (0, 512, 1544, 2056):
            for h in hs:
                cols += list(range(base + h * 64, base + h * 64 + 64))
    for grp in range(2):
        hs = range(4 * grp, 4 * grp + 4)
        for base in (1024, 2568):
            for h in hs:
                cols += list(range(base + h * 64, base + h * 64 + 64))
    cols += list(range(1536, 1544))
    return np.array(cols)


def _perm_w_in1():
    cols = []
    for grp in range(2):
        hs = range(4 * grp, 4 * grp + 4)
        for base in (0, 512):
            for h in hs:
                cols += list(range(base + h * 64, base + h * 64 + 64))
    for grp in range(2):
        for h in range(4 * grp, 4 * grp + 4):
            cols += list(range(1024 + h * 64, 1024 + h * 64 + 64))
    cols += list(range(1536, 1920))
    cols += list(range(1920, 2176))
    kr = list(range(2176, 2208))
    krp = kr[16:] + kr[:16]
    cols += list(range(1920, 1984)) + kr
    cols += list(range(1920, 1984)) + krp
    return np.array(cols)


def _perm_w_uq():
    cols = []
    for h in range(8):
        b = h * 96
        nope = list(range(b, b + 64))
        rope = list(range(b + 64, b + 96))
        cols += nope + rope + nope + rope[16:] + rope[:16]
    return np.array(cols)


def _perm_w_ukv():
    cols = []
    for grp in range(2):
        for h in range(4 * grp, 4 * grp + 4):
            cols += list(range(h * 128, h * 128 + 64))
    for grp in range(2):
        for h in range(4 * grp, 4 * grp + 4):
            cols += list(range(h * 128 + 64, h * 128 + 128))
    return np.array(cols)


def _perm_w_out():
    rows = []
    for grp in range(2):
        for base in (0, 512):
            for h in range(4 * grp, 4 * grp + 4):
                rows += list(range(base + h * 64, base + h * 64 + 64))
    return np.array(rows)


def _const_masks():
    kk = np.arange(128)[:, None]
    qq = np.arange(512)[None, :]
    tiles = []
    for j in range(4):
        tiles.append(np.where(128 * j + kk <= qq, 0.0, NEG))
    for jb in range(8):
        v = (qq // 64) + 8 - 2 * jb - (kk // 64)
        tiles.append(np.where((v >= 0) & (v <= 8), 0.0, NEG))
    for j in range(4):
        tiles.append(np.where(128 * j + kk < qq, 0.0, NEG))
    for j in range(4):
        tiles.append(np.where((128 * j + kk) // 64 <= qq // 64, 0.0, NEG))
    m = np.stack(tiles, axis=1).astype(np.float32)
    m = m[::-1].copy()
    return m.astype(ml_dtypes.bfloat16)


def _const_mats():
    i = np.arange(128)
    J = (i[:, None] + i[None, :] == 127).astype(np.float32)
    ones = np.ones((128, 128), np.float32)
    ntri = -(i[:, None] >= i[None, :]).astype(np.float32)
    nones = -ones
    row0 = np.zeros((128, 128), np.float32)
    row0[0, :] = 1.0
    return np.stack([J, ones, ntri, nones, row0], axis=1).astype(ml_dtypes.bfloat16)


def _rope_consts():
    inv = np.array(INV_FREQ_BITS, dtype=np.uint32).view(np.float32)
    c = np.zeros((96, 2), np.float32)
    c[64:96, 0] = np.concatenate([inv, inv])
    c[64:96, 1] = np.concatenate([-np.ones(16, np.float32), np.ones(16, np.float32)])
    return c


def _run(P, in_maps):
    res = run_bass_kernel_spmd(P.nc, in_maps, core_ids=list(range(NCORES)))
    return res.results


def _finish(P):
    P.S.final_wait("sp", P.outbufs)
    P.S.emit()


def _exchange(xin_list):
    out = []
    for c in range(NCORES):
        b, g = c // 2, c % 2
        out.append(np.stack([xin_list[2 * b + r][g] for r in range(2)], axis=0))
    return out


def build_a0():
    P = Prog()
    d = {
        "xT": P.din("xT", [D, T], F32),
        "norm_mix0": P.din("norm_mix0", [128, 8], F32),
        "w_in0": P.din("w_in0", [D, 3080], F32),
        "b_forget": P.din("b_forget", [8, 1], F32),
        "cmat": P.din("cmat", [128, 5, 128], BF16),
        "xin0": P.dout("xin0", [2, L0_SIZE], BF16),
        "xinf0": P.dout("xinf0", [2, 4, T], F32),
    }
    with ExitStack() as es:
        R = setup_row(P, es, d["cmat"])
        xT = P.sb(es, [128, 8, T], F32, "xT")
        hT = P.sb(es, [128, 8, T], BF16, "hT")
        xT_b, hT_b = Buf(), Buf()
        P.S.dma("sp", xT[:], d["xT"].rearrange("(kc p) t -> p kc t", p=128), w=[xT_b])
        phase_a0(P, R, xT, xT_b, hT, hT_b, d)
        _finish(P)
    return P


def build_attn0():
    P = Prog()
    d = {
        "xout0": P.din("xout0", [2, L0_SIZE], BF16),
        "xoutf0": P.din("xoutf0", [2, 4, T], F32),
        "rel_bias": P.din("rel_bias", [4, 320], F32),
        "cmat": P.din("cmat", [128, 5, 128], BF16),
        "cmask": P.din("cmask", [128, 20, 512], BF16),
        "ebuf": P.dint("ebuf", [4, 1536], F32),
        "cumd": P.dint("cumd", [4, 2, 3, SEQ], BF16),
        "yin0": P.dout("yin0", [2, 512, T], BF16),
    }
    phase_attn0(P, d)
    _finish(P)
    return P


def build_attn1():
    P = Prog()
    d = {
        "xout1": P.din("xout1", [2, L1_SIZE], BF16),
        "cmat": P.din("cmat", [128, 5, 128], BF16),
        "cmask": P.din("cmask", [128, 20, 512], BF16),
        "yin1": P.dout("yin1", [2, 512, T], BF16),
    }
    phase_attn1(P, d)
    _finish(P)
    return P


def build_b(L, last):
    P = Prog()
    d = {
        "xT": P.din("xT", [D, T], F32),
        "yout%d" % L: P.din("yout%d" % L, [2, 512, T], BF16),
        "w_out%d" % L: P.din("w_out%d" % L, [D, D], F32),
        "norm_mlp%d" % L: P.din("norm_mlp%d" % L, [128, 8], F32),
        "w_up%d" % L: P.din("w_up%d" % L, [D, 4096], F32),
        "w_down%d" % L: P.din("w_down%d" % L, [4096, D], F32),
        "cmat": P.din("cmat", [128, 5, 128], BF16),
    }
    if not last:
        d.update({
            "norm_mix1": P.din("norm_mix1", [128, 8], F32),
            "w_in1": P.din("w_in1", [D, 2368], F32),
            "q_norm": P.din("q_norm", [128, 3], F32),
            "kv_norm": P.din("kv_norm", [128, 2], F32),
            "w_uq": P.din("w_uq", [384, 1536], F32),
            "w_ukv": P.din("w_ukv", [256, 1024], F32),
            "ropec": P.din("ropec", [96, 2], F32),
            "pos": P.din("pos", [1, T], I32),
            "xin1": P.dout("xin1", [2, L1_SIZE], BF16),
            "xT1": P.dout("xT1", [D, T], F32),
        })
    else:
        d.update({
            "norm_final": P.din("norm_final", [128, 8], F32),
            "outT": P.dout("outT", [D, T], F32),
        })
    with ExitStack() as es:
        S = P.S
        R = setup_row(P, es, d["cmat"])
        xT = P.sb(es, [128, 8, T], F32, "xT")
        xT_b = Buf()
        S.dma("sp", xT[:], d["xT"].rearrange("(kc p) t -> p kc t", p=128), w=[xT_b])
        with ExitStack() as esh:
            hT = P.sb(esh, [128, 8, T], BF16, "hT")
            hT_b = Buf()
            phase_b(P, R, esh, xT, xT_b, hT, hT_b, d, L)
            S.barrier()
        if not last:
            P.store("sp", d["xT1"].rearrange("(kc p) t -> p kc t", p=128), xT[:], [xT_b])
            phase_a1(P, R, xT, xT_b, d)
        else:
            g, gb = load_gain(P, es, S, d["norm_final"], 8)
            R.ostage = Ring([(P.sb(es, [128, 512], F32, "ostage"), Buf()) for _ in range(3)])
            rms_fm(R, xT, xT_b, 8, g, gb, None, None, D, out_dram=d["outT"])
        _finish(P)
    return P


PAIRS = [[0, 1], [2, 3], [4, 5], [6, 7]]


_XCNT = [0]


def own_copy(P, regs, src, dst, o, sz, pre_deps, nobar):
    bo = Buf()
    w_all = sz // 128
    P.S.add("pool", lambda e: e.dma_start(
        out=dst[regs["g"], o:o + sz].rearrange("(p w) -> p w", w=w_all),
        in_=src[regs["g"], o:o + sz].rearrange("(p w) -> p w", w=w_all)),
        r=pre_deps, w=[bo], dma=True, nobar=nobar)
    return bo


def exchange_start(P, regs, src, dst, ranges, dt, pre_deps, nobar, do_own=True):
    S = P.S
    CH = 128 * 8192
    chunks = []
    for (o, sz) in ranges:
        off = o
        while off < o + sz:
            n = min(CH, o + sz - off)
            chunks.append((off, n, n // 128))
            off += n
    st = []
    own = []
    for (off, n, w) in chunks:
        _XCNT[0] += 1
        bnc = P.dint("xb%d" % _XCNT[0], [128, w], dt)
        gat = P.dint("xg%d" % _XCNT[0], [256, w], dt)
        b1, b2 = Buf(), Buf()
        S.add("pool", lambda e, bnc=bnc, off=off, n=n, w=w: e.dma_start(
            out=bnc, in_=src[regs["ng"], off:off + n].rearrange("(p w) -> p w", w=w)),
            r=pre_deps, w=[b1], dma=True, nobar=nobar)
        st.append((bnc, gat, b1, b2, off, n, w))
    for (bnc, gat, b1, b2, off, n, w) in st:
        S.collective(lambda e, bnc=bnc, gat=gat: e.collective_compute(
            "AllGather", ALU.bypass, replica_groups=PAIRS, ins=[bnc.opt()], outs=[gat.opt()]),
            r=[b1], w=[b2], nobar=nobar)
    if do_own:
        for (o, sz) in ranges:
            own.append(own_copy(P, regs, src, dst, o, sz, pre_deps, nobar))
    return (st, own, dst, nobar)


def exchange_finish(P, regs, state):
    S = P.S
    st, own, dst, nobar = state
    done = list(own)
    for (bnc, gat, b1, b2, off, n, w) in st:
        gv = gat.rearrange("(r p) w -> r p w", r=2)
        b3 = Buf()
        S.add("pool", lambda e, gv=gv, off=off, n=n, w=w: e.dma_start(
            out=dst[regs["ng"], off:off + n].rearrange("(p w) -> p w", w=w), in_=gv[regs["ng"]]),
            r=[b2], w=[b3], dma=True, nobar=nobar)
        done.append(b3)
    return done


def exchange(P, regs, src, dst, size, dt, name):
    S = P.S
    S.barrier()
    P.outbufs = []
    exchange_finish(P, regs, exchange_start(P, regs, src, dst, [(0, size)], dt, [], False))
    S.barrier()
    return dst


def build_fused(upto=5, nof=False):
    P = Prog()
    S = P.S
    n0, n1 = L0_SIZE // 128, L1_SIZE // 128
    d = {
        "xT": P.din("xT", [D, T], F32),
        "gsel": P.din("gsel", [1, 2], I32),
        "cmat": P.din("cmat", [128, 5, 128], BF16),
        "cmask": P.din("cmask", [128, 20, 512], BF16),
        "norm_mix0": P.din("norm_mix0", [128, 8], F32),
        "w_in0": P.din("w_in0", [D, 3080], F32),
        "b_forget": P.din("b_forget", [8, 1], F32),
        "rel_bias": P.din("rel_bias", [4, 320], F32),
        "norm_mix1": P.din("norm_mix1", [128, 8], F32),
        "w_in1": P.din("w_in1", [D, 2368], F32),
        "q_norm": P.din("q_norm", [128, 3], F32),
        "kv_norm": P.din("kv_norm", [128, 2], F32),
        "w_uq": P.din("w_uq", [384, 1536], F32),
        "w_ukv": P.din("w_ukv", [256, 1024], F32),
        "ropec": P.din("ropec", [96, 2], F32),
        "pos": P.din("pos", [1, T], I32),
        "norm_final": P.din("norm_final", [128, 8], F32),
        "outT": P.dout("outT", [D, T], F32),
        "ebuf": P.dint("ebuf", [4, 1536], F32),
        "cumd": P.dint("cumd", [4, 2, 3, SEQ], BF16),
    }
    for L in range(2):
        d["w_out%d" % L] = P.din("w_out%d" % L, [D, D], F32)
        d["norm_mlp%d" % L] = P.din("norm_mlp%d" % L, [128, 8], F32)
        d["w_up%d" % L] = P.din("w_up%d" % L, [D, 4096], F32)
        d["w_down%d" % L] = P.din("w_down%d" % L, [4096, D], F32)
    x0 = P.dint("x_in0", [2, L0_SIZE], BF16)
    x0o = P.dint("x_out0", [2, L0_SIZE], BF16)
    f0 = P.dint("f_in0", [2, 4 * T], F32)
    f0o = P.dint("f_out0", [2, 4 * T], F32)
    x1 = P.dint("x_in1", [2, L1_SIZE], BF16)
    x1o = P.dint("x_out1", [2, L1_SIZE], BF16)
    ys = [P.dint("y_in%d" % L, [2, 512 * T], BF16) for L in range(2)]
    yos = [P.dint("y_out%d" % L, [2, 512 * T], BF16) for L in range(2)]
    d["xin0"] = x0
    d["xinf0"] = f0.rearrange("r (h t) -> r h t", h=4)
    d["xin1"] = x1
    d["yin0"] = ys[0].rearrange("r (f t) -> r f t", f=512)
    d["yin1"] = ys[1].rearrange("r (f t) -> r f t", f=512)

    regs = {}

    def setup(e):
        r0, r1 = e.alloc_register("g"), e.alloc_register("ng")
        e.reg_load(r0, d["gsel"][0:1, 0:1])
        e.reg_load(r1, d["gsel"][0:1, 1:2])
        regs["g"] = e.snap(r0, min_val=0, max_val=1)
        regs["ng"] = e.snap(r1, min_val=0, max_val=1)

    S.add("pool", setup).aux = True
    with ExitStack() as es:
        xT = P.sb(es, [128, 8, T], F32, "xT")
        xT_b = Buf()
        S.dma("sp", xT[:], d["xT"].rearrange("(kc p) t -> p kc t", p=128), w=[xT_b])
        with ExitStack() as es1:
            R = setup_row(P, es1, d["cmat"])
            hT = P.sb(es1, [128, 8, T], BF16, "hT")
            hT_b = Buf()
            phase_a0(P, R, xT, xT_b, hT, hT_b, d)
            S.barrier()
        def early_out():
            P.outbufs = []
            P.store("sp", d["outT"].rearrange("(kc p) t -> p kc t", p=128), xT[:], [xT_b])
            _finish(P)
            return P

        d["xoutf0"] = exchange(P, regs, f0, f0o, 4 * T, F32, "f0").rearrange("r (h t) -> r h t", h=4)
        d["xout0"] = x0o
        st1 = exchange_start(P, regs, x0, x0o, [(0, L0_VF)], BF16, [], True, do_own=False)
        st2 = exchange_start(P, regs, x0, x0o, [(L0_VF, L0_QKC - L0_VF)], BF16, [], True, do_own=False)
        ownb = own_copy(P, regs, x0, x0o, 0, L0_QKC, [], True)
        d["xdep0k"] = exchange_finish(P, regs, st1) + [ownb]
        d["xdep0a"] = d["xdep0k"] + exchange_finish(P, regs, st2)
        d["xdep0b"] = exchange_finish(P, regs, exchange_start(P, regs, x0, x0o, [(L0_QKC, L0_SIZE - L0_QKC)],
                                                             BF16, [], True))
        if upto == 1:
            return early_out()
        phase_attn0(P, d)
        d["yout0"] = exchange(P, regs, ys[0], yos[0], 512 * T, BF16, "y0").rearrange("r (f t) -> r f t", f=512)
        if upto == 2:
            return early_out()
        with ExitStack() as es1:
            R = setup_row(P, es1, d["cmat"])
            with ExitStack() as esh:
                hT = P.sb(esh, [128, 8, T], BF16, "hT")
                hT_b = Buf()
                phase_b(P, R, esh, xT, xT_b, hT, hT_b, d, 0)
                S.barrier()
            phase_a1(P, R, xT, xT_b, d)
            S.barrier()
            P.outbufs = []
            st1 = exchange_start(P, regs, x1, x1o, [(0, L1_SBV)], BF16, [], True, do_own=False)
            st2 = exchange_start(P, regs, x1, x1o, [(L1_SBV, L1_MQ - L1_SBV)], BF16, [], True, do_own=False)
            ownb = own_copy(P, regs, x1, x1o, 0, L1_MQ, [], True)
            d["xdep1k"] = exchange_finish(P, regs, st1) + [ownb]
            d["xdep1a"] = d["xdep1k"] + exchange_finish(P, regs, st2)
            d["xdep1b"] = exchange_finish(P, regs, exchange_start(P, regs, x1, x1o, [(L1_MQ, L1_SIZE - L1_MQ)],
                                                                 BF16, [], True))
        d["xout1"] = x1o
        phase_attn1(P, d)
        d["yout1"] = exchange(P, regs, ys[1], yos[1], 512 * T, BF16, "y1").rearrange("r (f t) -> r f t", f=512)
        with ExitStack() as es1:
            R = setup_row(P, es1, d["cmat"])
            with ExitStack() as esh:
                hT = P.sb(esh, [128, 8, T], BF16, "hT")
                hT_b = Buf()
                phase_b(P, R, esh, xT, xT_b, hT, hT_b, d, 1)
                S.barrier()
            P.outbufs = []
            gf, gfb = load_gain(P, es1, S, d["norm_final"], 8)
            R.ostage = Ring([(P.sb(es1, [128, 512], F32, "ostage"), Buf()) for _ in range(3)])
            rms_fm(R, xT, xT_b, 8, gf, gfb, None, None, D, out_dram=d["outT"])
            _finish(P)
    return P


_CACHE = {}


def _prog(key, fn):
    if key not in _CACHE:
        _CACHE[key] = fn()
    return _CACHE[key]


def kernel(x, positions, norm_mix, norm_mlp, norm_final, w_in_ab, b_forget, rel_bias, w_out_ab,
           w_in_cd, q_norm, kv_norm, w_uq, w_ukv, w_out_cd, w_up, w_down):
    f32 = lambda a: np.ascontiguousarray(np.asarray(a, dtype=np.float32))
    x = f32(x)
    cmat = _const_mats()
    cmask = _const_masks()
    ropec = _rope_consts()
    w_in0 = f32(np.asarray(w_in_ab)[0][:, _perm_w_in0()])
    w_in1 = f32(np.asarray(w_in_cd)[0][:, _perm_w_in1()])
    w_uq_p = f32(np.asarray(w_uq)[0][:, _perm_w_uq()])
    w_ukv_p = f32(np.asarray(w_ukv)[0][:, _perm_w_ukv()])
    w_out0 = f32(np.asarray(w_out_ab)[0][_perm_w_out(), :])
    w_out1 = f32(np.asarray(w_out_cd)[0][_perm_w_out(), :])
    pos = np.asarray(positions).astype(np.int32)
    gl = lambda v: f32(np.asarray(v, dtype=np.float32).reshape(-1, 128).T)
    xTs = [f32(x[c // 2, (c % 2) * T:(c % 2 + 1) * T, :].T) for c in range(NCORES)]
    if FUSED:
        P = _prog("fused", build_fused)
        rbias = np.asarray(rel_bias, dtype=np.float32)[0]
        common = {
            "cmat": cmat, "cmask": cmask, "norm_mix0": gl(norm_mix[0]), "w_in0": w_in0,
            "b_forget": f32(np.asarray(b_forget)[0].reshape(8, 1)),
            "norm_mix1": gl(norm_mix[1]), "w_in1": w_in1, "q_norm": gl(np.asarray(q_norm)[0]),
            "kv_norm": gl(np.asarray(kv_norm)[0]), "w_uq": w_uq_p, "w_ukv": w_ukv_p, "ropec": ropec,
            "norm_final": gl(norm_final), "w_out0": w_out0, "w_out1": w_out1,
            "norm_mlp0": gl(norm_mlp[0]), "norm_mlp1": gl(norm_mlp[1]),
            "w_up0": f32(np.asarray(w_up)[0]), "w_up1": f32(np.asarray(w_up)[1]),
            "w_down0": f32(np.asarray(w_down)[0]), "w_down1": f32(np.asarray(w_down)[1]),
        }
        maps = []
        for c in range(NCORES):
            g = c % 2
            m = dict(common)
            m.update({"xT": xTs[c], "gsel": np.array([[g, 1 - g]], np.int32),
                      "rel_bias": f32(rbias[4 * g:4 * g + 4]),
                      "pos": np.ascontiguousarray(pos[c // 2, g * T:(g + 1) * T].reshape(1, T))})
            maps.append(m)
        rr = _run(P, maps)
        out = np.empty((4, SEQ, D), np.float32)
        for c in range(NCORES):
            out[c // 2, (c % 2) * T:(c % 2 + 1) * T, :] = rr[c]["outT"].T
        return out

    P = _prog("a0", build_a0)
    maps = [{"xT": xTs[c], "norm_mix0": gl(norm_mix[0]), "w_in0": w_in0,
             "b_forget": f32(np.asarray(b_forget)[0].reshape(8, 1)), "cmat": cmat} for c in range(NCORES)]
    r1 = _run(P, maps)
    xout0 = _exchange([r["xin0"] for r in r1])
    xoutf0 = _exchange([r["xinf0"] for r in r1])
    P = _prog("attn0", build_attn0)
    rbias = np.asarray(rel_bias, dtype=np.float32)[0]
    maps = [{"xout0": xout0[c], "xoutf0": xoutf0[c], "rel_bias": f32(rbias[4 * (c % 2):4 * (c % 2) + 4]),
             "cmat": cmat, "cmask": cmask} for c in range(NCORES)]
    r2 = _run(P, maps)
    yout0 = _exchange([r["yin0"] for r in r2])
    P = _prog("b0", lambda: build_b(0, False))
    maps = [{"xT": xTs[c], "yout0": yout0[c], "w_out0": w_out0, "norm_mlp0": gl(norm_mlp[0]),
             "w_up0": f32(np.asarray(w_up)[0]), "w_down0": f32(np.asarray(w_down)[0]), "cmat": cmat,
             "norm_mix1": gl(norm_mix[1]), "w_in1": w_in1, "q_norm": gl(np.asarray(q_norm)[0]),
             "kv_norm": gl(np.asarray(kv_norm)[0]), "w_uq": w_uq_p, "w_ukv": w_ukv_p, "ropec": ropec,
             "pos": np.ascontiguousarray(pos[c // 2, (c % 2) * T:(c % 2 + 1) * T].reshape(1, T))}
            for c in range(NCORES)]
    r3 = _run(P, maps)
    xout1 = _exchange([r["xin1"] for r in r3])
    P = _prog("attn1", build_attn1)
    maps = [{"xout1": xout1[c], "cmat": cmat, "cmask": cmask} for c in range(NCORES)]
    r4 = _run(P, maps)
    yout1 = _exchange([r["yin1"] for r in r4])
    P = _prog("b1", lambda: build_b(1, True))
    maps = [{"xT": r3[c]["xT1"], "yout1": yout1[c], "w_out1": w_out1, "norm_mlp1": gl(norm_mlp[1]),
             "w_up1": f32(np.asarray(w_up)[1]), "w_down1": f32(np.asarray(w_down)[1]), "cmat": cmat,
             "norm_final": gl(norm_final)} for c in range(NCORES)]
    r5 = _run(P, maps)
    out = np.empty((4, SEQ, D), np.float32)
    for c in range(NCORES):
        out[c // 2, (c % 2) * T:(c % 2 + 1) * T, :] = r5[c]["outT"].T
    return out
```

```python
import numpy as np
import ml_dtypes
import concourse.bass as bass
import concourse.mybir as mybir
from concourse.bass_utils import run_bass_kernel_spmd
from contextlib import ExitStack

F32 = mybir.dt.float32
BF16 = mybir.dt.bfloat16
I32 = mybir.dt.int32
AF = mybir.ActivationFunctionType
ALU = mybir.AluOpType

NCORES = 8
FUSED = True
D = 1024
T = 2048
SEQ = 4096
NTB = T // 512
EPS = 1e-6
NEG = -30000.0
MLA_SCALE = float(96 ** -0.5)
INV_FREQ_BITS = [0x3f800000, 0x3f0ff59a, 0x3ea1e89b, 0x3e361887, 0x3dcccccd, 0x3d6655c3, 0x3d0186e2, 0x3c91ad39,
                 0x3c23d70a, 0x3bb8449c, 0x3b4f3e37, 0x3ae91528, 0x3a83126f, 0x3a136a16, 0x39a5cb5f, 0x393a7753]

L0_QKF = 0
L0_VF = L0_QKF + 512 * T
L0_QKC = L0_VF + T * 256
L0_VC = L0_QKC + 512 * T
L0_SIZE = L0_VC + T * 256
L1_SBQK = 0
L1_SBV = L1_SBQK + 512 * T
L1_MQ = L1_SBV + T * 256
L1_MKN = L1_MQ + 4 * 96 * T
L1_KR = L1_MKN + 256 * T
L1_MV = L1_KR + 32 * T
L1_SIZE = L1_MV + T * 256


class Buf:
    __slots__ = ("w", "r")

    def __init__(self):
        self.w = None
        self.r = []


class Op:
    __slots__ = ("eng", "fn", "deps", "dma", "sig", "sem", "val", "prev_use", "cc", "aux")

    def __init__(self, eng, fn, dma):
        self.eng = eng
        self.fn = fn
        self.dma = dma
        self.cc = False
        self.aux = False
        self.deps = []
        self.sig = False
        self.sem = None
        self.val = 0
        self.prev_use = None


class Sched:
    ENGS = ("pe", "act", "dve", "pool", "sp")
    NDMASEM = {"sp": 16, "pool": 8, "act": 4}

    def __init__(self, nc):
        self.nc = nc
        self.ops = {e: [] for e in self.ENGS}
        self.since_bar = []

    def add(self, eng, fn, r=(), w=(), dma=False, nobar=False):
        op = Op(eng, fn, dma)
        deps = {}
        for b in r:
            if b.w is not None:
                deps[id(b.w)] = b.w
        for b in w:
            if b.w is not None:
                deps[id(b.w)] = b.w
            for x in b.r:
                deps[id(x)] = x
        for b in r:
            b.r.append(op)
        for b in w:
            b.w = op
            b.r = []
        for d in deps.values():
            if d is op:
                continue
            if (not d.dma) and (not dma) and d.eng == "pe" and eng == "pe":
                continue
            op.deps.append(d)
            d.sig = True
        if dma:
            op.sig = True
            if not nobar:
                self.since_bar.append(op)
        self.ops[eng].append(op)
        return op

    def barrier(self):
        lasts = []
        for e in self.ENGS:
            for op in reversed(self.ops[e]):
                if (not op.dma) and op.fn is not None and not op.aux:
                    lasts.append(op)
                    break
        dmas = self.since_bar
        self.since_bar = []
        for e in self.ENGS:
            op = Op(e, None, False)
            for d in lasts:
                if d.eng != e:
                    op.deps.append(d)
                    d.sig = True
            for d in reversed(dmas):
                op.deps.append(d)
            self.ops[e].append(op)

    def mm(self, out, lhsT, rhs, start=True, stop=True, r=(), w=(), **kw):
        return self.add("pe", lambda e: e.matmul(out, lhsT, rhs, start=start, stop=stop, **kw), r, w)

    def act(self, out, in_, func, bias=None, scale=None, r=(), w=()):
        kw = {}
        if bias is not None:
            kw["bias"] = bias
        if scale is not None:
            kw["scale"] = scale
        return self.add("act", lambda e: e.activation(out, in_, func, **kw), r, w)

    def tt(self, eng, out, in0, in1, op, r=(), w=()):
        return self.add(eng, lambda e: e.tensor_tensor(out, in0, in1, op), r, w)

    def ts(self, eng, out, in0, s1, op0, s2=None, op1=None, r=(), w=()):
        if op1 is None:
            return self.add(eng, lambda e: e.tensor_scalar(out, in0, s1, None, op0), r, w)
        return self.add(eng, lambda e: e.tensor_scalar(out, in0, s1, s2, op0, op1), r, w)

    def stt(self, out, in0, scalar, in1, op0, op1, r=(), w=()):
        return self.add("dve", lambda e: e.scalar_tensor_tensor(out, in0, scalar, in1, op0, op1), r, w)

    def copy(self, eng, out, in_, r=(), w=()):
        if eng == "act":
            return self.add("act", lambda e: e.copy(out, in_), r, w)
        return self.add(eng, lambda e: e.tensor_copy(out, in_), r, w)

    def memset(self, eng, ap, val, r=(), w=()):
        return self.add(eng, lambda e: e.memset(ap, val), r, w)

    def recip(self, out, in_, r=(), w=()):
        return self.add("dve", lambda e: e.reciprocal(out, in_), r, w)

    def dma(self, q, out, in_, r=(), w=()):
        return self.add(q, lambda e: e.dma_start(out=out, in_=in_), r, w, dma=True)

    def collective(self, fn, r=(), w=(), nobar=False):
        op = self.add("pool", fn, r, w, dma=True, nobar=nobar)
        op.cc = True
        return op

    def final_wait(self, eng, bufs):
        return self.add(eng, None, r=bufs)

    def emit(self):
        nc = self.nc
        with ExitStack() as es:
            block = es.enter_context(nc.Block())
            csem = {}
            for e in ("pe", "act", "dve", "pool"):
                csem[e] = es.enter_context(nc.semaphore("c_" + e))
            ccsem = es.enter_context(nc.semaphore("cc_sem"))
            dsem = {}
            for q, n in self.NDMASEM.items():
                dsem[q] = [es.enter_context(nc.semaphore("d_%s%d" % (q, i))) for i in range(n)]
            for e in self.ENGS:
                cnt = 0
                dcnt = 0
                uses = {}
                ccnt = 0
                for op in self.ops[e]:
                    if op.cc:
                        ccnt += 1
                        op.sem = ccsem
                        op.val = ccnt
                    elif op.dma:
                        pool = dsem[e]
                        k = dcnt % len(pool)
                        dcnt += 1
                        op.sem = pool[k]
                        prev = uses.get(k)
                        op.prev_use = prev
                        op.val = (prev.val if prev is not None else 0) + 16
                        uses[k] = op
                    elif op.sig:
                        cnt += 1
                        op.sem = csem[e]
                        op.val = cnt
            sched = self

            def run(eng_name):
                def body(e):
                    waited = {}

                    def wait(sem, val):
                        key = id(sem)
                        if waited.get(key, 0) >= val:
                            return
                        waited[key] = val
                        e.wait_ge(sem, val)

                    for op in sched.ops[eng_name]:
                        for d in op.deps:
                            wait(d.sem, d.val)
                        if op.dma and (not op.cc) and op.prev_use is not None:
                            wait(op.prev_use.sem, op.prev_use.val)
                        if op.fn is None:
                            continue
                        inst = op.fn(e)
                        if op.sig:
                            if op.cc:
                                inst.then_inc(op.sem)
                            else:
                                inst.then_inc(op.sem, 16 if op.dma else 1)

                return body

            block.tensor(run("pe"))
            block.scalar(run("act"))
            block.vector(run("dve"))
            block.gpsimd(run("pool"))
            block.sync(run("sp"))
        return nc


class Ring:
    def __init__(self, items):
        self.items = items
        self.i = 0

    def next(self):
        it = self.items[self.i % len(self.items)]
        self.i += 1
        return it


class Prog:
    def __init__(self):
        self.nc = bass.Bass("TRN2", target_bir_lowering=False)
        self.S = Sched(self.nc)
        self.es = ExitStack()
        self.outbufs = []
        self.nname = 0

    def name(self, p):
        self.nname += 1
        return "%s_%d" % (p, self.nname)

    def din(self, name, shape, dt):
        return self.nc.dram_tensor(name, list(shape), dt, kind="ExternalInput").ap()

    def dout(self, name, shape, dt):
        return self.nc.dram_tensor(name, list(shape), dt, kind="ExternalOutput").ap()

    def dint(self, name, shape, dt):
        return self.nc.dram_tensor(name, list(shape), dt).ap()

    def sb(self, es, shape, dt, name="t"):
        return es.enter_context(self.nc.sbuf_tensor(self.name(name), list(shape), dt))

    def ps(self, es, shape=(128, 512), dt=F32, name="p"):
        return es.enter_context(self.nc.psum_tensor(self.name(name), list(shape), dt))

    def store(self, q, dram_ap, sb_ap, r):
        b = Buf()
        self.S.dma(q, dram_ap, sb_ap, r=r, w=[b])
        self.outbufs.append(b)
        return b


def flat(ap2, off, shape):
    n = int(np.prod(shape))
    v = ap2[off:off + n]
    if len(shape) == 2:
        return v.rearrange("(a b) -> a b", b=shape[1])
    if len(shape) == 3:
        return v.rearrange("(a b c) -> a b c", b=shape[1], c=shape[2])
    return v


class RowCtx:
    def __init__(self, P, es, cmat):
        self.P = P
        self.es = es
        self.xsq = Ring([(P.sb(es, [128, 512], BF16, "xsq"), Buf()) for _ in range(3)])
        self.rstd = Ring([(P.sb(es, [128, 512], F32, "rstd"), Buf()) for _ in range(2)])
        self.wslab = Ring([(P.sb(es, [128, 4096], BF16, "wslab"), Buf()) for _ in range(3)])
        self.mmps = Ring([(P.ps(es), Buf()) for _ in range(4)])
        self.ssps = Ring([(P.ps(es), Buf()) for _ in range(2)])
        self.stage = Ring([(P.sb(es, [128, 2048], BF16, "stg"), Buf()) for _ in range(2)])
        self.stage_s = Ring([(P.sb(es, [128, 512], BF16, "stgs"), Buf()) for _ in range(3)])
        self.cmat = cmat

    def load_slab(self, src3, nk, ncols):
        t, b = self.wslab.next()
        v = t[:, 0:nk * ncols].rearrange("p (k c) -> p k c", c=ncols)
        self.P.S.dma("pool", v, src3, w=[b])
        return v, b


def rms_block(R, src3, src_b, nk, gain, gain_b, dim, emit_out):
    S = R.P.S
    ones = R.cmat[0][:, 1, :]
    pst, psb = R.ssps.next()
    for kc in range(nk):
        xq, xqb = R.xsq.next()
        S.act(xq[:], src3[:, kc, :], AF.Square, r=[src_b], w=[xqb])
        S.mm(pst[:], ones, xq[:], start=(kc == 0), stop=(kc == nk - 1), r=[xqb, R.cmat[1]], w=[psb])
    rt, rb = R.rstd.next()
    S.act(rt[:], pst[:], AF.Ln, bias=R.eps[:, 0:1], scale=1.0 / dim, r=[psb, R.eps_b], w=[rb])
    S.act(rt[:], rt[:], AF.Exp, scale=-0.5, r=[rb], w=[rb])
    for kc in range(nk):
        emit_out(kc, rt, rb)


def rms_fm(R, src, src_b, nk, gain, gain_b, dst, dst_b, dim, out_dram=None):
    P, S = R.P, R.P.S
    for tb in range(NTB):
        ts_ = slice(tb * 512, (tb + 1) * 512)

        def emit_out(kc, rt, rb, ts_=ts_):
            if out_dram is None:
                S.stt(dst[:, kc, ts_], src[:, kc, ts_], gain[:, kc:kc + 1], rt[:], ALU.mult, ALU.mult,
                      r=[src_b, gain_b, rb], w=[dst_b])
            else:
                ot, ob = R.ostage.next()
                S.stt(ot[:], src[:, kc, ts_], gain[:, kc:kc + 1], rt[:], ALU.mult, ALU.mult,
                      r=[src_b, gain_b, rb], w=[ob])
                P.store("sp", out_dram[kc * 128:(kc + 1) * 128, ts_], ot[:], [ob])

        rms_block(R, src[:, :, ts_], src_b, nk, gain, gain_b, dim, emit_out)


def lin_fm(R, wsrc, nk, ncols, rhs, rhs_b, epilogue, mrows=None):
    S = R.P.S
    wt, wb = R.load_slab(wsrc, nk, ncols)
    chunks = []
    c0 = 0
    while c0 < ncols:
        m = min(128, ncols - c0) if mrows is None else mrows
        chunks.append((c0, m))
        c0 += m
    for ci, (c0, m) in enumerate(chunks):
        for tb in range(NTB):
            pt, pb = R.mmps.next()
            for kc in range(nk):
                S.mm(pt[0:m, :], wt[:, kc, c0:c0 + m], rhs[:, kc, tb * 512:(tb + 1) * 512],
                     start=(kc == 0), stop=(kc == nk - 1), r=[wb, rhs_b], w=[pb])
            epilogue(ci, tb, pt, pb)


def lin_tm(R, wsrc, nk, ncols, lhs, lhs_b, epilogue):
    S = R.P.S
    wt, wb = R.load_slab(wsrc, nk, ncols)
    for tt_ in range(T // 128):
        pt, pb = R.mmps.next()
        for kc in range(nk):
            S.mm(pt[:, 0:ncols], lhs[:, kc, tt_ * 128:(tt_ + 1) * 128], wt[:, kc, 0:ncols],
                 start=(kc == 0), stop=(kc == nk - 1), r=[wb, lhs_b], w=[pb])
        epilogue(tt_, pt, pb)


def wview(w2, r0, nrows, c0, ncols):
    return w2[r0:r0 + nrows, c0:c0 + ncols].rearrange("(kc p) c -> p kc c", p=128)


def store_fm_chunk(R, dram2, row0, scale):
    S = R.P.S
    state = {}

    def epi(ci, tb, pt, pb):
        if tb == 0:
            state["st"] = R.stage.next()
        st, sbuf = state["st"]
        S.act(st[:, tb * 512:(tb + 1) * 512], pt[:], AF.Copy, scale=scale(ci) if callable(scale) else scale,
              r=[pb], w=[sbuf])
        if tb == NTB - 1:
            R.P.store("sp", dram2[row0 + ci * 128: row0 + (ci + 1) * 128, :], st[:], [sbuf])

    return epi


def store_tm(R, dram2, ncols, col0=0):
    S = R.P.S

    def epi(tt_, pt, pb):
        st, sbuf = R.stage_s.next()
        S.copy("dve", st[:, 0:ncols], pt[:, 0:ncols], r=[pb], w=[sbuf])
        R.P.store("sp", dram2[tt_ * 128:(tt_ + 1) * 128, col0:col0 + ncols], st[:, 0:ncols], [sbuf])

    return epi


def load_small(P, es, S, dram, shape, dt, q="sp"):
    t = P.sb(es, shape, dt, "sm")
    b = Buf()
    S.dma(q, t[:], dram, w=[b])
    return t, b


def load_gain(P, es, S, dram2, nk):
    t = P.sb(es, [128, nk], F32, "gain")
    b = Buf()
    S.dma("sp", t[:], dram2, w=[b])
    return t, b


def setup_row(P, es, cmat_d):
    S = P.S
    cm = P.sb(es, [128, 5, 128], BF16, "cmat")
    cmb = Buf()
    S.dma("sp", cm[:], cmat_d, w=[cmb])
    R = RowCtx(P, es, (cm, cmb))
    R.eps = P.sb(es, [128, 1], F32, "eps")
    R.eps_b = Buf()
    S.memset("dve", R.eps[:], EPS, w=[R.eps_b])
    return R


def phase_a0(P, R, xT, xT_b, hT, hT_b, d):
    S, es = P.S, R.es
    g, gb = load_gain(P, es, S, d["norm_mix0"], 8)
    rms_fm(R, xT, xT_b, 8, g, gb, hT, hT_b, D)
    xin = d["xin0"]
    w = d["w_in0"]
    for s in range(4):
        grp = s // 2
        qk = flat(xin[grp], L0_QKF if s % 2 == 0 else L0_QKC, (512, T))
        lin_fm(R, wview(w, 0, 1024, s * 512, 512), 8, 512, hT, hT_b,
               store_fm_chunk(R, qk, 0, lambda ci: 0.125 if ci < 2 else 1.0))
    for grp in range(2):
        vf = flat(xin[grp], L0_VF, (T, 256))
        vc = flat(xin[grp], L0_VC, (T, 256))

        def epi_v0(tt_, pt, pb, vf=vf, vc=vc):
            st, sbuf = R.stage_s.next()
            S.copy("act", st[:], pt[:], r=[pb], w=[sbuf])
            P.store("sp", vf[tt_ * 128:(tt_ + 1) * 128, :], st[:, 0:256], [sbuf])
            P.store("sp", vc[tt_ * 128:(tt_ + 1) * 128, :], st[:, 256:512], [sbuf])

        lin_tm(R, wview(w, 0, 1024, 2048 + grp * 512, 512), 8, 512, hT, hT_b, epi_v0)
    nb, nbb = load_small(P, es, S, d["b_forget"], [8, 1], F32)
    S.ts("dve", nb[:], nb[:], -1.0, ALU.mult, r=[nbb], w=[nbb])
    lf = P.sb(es, [8, T], F32, "lf")
    lfb = Buf()

    def epi_f(ci, tb, pt, pb):
        sl = slice(tb * 512, (tb + 1) * 512)
        S.act(lf[:, sl], pt[0:8, :], AF.Exp, bias=nb[:, 0:1], scale=-1.0, r=[pb, nbb], w=[lfb])
        S.act(lf[:, sl], lf[:, sl], AF.Ln, bias=1.0, r=[lfb], w=[lfb])
        S.ts("dve", lf[:, sl], lf[:, sl], -1.0, ALU.mult, r=[lfb], w=[lfb])

    lin_fm(R, wview(w, 0, 1024, 3072, 8), 8, 8, hT, hT_b, epi_f)
    for grp in range(2):
        P.store("sp", d["xinf0"][grp], lf[grp * 4:(grp + 1) * 4, :], [lfb])


def phase_b(P, R, es, xT, xT_b, hT, hT_b, d, L):
    S = P.S
    yout = d["yout%d" % L]
    S.dma("sp", hT[:], yout.rearrange("r (c p) t -> p (r c) t", p=128), w=[hT_b])
    wo = d["w_out%d" % L]
    for s in range(2):
        def epi(ci, tb, pt, pb, s=s):
            dc = s * 4 + ci
            sl = slice(tb * 512, (tb + 1) * 512)
            S.tt("dve", xT[:, dc, sl], pt[:], xT[:, dc, sl], ALU.add, r=[pb, xT_b], w=[xT_b])

        lin_fm(R, wview(wo, 0, 1024, s * 512, 512), 8, 512, hT, hT_b, epi)
    g, gb = load_gain(P, es, S, d["norm_mlp%d" % L], 8)
    rms_fm(R, xT, xT_b, 8, g, gb, hT, hT_b, D)
    wu, wd = d["w_up%d" % L], d["w_down%d" % L]
    with ExitStack() as es2:
        acts = Ring([(P.sb(es2, [128, 4, T], BF16, "act"), Buf()) for _ in range(2)])
        relu = Ring([(P.sb(es2, [128, 512], F32, "relu"), Buf()) for _ in range(2)])

        def up(s):
            at, ab = acts.next()

            def epi(ci, tb, pt, pb):
                rt, rb = relu.next()
                S.act(rt[:], pt[:], AF.Relu, r=[pb], w=[rb])
                S.act(at[:, ci, tb * 512:(tb + 1) * 512], rt[:], AF.Square, r=[rb], w=[ab])

            lin_fm(R, wview(wu, 0, 1024, s * 512, 512), 8, 512, hT, hT_b, epi)
            return at, ab

        def down(s, at, ab):
            wv, wb = R.load_slab(wd[s * 512:(s + 1) * 512, :].rearrange("(kc p) c -> p kc c", p=128), 4, 1024)
            for dc in range(8):
                for tb in range(NTB):
                    pt, pb = R.mmps.next()
                    sl = slice(tb * 512, (tb + 1) * 512)
                    for kc in range(4):
                        S.mm(pt[:], wv[:, kc, dc * 128:(dc + 1) * 128], at[:, kc, sl],
                             start=(kc == 0), stop=(kc == 3), r=[wb, ab], w=[pb])
                    S.tt("dve", xT[:, dc, sl], pt[:], xT[:, dc, sl], ALU.add, r=[pb, xT_b], w=[xT_b])

        prev = None
        for s in range(8):
            cur = up(s)
            if prev is not None:
                down(s - 1, *prev)
            prev = cur
        down(7, *prev)
        S.barrier()


def phase_a1(P, R, xT, xT_b, d):
    S = P.S
    xin = d["xin1"]
    w = d["w_in1"]
    p = slice(64, 96)
    with ExitStack() as es2:
        cqn = P.sb(es2, [128, 3, T], BF16, "cqn")
        ckvn = P.sb(es2, [128, 2, T], BF16, "ckvn")
        tab = P.sb(es2, [96, 2, T], BF16, "ropetab")
        cqnb, ckvnb, tabb = Buf(), Buf(), Buf()
        with ExitStack() as es3:
            hT = P.sb(es3, [128, 8, T], BF16, "hT")
            hT_b = Buf()
            g, gb = load_gain(P, es3, S, d["norm_mix1"], 8)
            rms_fm(R, xT, xT_b, 8, g, gb, hT, hT_b, D)
            rc, rcb = load_small(P, es3, S, d["ropec"], [96, 2], F32)
            posi = P.sb(es3, [96, 512], I32, "posi")
            ang = P.sb(es3, [96, 512], F32, "ang")
            kk = P.sb(es3, [96, 512], F32, "kk")
            rr = P.sb(es3, [96, 512], F32, "rr")
            mm_ = P.sb(es3, [96, 512], F32, "mm")
            ab_ = Buf()
            TWO_PI = 2.0 * np.pi
            C1 = 6.28125
            C2 = float(np.float32(TWO_PI - C1))
            MAGIC = 12582912.0

            def wrap(x):
                S.ts("dve", mm_[p, :], x[p, :], float(np.pi), ALU.is_gt, r=[ab_], w=[ab_])
                S.stt(x[p, :], mm_[p, :], -TWO_PI, x[p, :], ALU.mult, ALU.add, r=[ab_], w=[ab_])
                S.ts("dve", mm_[p, :], x[p, :], -float(np.pi), ALU.is_lt, r=[ab_], w=[ab_])
                S.stt(x[p, :], mm_[p, :], TWO_PI, x[p, :], ALU.mult, ALU.add, r=[ab_], w=[ab_])
                S.ts("dve", x[p, :], x[p, :], 3.1415925, ALU.min, -3.1415925, ALU.max, r=[ab_], w=[ab_])

            for tb in range(NTB):
                sl = slice(tb * 512, (tb + 1) * 512)
                S.dma("sp", posi[p, :], bass.AP(d["pos"].tensor, tb * 512, [[0, 32], [1, 512]]), w=[ab_])
                S.copy("dve", ang[p, :], posi[p, :], r=[ab_], w=[ab_])
                S.ts("dve", ang[p, :], ang[p, :], rc[p, 0:1], ALU.mult, r=[ab_, rcb], w=[ab_])
                S.ts("dve", kk[p, :], ang[p, :], float(np.float32(1.0 / TWO_PI)), ALU.mult, r=[ab_], w=[ab_])
                S.ts("dve", kk[p, :], kk[p, :], MAGIC, ALU.add, r=[ab_], w=[ab_])
                S.ts("dve", kk[p, :], kk[p, :], MAGIC, ALU.subtract, r=[ab_], w=[ab_])
                S.stt(rr[p, :], kk[p, :], -C1, ang[p, :], ALU.mult, ALU.add, r=[ab_], w=[ab_])
                S.stt(rr[p, :], kk[p, :], -C2, rr[p, :], ALU.mult, ALU.add, r=[ab_], w=[ab_])
                wrap(rr)
                S.act(kk[p, :], rr[p, :], AF.Sin, r=[ab_], w=[ab_])
                S.ts("dve", tab[p, 1, sl], kk[p, :], rc[p, 1:2], ALU.mult, r=[ab_, rcb], w=[tabb])
                S.ts("dve", rr[p, :], rr[p, :], float(np.pi / 2), ALU.add, r=[ab_], w=[ab_])
                wrap(rr)
                S.act(tab[p, 0, sl], rr[p, :], AF.Sin, r=[ab_], w=[tabb])
            for s_ in range(2):
                qk = flat(xin[s_], L1_SBQK, (512, T))
                lin_fm(R, wview(w, 0, 1024, s_ * 512, 512), 8, 512, hT, hT_b,
                       store_fm_chunk(R, qk, 0, lambda ci: 0.125 if ci < 2 else 1.0))

            def epi_v(tt_, pt, pb):
                st, sbuf = R.stage_s.next()
                S.copy("act", st[:], pt[:], r=[pb], w=[sbuf])
                for grp in range(2):
                    vd = flat(xin[grp], L1_SBV, (T, 256))
                    P.store("sp", vd[tt_ * 128:(tt_ + 1) * 128, :], st[:, grp * 256:(grp + 1) * 256], [sbuf])

            lin_tm(R, wview(w, 0, 1024, 1024, 512), 8, 512, hT, hT_b, epi_v)
            if "after_sb" in d:
                d["after_sb"]()
            gq, gqb = load_gain(P, es3, S, d["q_norm"], 3)
            gk, gkb = load_gain(P, es3, S, d["kv_norm"], 2)
            cqblk = P.sb(es3, [128, 3, 512], F32, "cqblk")
            ckvblk = P.sb(es3, [128, 2, 512], F32, "ckvblk")
            cqbb, ckvbb = Buf(), Buf()
            t1r = Ring([(P.sb(es3, [96, 512], F32, "t1"), Buf()) for _ in range(2)])
            kro = P.sb(es3, [96, T], BF16, "kro")
            krob = Buf()
            wq_, wqb = R.load_slab(wview(w, 0, 1024, 1536, 384), 8, 384)
            wk_, wkb = R.load_slab(wview(w, 0, 1024, 1920, 448), 8, 448)
            for tb in range(NTB):
                sl = slice(tb * 512, (tb + 1) * 512)

                def proj(wt, wb, c0, m):
                    pt, pb = R.mmps.next()
                    for kc in range(8):
                        S.mm(pt[0:m, :], wt[:, kc, c0:c0 + m], hT[:, kc, sl], start=(kc == 0), stop=(kc == 7),
                             r=[wb, hT_b], w=[pb])
                    return pt, pb

                for ci in range(3):
                    pt, pb = proj(wq_, wqb, ci * 128, 128)
                    S.copy("act", cqblk[:, ci, :], pt[:], r=[pb], w=[cqbb])

                def out_q(kc, rt, rb, sl=sl):
                    S.stt(cqn[:, kc, sl], cqblk[:, kc, :], gq[:, kc:kc + 1], rt[:], ALU.mult, ALU.mult,
                          r=[cqbb, gqb, rb], w=[cqnb])

                rms_block(R, cqblk, cqbb, 3, gq, gqb, 384, out_q)
                for ci in range(2):
                    pt, pb = proj(wk_, wkb, ci * 128, 128)
                    S.copy("act", ckvblk[:, ci, :], pt[:], r=[pb], w=[ckvbb])

                def out_kv(kc, rt, rb, sl=sl):
                    S.stt(ckvn[:, kc, sl], ckvblk[:, kc, :], gk[:, kc:kc + 1], rt[:], ALU.mult, ALU.mult,
                          r=[ckvbb, gkb, rb], w=[ckvnb])

                rms_block(R, ckvblk, ckvbb, 2, gk, gkb, 256, out_kv)
                pa, pab = proj(wk_, wkb, 256, 96)
                pbt, pbb = proj(wk_, wkb, 352, 96)
                t1, t1b = t1r.next()
                t2, t2b = t1r.next()
                S.tt("dve", t1[p, :], pa[p, :], tab[p, 0, sl], ALU.mult, r=[pab, tabb], w=[t1b])
                S.tt("dve", t2[p, :], pbt[p, :], tab[p, 1, sl], ALU.mult, r=[pbb, tabb], w=[t2b])
                S.tt("dve", kro[p, sl], t1[p, :], t2[p, :], ALU.add, r=[t1b, t2b], w=[krob])
            for grp in range(2):
                P.store("sp", flat(xin[grp], L1_KR, (32, T)), kro[p, :], [krob])
            S.barrier()
        with ExitStack() as es3:
            wq = d["w_uq"]
            qst = Ring([(P.sb(es3, [96, T], BF16, "qst"), Buf()) for _ in range(2)])
            t1r = Ring([(P.sb(es3, [96, 512], F32, "t1"), Buf()) for _ in range(2)])
            for hp in range(4):
                wt, wb = R.load_slab(wview(wq, 0, 384, hp * 384, 384), 3, 384)
                for hh in range(2):
                    h = hp * 2 + hh
                    st, stb = qst.next()
                    for tb in range(NTB):
                        sl = slice(tb * 512, (tb + 1) * 512)
                        pa, pab = R.mmps.next()
                        pbt, pbb = R.mmps.next()
                        for kc in range(3):
                            S.mm(pa[0:96, :], wt[:, kc, hh * 192:hh * 192 + 96], cqn[:, kc, sl],
                                 start=(kc == 0), stop=(kc == 2), r=[wb, cqnb], w=[pab])
                        for kc in range(3):
                            S.mm(pbt[0:96, :], wt[:, kc, hh * 192 + 96:hh * 192 + 192], cqn[:, kc, sl],
                                 start=(kc == 0), stop=(kc == 2), r=[wb, cqnb], w=[pbb])
                        S.act(st[0:64, sl], pa[0:64, :], AF.Copy, scale=MLA_SCALE, r=[pab], w=[stb])
                        t1, t1b = t1r.next()
                        t2, t2b = t1r.next()
                        S.stt(t1[p, :], pa[p, :], MLA_SCALE, tab[p, 0, sl], ALU.mult, ALU.mult, r=[pab, tabb], w=[t1b])
                        S.stt(t2[p, :], pbt[p, :], MLA_SCALE, tab[p, 1, sl], ALU.mult, ALU.mult, r=[pbb, tabb], w=[t2b])
                        S.tt("dve", st[p, sl], t1[p, :], t2[p, :], ALU.add, r=[t1b, t2b], w=[stb])
                    grp, hl = h // 4, h % 4
                    mq = flat(xin[grp], L1_MQ, (4, 96, T))
                    P.store("sp", mq[hl], st[:], [stb])
            wkv = d["w_ukv"]
            for grp in range(2):
                kn = flat(xin[grp], L1_MKN, (256, T))
                lin_fm(R, wview(wkv, 0, 256, grp * 256, 256), 2, 256, ckvn, ckvnb,
                       store_fm_chunk(R, kn, 0, 1.0))

            def epi_mv(tt_, pt, pb):
                st, sbuf = R.stage_s.next()
                S.copy("act", st[:], pt[:], r=[pb], w=[sbuf])
                for grp in range(2):
                    vd = flat(xin[grp], L1_MV, (T, 256))
                    P.store("sp", vd[tt_ * 128:(tt_ + 1) * 128, :], st[:, grp * 256:(grp + 1) * 256], [sbuf])

            lin_tm(R, wview(wkv, 0, 256, 512, 512), 2, 512, ckvn, ckvnb, epi_mv)
            S.barrier()


class AttCtx:
    def __init__(self, P, es, d, ntiles_mask, mask_first):
        S = P.S
        self.P, self.es = P, es
        self.cm = P.sb(es, [128, 5, 128], BF16, "cmat")
        self.cmb = Buf()
        S.dma("sp", self.cm[:], d["cmat"], w=[self.cmb])
        self.masks = P.sb(es, [128, ntiles_mask, 512], BF16, "masks")
        self.maskb = Buf()
        S.dma("sp", self.masks[:], d["cmask"][:, mask_first:mask_first + ntiles_mask, :], w=[self.maskb])
        self.kq = Ring([((P.sb(es, [128, SEQ], BF16, "kt"), P.sb(es, [128, SEQ], BF16, "qt")),
                         [Buf() for _ in range(4)]) for _ in range(2)])
        self.sps = Ring([(P.ps(es), Buf()) for _ in range(4)])
        self.nps = Ring([(P.ps(es), Buf()) for _ in range(2)])
        self.dps = Ring([(P.ps(es), Buf()) for _ in range(1)])
        self.pt = Ring([(P.sb(es, [128, 512], BF16, "pT"), Buf()) for _ in range(4)])
        self.ost = Ring([(P.sb(es, [65, SEQ], BF16, "ost"), Buf()) for _ in range(2)])
        self.rc = Ring([(P.sb(es, [65, 512], F32, "rc"), Buf()) for _ in range(1)])
        self.dn = Ring([(P.sb(es, [1, 512], F32, "dn"), Buf()) for _ in range(1)])
        self.rcpf = Ring([(P.sb(es, [128, 512], F32, "rcpf"), Buf()) for _ in range(1)])
        S.memset("dve", self.rcpf.items[0][0][:], 0.0, w=[self.rcpf.items[0][1]])
        self.row0f = P.sb(es, [128, 128], F32, "row0f")
        self.row0fb = Buf()
        S.memset("dve", self.row0f[:], 0.0, w=[self.row0fb])
        S.memset("dve", self.row0f[0:1, :], 1.0, w=[self.row0fb])


def load_v(A, xout, off, ncols, dep=()):
    P, S = A.P, A.P.S
    nh = ncols // 64
    n = 32 * nh * 66
    vf = P.sb(A.es, [128, n + 64], BF16, "v")
    vb = [Buf() for _ in range(6)]
    S.memset("dve", vf[:, n:n + 64], 0.0, w=[vb[0]])
    v4 = vf[:, 0:n].rearrange("p (j h c) -> p j h c", h=nh, c=66)
    S.memset("dve", v4[:, :, :, 0:1], 1.0, w=[vb[1]])
    i = 0
    for r in range(2):
        for h in range(nh):
            src = flat(xout[r], off, (T, ncols))[:, h * 64:(h + 1) * 64].rearrange("(j p) c -> p j c", p=128)
            S.dma("sp", v4[:, r * 16:(r + 1) * 16, h, 1:65], src, r=list(dep), w=[vb[2 + i % 4]])
            i += 1

    def lhs(kb, h):
        b = (kb * nh + h) * 66
        return vf[:, b:b + 128]

    return lhs, vb


def softmax_head(A, kt, qt, kqb, krows, v, vb, vh, tiles, yin, hrow):
    P, S = A.P, A.P.S
    J = A.cm[:, 0, :]
    ost, ostb = A.ost.next()
    n = len(tiles)
    sts = [None] * n

    def issue_s(i):
        qb, kb, m, mb, first, last, (c0, c1) = tiles[i]
        st, sb_ = A.sps.next()
        sts[i] = (st, sb_)
        S.mm(st[:, c0:c1], kt[:, kb * 128:(kb + 1) * 128], qt[:, qb * 512 + c0:qb * 512 + c1],
             start=True, stop=(m is None), r=list(kqb), w=[sb_])
        if m is not None:
            S.mm(st[:, c0:c1], J, m[:, c0:c1], start=False, stop=True, r=[A.cmb, mb], w=[sb_])

    acc = {}
    pending = []

    def flush(upto):
        while pending and pending[0][0] <= upto:
            pending.pop(0)[1]()

    LA = 3
    pts = [None] * n

    def do_pv(j):
        qb, kb, m, mb, first, last, (c0, c1) = tiles[j]
        pt, ptb = pts[j]
        if first:
            assert (c0, c1) == (0, 512)
            acc["n"] = A.nps.next()
        nt, nb = acc["n"]
        S.mm(nt[:, c0:c1], v(kb, vh), pt[:, c0:c1], start=first, stop=last, r=list(vb) + [ptb], w=[nb],
             skip_group_check=True)
        flush(j)
        if last:
            dn, dnb = A.dn.next()
            rf, rfb = A.rcpf.next()

            def a1(nt=nt, nb=nb, dn=dn, dnb=dnb):
                S.act(dn[0:1, :], nt[0:1, :], AF.Ln, r=[nb], w=[dnb])

            def a2(dn=dn, dnb=dnb, rf=rf, rfb=rfb):
                S.act(rf[0:1, :], dn[0:1, :], AF.Exp, scale=-1.0, r=[dnb], w=[rfb])

            def fin(nt=nt, nb=nb, rf=rf, rfb=rfb, qb=qb):
                bt, bb = A.dps.next()
                S.mm(bt[:], A.row0f[:], rf[:], start=True, stop=True, r=[A.row0fb, rfb], w=[bb])
                rc, rcb = A.rc.next()
                S.copy("dve", rc[:], bt[0:65, :], r=[bb], w=[rcb])
                S.tt("dve", ost[:, qb * 512:(qb + 1) * 512], nt[0:65, :], rc[:], ALU.mult, r=[nb, rcb], w=[ostb])

            pending.append((j + 2, a1))
            pending.append((j + 3, a2))
            pending.append((j + 8, fin))

    for i0 in range(min(LA, n)):
        issue_s(i0)
    LAG = 2
    for i in range(n + LAG):
        if i < n:
            if i + LA < n:
                issue_s(i + LA)
            qb, kb, m, mb, first, last, (c0, c1) = tiles[i]
            st, sb_ = sts[i]
            pts[i] = A.pt.next()
            S.act(pts[i][0][:, c0:c1], st[:, c0:c1], AF.Exp, r=[sb_], w=[pts[i][1]])
        if i >= LAG:
            do_pv(i - LAG)
    flush(n + 100)
    for r in range(2):
        P.store("pool", yin[r, hrow:hrow + 64, :], ost[1:65, r * T:(r + 1) * T], [ostb])


def load_kq_pair(A, xout, offk, nk_total, krow0, offq, nq_total, qrow0, kt, qt, kbuf, qbuf, dep=()):
    S = A.P.S
    for r in range(2):
        srck = flat(xout[r], offk, (nk_total, T))[krow0:krow0 + 64, :]
        S.dma("sp", kt[0:64, r * T:(r + 1) * T], srck, r=list(dep), w=[kbuf])
        srcq = flat(xout[r], offq, (nq_total, T))[qrow0:qrow0 + 64, :]
        S.dma("sp", qt[0:64, r * T:(r + 1) * T], srcq, r=list(dep), w=[qbuf])


def load_kq_rows(A, xout, off, nrows_total, row0, nrows, kt_or_qt, dst_row0, buf, dep=()):
    S = A.P.S
    for r in range(2):
        src = flat(xout[r], off, (nrows_total, T))[row0:row0 + nrows, :]
        S.dma("sp", kt_or_qt[dst_row0:dst_row0 + nrows, r * T:(r + 1) * T], src, r=list(dep), w=[buf])


def phase_attn0(P, d):
    S = P.S
    xout, yin = d["xout0"], d["yin0"]
    cumd = d["cumd"]
    Ed = d["ebuf"]
    cumdb, Edb = Buf(), Buf()
    with ExitStack() as es:
        lf = P.sb(es, [4, SEQ], F32, "lf")
        lfb = Buf()
        for r in range(2):
            S.dma("sp", lf[:, r * T:(r + 1) * T], d["xoutf0"][r], r=list(d.get("xdepf", [])), w=[lfb])
        onesf = P.sb(es, [4, SEQ], F32, "onesf")
        S.memset("dve", onesf[:], 1.0, w=[lfb])
        cum = P.sb(es, [4, SEQ], F32, "cum")
        S.add("dve", lambda e: e.tensor_tensor_scan(cum[:], onesf[:], lf[:], 0.0, ALU.mult, ALU.add), r=[lfb], w=[lfb])
        c3 = P.sb(es, [4, 2, 3, SEQ], BF16, "c3")
        c3b = Buf()
        S.copy("dve", c3[:, 1, 0, :], cum[:], r=[lfb], w=[c3b])
        S.tt("dve", cum[:], cum[:], c3[:, 1, 0, :], ALU.subtract, r=[lfb, c3b], w=[lfb])
        S.copy("dve", c3[:, 1, 1, :], cum[:], r=[lfb], w=[c3b])
        S.tt("dve", cum[:], cum[:], c3[:, 1, 1, :], ALU.subtract, r=[lfb, c3b], w=[lfb])
        S.copy("dve", c3[:, 1, 2, :], cum[:], r=[lfb], w=[c3b])
        for i in range(3):
            S.ts("dve", c3[:, 0, i, :], c3[:, 1, i, :], -1.0, ALU.mult, r=[c3b], w=[c3b])
        S.dma("sp", cumd, c3[:], r=[c3b], w=[cumdb])
        rb, rbb = load_small(P, es, S, d["rel_bias"], [4, 320], F32)
        E = P.sb(es, [4, 1536], F32, "E")
        Eb = Buf()
        S.memset("dve", E[:], 0.0, w=[Eb])
        S.ts("dve", E[:, 0:449], E[:, 0:449], rb[:, 0:1], ALU.add, r=[rbb, Eb], w=[Eb])
        S.copy("dve", E[:, 449:767], rb[:, 1:319], r=[rbb, Eb], w=[Eb])
        S.ts("dve", E[:, 767:1536], E[:, 767:1536], rb[:, 319:320], ALU.add, r=[rbb, Eb], w=[Eb])
        S.dma("sp", Ed, E[:], r=[Eb], w=[Edb])
        S.barrier()
    with ExitStack() as es:
        A = AttCtx(P, es, d, 12, 0)
        depa, depb = d.get("xdep0a", []), d.get("xdep0b", [])
        bias = P.sb(es, [128, 32, 512], BF16, "bias")
        biasb = Buf()
        hkw = P.sb(es, [128, 1408], F32, "hkw")
        hkwb = Buf()
        VF = {}

        def build_bias():
            for h in range(4):
                src = bass.AP(Ed.tensor, h * 1536, [[1, 128], [1, 1408]])
                S.dma("sp", hkw[:], src, r=[Edb], w=[hkwb])
                for jb in range(8):
                    o = 896 - 128 * jb
                    S.tt("dve", bias[:, h * 8 + jb, :], hkw[:, o:o + 512], A.masks[:, 4 + jb, :], ALU.add,
                         r=[hkwb, A.maskb], w=[biasb])

        def load_head(j):
            (kt, qt), kqb = A.kq.next()
            kB, kxB, qB, qxB = kqb
            S.memset("dve", kt[64:128, :], 0.0, w=[kxB])
            S.memset("dve", qt[64:128, :], 0.0, w=[qxB])
            if j < 4:
                load_kq_pair(A, xout, L0_QKF, 512, 256 + j * 64, L0_QKF, 512, j * 64, kt, qt, kB, qB,
                             d.get("xdep0k", depa))
                S.memset("dve", kt[64:70, :], 1.0, w=[kxB])
                S.memset("dve", qt[64:70, :], 1.0, w=[qxB])
                S.dma("sp", kt[64:67, :], cumd[j, 0], r=[cumdb], w=[kxB])
                S.dma("sp", qt[67:70, :], cumd[j, 1], r=[cumdb], w=[qxB])
            else:
                jj = j - 4
                load_kq_pair(A, xout, L0_QKC, 512, 256 + jj * 64, L0_QKC, 512, jj * 64, kt, qt, kB, qB, depb)
            return kt, qt, kqb

        def run_head(j, kt, qt, kqb):
            tiles = []
            if j < 4:
                for qb in range(8):
                    nk = 4 * qb + 4
                    for kb in range(nk):
                        jm = kb - 4 * qb
                        m = A.masks[:, jm, :] if jm >= 0 else None
                        tiles.append((qb, kb, m, A.maskb, kb == 0, kb == nk - 1, (128 * max(jm, 0), 512)))
                softmax_head(A, kt, qt, kqb, 70, VF["v"], VF["b"], j, tiles, yin, j * 64)
            else:
                jj = j - 4
                CR = {0: (0, 128), 1: (0, 256), 2: (0, 384), 3: (0, 512), 4: (0, 512), 5: (128, 512),
                      6: (256, 512), 7: (384, 512)}
                for qb in range(8):
                    jbs = [jb for jb in (3, 4, 0, 1, 2, 5, 6, 7) if 4 * qb - 4 + jb >= 0]
                    for jb in jbs:
                        kb = 4 * qb - 4 + jb
                        tiles.append((qb, kb, bias[:, jj * 8 + jb, :], biasb, jb == jbs[0], jb == jbs[-1], CR[jb]))
                softmax_head(A, kt, qt, kqb, 64, VC["v"], VC["b"], jj, tiles, yin, 256 + jj * 64)

        VC = {}
        nxt = load_head(0)
        VF["v"], VF["b"] = load_v(A, xout, L0_VF, 256, depa)
        build_bias()
        for j in range(8):
            cur = nxt
            if j == 2:
                VC["v"], VC["b"] = load_v(A, xout, L0_VC, 256, depb)
            if j + 1 < 8:
                nxt = load_head(j + 1)
            run_head(j, *cur)
        S.barrier()


def sb_head(A, kt, qt, kqb, v, vb, vcol, yin, hrow, X):
    P, S = A.P, A.P.S
    J = A.cm[:, 0, :]
    NTRI = A.cm[:, 2, :]
    NONES = A.cm[:, 3, :]
    ost, ostb = A.ost.next()
    tiles = []
    for qb in range(8):
        kbs = list(range(4 * qb + 3, -1, -1))
        for kb in kbs:
            jm = kb - 4 * qb
            tiles.append((qb, kb, jm, kb == kbs[0], kb == kbs[-1]))
    n = len(tiles)
    zs, sps, rss, es_ = [None] * n, [None] * n, [None] * n, [None] * n
    acc = {}

    def stage1(i):
        qb, kb, jm, first, last = tiles[i]
        c0 = 128 * max(jm, 0)
        zt, zb = A.sps.next()
        zs[i] = (zt, zb)
        S.mm(zt[:, c0:], kt[:, kb * 128:(kb + 1) * 128], qt[:, qb * 512 + c0:(qb + 1) * 512],
             start=True, stop=False, r=list(kqb), w=[zb])
        if jm >= 0:
            S.mm(zt[:, c0:], J, A.masks[:, jm, c0:], start=False, stop=False, r=[A.cmb, A.maskb], w=[zb])
        et, eb = X["e"].next()
        S.act(et[:, c0:], zt[:, c0:], AF.Exp, r=[zb], w=[eb])
        es_[i] = (et, eb)

    def stage1b(i):
        qb, kb, jm, first, last = tiles[i]
        c0 = 128 * max(jm, 0)
        et, eb = es_[i]
        spt, spb = X["sp"].next()
        sps[i] = (spt, spb)
        S.act(spt[:, c0:], et[:, c0:], AF.Ln, bias=1.0, r=[eb], w=[spb])
        rt, rb = X["rs"].next()
        rss[i] = (rt, rb)
        if c0 > 0:
            S.memset("dve", rt[:, 0:c0], 0.0, w=[rb])
        if first:
            S.copy("dve", rt[:, c0:], spt[:, c0:], r=[spb], w=[rb])
        else:
            pr, prb = rss[i - 1]
            S.tt("dve", rt[:, c0:], pr[:, c0:], spt[:, c0:], ALU.add, r=[prb, spb], w=[rb])

    def stage2(i):
        qb, kb, jm, first, last = tiles[i]
        c0 = 128 * max(jm, 0)
        zt, zb = zs[i]
        spt, spb = sps[i]
        S.mm(zt[:, c0:], NTRI, spt[:, c0:], start=False, stop=first, r=[A.cmb, spb], w=[zb])
        if not first:
            pr, prb = rss[i - 1]
            S.mm(zt[:, c0:], NONES, pr[:, c0:], start=False, stop=True, r=[A.cmb, prb], w=[zb])
        pt, ptb = A.pt.next()
        sps[i] = (pt, ptb)
        if first and c0 > 0:
            S.memset("dve", pt[:, 0:c0], 0.0, w=[ptb])
        S.act(pt[:, c0:], zt[:, c0:], AF.Exp, r=[zb], w=[ptb])

    def stage3(i):
        qb, kb, jm, first, last = tiles[i]
        c0 = 0 if first else 128 * max(jm, 0)
        pt, ptb = sps[i]
        if first:
            acc["n"] = A.nps.next()
        nt, nb = acc["n"]
        S.mm(nt[:, c0:], v(kb, vcol), pt[:, c0:], start=first, stop=last, r=list(vb) + [ptb], w=[nb],
             skip_group_check=True)
        if last:
            S.copy("dve", ost[:, qb * 512:(qb + 1) * 512], nt[0:65, :], r=[nb], w=[ostb])

    for step in range(-1, n + 2):
        if 0 <= step + 1 < n:
            stage1(step + 1)
        if 0 <= step < n:
            stage1b(step)
        if 0 <= step - 1 < n:
            stage2(step - 1)
        if 0 <= step - 2 < n:
            stage3(step - 2)
    for r in range(2):
        P.store("pool", yin[r, hrow:hrow + 64, :], ost[1:65, r * T:(r + 1) * T], [ostb])


def phase_attn1(P, d):
    S = P.S
    with ExitStack() as es:
        A = AttCtx(P, es, d, 8, 12)
        xout, yin = d["xout1"], d["yin1"]
        dep1, dep2 = d.get("xdep1a", []), d.get("xdep1b", [])
        VS = {}
        VM = {}
        X = {
            "e": Ring([A.dps.items[0], (P.ps(es), Buf())]),
            "sp": Ring([(P.sb(es, [128, 512], BF16, "sp"), Buf()) for _ in range(3)]),
            "rs": Ring([(P.sb(es, [128, 512], BF16, "rs"), Buf()) for _ in range(4)]),
        }

        def load_head(j):
            (kt, qt), kqb = A.kq.next()
            kB, kxB, qB, qxB = kqb
            S.memset("dve", kt[64:128, :], 0.0, w=[kxB])
            S.memset("dve", qt[64:128, :], 0.0, w=[qxB])
            if j < 4:
                load_kq_pair(A, xout, L1_SBQK, 512, 256 + j * 64, L1_SBQK, 512, j * 64, kt, qt, kB, qB,
                             d.get("xdep1k", dep1))
            else:
                jj = j - 4
                for r in range(2):
                    srck = flat(xout[r], L1_MKN, (256, T))[jj * 64:jj * 64 + 64, :]
                    S.dma("sp", kt[0:64, r * T:(r + 1) * T], srck, r=list(dep2), w=[kB])
                    src = flat(xout[r], L1_MQ, (4, 96, T))[jj]
                    S.dma("sp", qt[0:96, r * T:(r + 1) * T], src, r=list(dep2), w=[qB, qxB])
                load_kq_rows(A, xout, L1_KR, 32, 0, 32, kt, 64, kxB, dep2)
            return kt, qt, kqb

        def run_head(j, kt, qt, kqb):
            if j < 4:
                sb_head(A, kt, qt, kqb, VS["v"], VS["b"], j, yin, j * 64, X)
            else:
                jj = j - 4
                tiles = []
                for qb in range(8):
                    nk = 4 * qb + 4
                    for kb in range(nk):
                        jm = kb - 4 * qb
                        m = A.masks[:, 4 + jm, :] if jm >= 0 else None
                        tiles.append((qb, kb, m, A.maskb, kb == 0, kb == nk - 1, (128 * max(jm, 0), 512)))
                softmax_head(A, kt, qt, kqb, 96, VM["v"], VM["b"], jj, tiles, yin, 256 + jj * 64)

        nxt = load_head(0)
        VS["v"], VS["b"] = load_v(A, xout, L1_SBV, 256, dep1)
        for j in range(8):
            cur = nxt
            if j == 2:
                VM["v"], VM["b"] = load_v(A, xout, L1_MV, 256, dep2)
            if j + 1 < 8:
                nxt = load_head(j + 1)
            run_head(j, *cur)
        S.barrier()


def _perm_w_in0():
    cols = []
    for grp in range(2):
        hs = range(4 * grp, 4 * grp + 4)
        for base in (0, 512, 1544, 2056):
            for h in hs:
                cols += list(range(base + h * 64, base + h * 64 + 64))
    for grp in range(2):
        hs = range(4 * grp, 4 * grp + 4)
        for base in (1024, 2568):
            for h in hs:
                cols += list(range(base + h * 64, base + h * 64 + 64))
    cols += list(range(1536, 1544))
    return np.array(cols)


def _perm_w_in1():
    cols = []
    for grp in range(2):
        hs = range(4 * grp, 4 * grp + 4)
        for base in (0, 512):
            for h in hs:
                cols += list(range(base + h * 64, base + h * 64 + 64))
    for grp in range(2):
        for h in range(4 * grp, 4 * grp + 4):
            cols += list(range(1024 + h * 64, 1024 + h * 64 + 64))
    cols += list(range(1536, 1920))
    cols += list(range(1920, 2176))
    kr = list(range(2176, 2208))
    krp = kr[16:] + kr[:16]
    cols += list(range(1920, 1984)) + kr
    cols += list(range(1920, 1984)) + krp
    return np.array(cols)


def _perm_w_uq():
    cols = []
    for h in range(8):
        b = h * 96
        nope = list(range(b, b + 64))
        rope = list(range(b + 64, b + 96))
        cols += nope + rope + nope + rope[16:] + rope[:16]
    return np.array(cols)


def _perm_w_ukv():
    cols = []
    for grp in range(2):
        for h in range(4 * grp, 4 * grp + 4):
            cols += list(range(h * 128, h * 128 + 64))
    for grp in range(2):
        for h in range(4 * grp, 4 * grp + 4):
            cols += list(range(h * 128 + 64, h * 128 + 128))
    return np.array(cols)


def _perm_w_out():
    rows = []
    for grp in range(2):
        for base in (0, 512):
            for h in range(4 * grp, 4 * grp + 4):
                rows += list(range(base + h * 64, base + h * 64 + 64))
    return np.array(rows)


def _const_masks():
    kk = np.arange(128)[:, None]
    qq = np.arange(512)[None, :]
    tiles = []
    for j in range(4):
        tiles.append(np.where(128 * j + kk <= qq, 0.0, NEG))
    for jb in range(8):
        v = (qq // 64) + 8 - 2 * jb - (kk // 64)
        tiles.append(np.where((v >= 0) & (v <= 8), 0.0, NEG))
    for j in range(4):
        tiles.append(np.where(128 * j + kk < qq, 0.0, NEG))
    for j in range(4):
        tiles.append(np.where((128 * j + kk) // 64 <= qq // 64, 0.0, NEG))
    m = np.stack(tiles, axis=1).astype(np.float32)
    m = m[::-1].copy()
    return m.astype(ml_dtypes.bfloat16)


def _const_mats():
    i = np.arange(128)
    J = (i[:, None] + i[None, :] == 127).astype(np.float32)
    ones = np.ones((128, 128), np.float32)
    ntri = -(i[:, None] >= i[None, :]).astype(np.float32)
    nones = -ones
    row0 = np.zeros((128, 128), np.float32)
    row0[0, :] = 1.0
    return np.stack([J, ones, ntri, nones, row0], axis=1).astype(ml_dtypes.bfloat16)


def _rope_consts():
    inv = np.array(INV_FREQ_BITS, dtype=np.uint32).view(np.float32)
    c = np.zeros((96, 2), np.float32)
    c[64:96, 0] = np.concatenate([inv, inv])
    c[64:96, 1] = np.concatenate([-np.ones(16, np.float32), np.ones(16, np.float32)])
    return c


def _run(P, in_maps):
    res = run_bass_kernel_spmd(P.nc, in_maps, core_ids=list(range(NCORES)))
    return res.results


def _finish(P):
    P.S.final_wait("sp", P.outbufs)
    P.S.emit()


def _exchange(xin_list):
    out = []
    for c in range(NCORES):
        b, g = c // 2, c % 2
        out.append(np.stack([xin_list[2 * b + r][g] for r in range(2)], axis=0))
    return out


def build_a0():
    P = Prog()
    d = {
        "xT": P.din("xT", [D, T], F32),
        "norm_mix0": P.din("norm_mix0", [128, 8], F32),
        "w_in0": P.din("w_in0", [D, 3080], F32),
        "b_forget": P.din("b_forget", [8, 1], F32),
        "cmat": P.din("cmat", [128, 5, 128], BF16),
        "xin0": P.dout("xin0", [2, L0_SIZE], BF16),
        "xinf0": P.dout("xinf0", [2, 4, T], F32),
    }
    with ExitStack() as es:
        R = setup_row(P, es, d["cmat"])
        xT = P.sb(es, [128, 8, T], F32, "xT")
        hT = P.sb(es, [128, 8, T], BF16, "hT")
        xT_b, hT_b = Buf(), Buf()
        P.S.dma("sp", xT[:], d["xT"].rearrange("(kc p) t -> p kc t", p=128), w=[xT_b])
        phase_a0(P, R, xT, xT_b, hT, hT_b, d)
        _finish(P)
    return P


def build_attn0():
    P = Prog()
    d = {
        "xout0": P.din("xout0", [2, L0_SIZE], BF16),
        "xoutf0": P.din("xoutf0", [2, 4, T], F32),
        "rel_bias": P.din("rel_bias", [4, 320], F32),
        "cmat": P.din("cmat", [128, 5, 128], BF16),
        "cmask": P.din("cmask", [128, 20, 512], BF16),
        "ebuf": P.dint("ebuf", [4, 1536], F32),
        "cumd": P.dint("cumd", [4, 2, 3, SEQ], BF16),
        "yin0": P.dout("yin0", [2, 512, T], BF16),
    }
    phase_attn0(P, d)
    _finish(P)
    return P


def build_attn1():
    P = Prog()
    d = {
        "xout1": P.din("xout1", [2, L1_SIZE], BF16),
        "cmat": P.din("cmat", [128, 5, 128], BF16),
        "cmask": P.din("cmask", [128, 20, 512], BF16),
        "yin1": P.dout("yin1", [2, 512, T], BF16),
    }
    phase_attn1(P, d)
    _finish(P)
    return P


def build_b(L, last):
    P = Prog()
    d = {
        "xT": P.din("xT", [D, T], F32),
        "yout%d" % L: P.din("yout%d" % L, [2, 512, T], BF16),
        "w_out%d" % L: P.din("w_out%d" % L, [D, D], F32),
        "norm_mlp%d" % L: P.din("norm_mlp%d" % L, [128, 8], F32),
        "w_up%d" % L: P.din("w_up%d" % L, [D, 4096], F32),
        "w_down%d" % L: P.din("w_down%d" % L, [4096, D], F32),
        "cmat": P.din("cmat", [128, 5, 128], BF16),
    }
    if not last:
        d.update({
            "norm_mix1": P.din("norm_mix1", [128, 8], F32),
            "w_in1": P.din("w_in1", [D, 2368], F32),
            "q_norm": P.din("q_norm", [128, 3], F32),
            "kv_norm": P.din("kv_norm", [128, 2], F32),
            "w_uq": P.din("w_uq", [384, 1536], F32),
            "w_ukv": P.din("w_ukv", [256, 1024], F32),
            "ropec": P.din("ropec", [96, 2], F32),
            "pos": P.din("pos", [1, T], I32),
            "xin1": P.dout("xin1", [2, L1_SIZE], BF16),
            "xT1": P.dout("xT1", [D, T], F32),
        })
    else:
        d.update({
            "norm_final": P.din("norm_final", [128, 8], F32),
            "outT": P.dout("outT", [D, T], F32),
        })
    with ExitStack() as es:
        S = P.S
        R = setup_row(P, es, d["cmat"])
        xT = P.sb(es, [128, 8, T], F32, "xT")
        xT_b = Buf()
        S.dma("sp", xT[:], d["xT"].rearrange("(kc p) t -> p kc t", p=128), w=[xT_b])
        with ExitStack() as esh:
            hT = P.sb(esh, [128, 8, T], BF16, "hT")
            hT_b = Buf()
            phase_b(P, R, esh, xT, xT_b, hT, hT_b, d, L)
            S.barrier()
        if not last:
            P.store("sp", d["xT1"].rearrange("(kc p) t -> p kc t", p=128), xT[:], [xT_b])
            phase_a1(P, R, xT, xT_b, d)
        else:
            g, gb = load_gain(P, es, S, d["norm_final"], 8)
            R.ostage = Ring([(P.sb(es, [128, 512], F32, "ostage"), Buf()) for _ in range(3)])
            rms_fm(R, xT, xT_b, 8, g, gb, None, None, D, out_dram=d["outT"])
        _finish(P)
    return P


PAIRS = [[0, 1], [2, 3], [4, 5], [6, 7]]


_XCNT = [0]


def own_copy(P, regs, src, dst, o, sz, pre_deps, nobar):
    bo = Buf()
    w_all = sz // 128
    P.S.add("pool", lambda e: e.dma_start(
        out=dst[regs["g"], o:o + sz].rearrange("(p w) -> p w", w=w_all),
        in_=src[regs["g"], o:o + sz].rearrange("(p w) -> p w", w=w_all)),
        r=pre_deps, w=[bo], dma=True, nobar=nobar)
    return bo


def exchange_start(P, regs, src, dst, ranges, dt, pre_deps, nobar, do_own=True):
    S = P.S
    CH = 128 * 8192
    chunks = []
    for (o, sz) in ranges:
        off = o
        while off < o + sz:
            n = min(CH, o + sz - off)
            chunks.append((off, n, n // 128))
            off += n
    st = []
    own = []
    for (off, n, w) in chunks:
        _XCNT[0] += 1
        bnc = P.dint("xb%d" % _XCNT[0], [128, w], dt)
        gat = P.dint("xg%d" % _XCNT[0], [256, w], dt)
        b1, b2 = Buf(), Buf()
        S.add("pool", lambda e, bnc=bnc, off=off, n=n, w=w: e.dma_start(
            out=bnc, in_=src[regs["ng"], off:off + n].rearrange("(p w) -> p w", w=w)),
            r=pre_deps, w=[b1], dma=True, nobar=nobar)
        st.append((bnc, gat, b1, b2, off, n, w))
    for (bnc, gat, b1, b2, off, n, w) in st:
        S.collective(lambda e, bnc=bnc, gat=gat: e.collective_compute(
            "AllGather", ALU.bypass, replica_groups=PAIRS, ins=[bnc.opt()], outs=[gat.opt()]),
            r=[b1], w=[b2], nobar=nobar)
    if do_own:
        for (o, sz) in ranges:
            own.append(own_copy(P, regs, src, dst, o, sz, pre_deps, nobar))
    return (st, own, dst, nobar)


def exchange_finish(P, regs, state):
    S = P.S
    st, own, dst, nobar = state
    done = list(own)
    for (bnc, gat, b1, b2, off, n, w) in st:
        gv = gat.rearrange("(r p) w -> r p w", r=2)
        b3 = Buf()
        S.add("pool", lambda e, gv=gv, off=off, n=n, w=w: e.dma_start(
            out=dst[regs["ng"], off:off + n].rearrange("(p w) -> p w", w=w), in_=gv[regs["ng"]]),
            r=[b2], w=[b3], dma=True, nobar=nobar)
        done.append(b3)
    return done


def exchange(P, regs, src, dst, size, dt, name):
    S = P.S
    S.barrier()
    P.outbufs = []
    exchange_finish(P, regs, exchange_start(P, regs, src, dst, [(0, size)], dt, [], False))
    S.barrier()
    return dst


def build_fused(upto=5, nof=False):
    P = Prog()
    S = P.S
    n0, n1 = L0_SIZE // 128, L1_SIZE // 128
    d = {
        "xT": P.din("xT", [D, T], F32),
        "gsel": P.din("gsel", [1, 2], I32),
        "cmat": P.din("cmat", [128, 5, 128], BF16),
        "cmask": P.din("cmask", [128, 20, 512], BF16),
        "norm_mix0": P.din("norm_mix0", [128, 8], F32),
        "w_in0": P.din("w_in0", [D, 3080], F32),
        "b_forget": P.din("b_forget", [8, 1], F32),
        "rel_bias": P.din("rel_bias", [4, 320], F32),
        "norm_mix1": P.din("norm_mix1", [128, 8], F32),
        "w_in1": P.din("w_in1", [D, 2368], F32),
        "q_norm": P.din("q_norm", [128, 3], F32),
        "kv_norm": P.din("kv_norm", [128, 2], F32),
        "w_uq": P.din("w_uq", [384, 1536], F32),
        "w_ukv": P.din("w_ukv", [256, 1024], F32),
        "ropec": P.din("ropec", [96, 2], F32),
        "pos": P.din("pos", [1, T], I32),
        "norm_final": P.din("norm_final", [128, 8], F32),
        "outT": P.dout("outT", [D, T], F32),
        "ebuf": P.dint("ebuf", [4, 1536], F32),
        "cumd": P.dint("cumd", [4, 2, 3, SEQ], BF16),
    }
    for L in range(2):
        d["w_out%d" % L] = P.din("w_out%d" % L, [D, D], F32)
        d["norm_mlp%d" % L] = P.din("norm_mlp%d" % L, [128, 8], F32)
        d["w_up%d" % L] = P.din("w_up%d" % L, [D, 4096], F32)
        d["w_down%d" % L] = P.din("w_down%d" % L, [4096, D], F32)
    x0 = P.dint("x_in0", [2, L0_SIZE], BF16)
    x0o = P.dint("x_out0", [2, L0_SIZE], BF16)
    f0 = P.dint("f_in0", [2, 4 * T], F32)
    f0o = P.dint("f_out0", [2, 4 * T], F32)
    x1 = P.dint("x_in1", [2, L1_SIZE], BF16)
    x1o = P.dint("x_out1", [2, L1_SIZE], BF16)
    ys = [P.dint("y_in%d" % L, [2, 512 * T], BF16) for L in range(2)]
    yos = [P.dint("y_out%d" % L, [2, 512 * T], BF16) for L in range(2)]
    d["xin0"] = x0
    d["xinf0"] = f0.rearrange("r (h t) -> r h t", h=4)
    d["xin1"] = x1
    d["yin0"] = ys[0].rearrange("r (f t) -> r f t", f=512)
    d["yin1"] = ys[1].rearrange("r (f t) -> r f t", f=512)

    regs = {}

    def setup(e):
        r0, r1 = e.alloc_register("g"), e.alloc_register("ng")
        e.reg_load(r0, d["gsel"][0:1, 0:1])
        e.reg_load(r1, d["gsel"][0:1, 1:2])
        regs["g"] = e.snap(r0, min_val=0, max_val=1)
        regs["ng"] = e.snap(r1, min_val=0, max_val=1)

    S.add("pool", setup).aux = True
    with ExitStack() as es:
        xT = P.sb(es, [128, 8, T], F32, "xT")
        xT_b = Buf()
        S.dma("sp", xT[:], d["xT"].rearrange("(kc p) t -> p kc t", p=128), w=[xT_b])
        with ExitStack() as es1:
            R = setup_row(P, es1, d["cmat"])
            hT = P.sb(es1, [128, 8, T], BF16, "hT")
            hT_b = Buf()
            phase_a0(P, R, xT, xT_b, hT, hT_b, d)
            S.barrier()
        def early_out():
            P.outbufs = []
            P.store("sp", d["outT"].rearrange("(kc p) t -> p kc t", p=128), xT[:], [xT_b])
            _finish(P)
            return P

        S.barrier()
        P.outbufs = []
        d["xdepf"] = exchange_finish(P, regs, exchange_start(P, regs, f0, f0o, [(0, 4 * T)], F32, [], True))
        d["xoutf0"] = f0o.rearrange("r (h t) -> r h t", h=4)
        d["xout0"] = x0o
        st1 = exchange_start(P, regs, x0, x0o, [(0, L0_VF)], BF16, [], True, do_own=False)
        st2 = exchange_start(P, regs, x0, x0o, [(L0_VF, L0_QKC - L0_VF)], BF16, [], True, do_own=False)
        ownb = own_copy(P, regs, x0, x0o, 0, L0_QKC, [], True)
        d["xdep0k"] = exchange_finish(P, regs, st1) + [ownb]
        d["xdep0a"] = d["xdep0k"] + exchange_finish(P, regs, st2)
        d["xdep0b"] = exchange_finish(P, regs, exchange_start(P, regs, x0, x0o, [(L0_QKC, L0_SIZE - L0_QKC)],
                                                             BF16, [], True))
        if upto == 1:
            return early_out()
        phase_attn0(P, d)
        d["yout0"] = exchange(P, regs, ys[0], yos[0], 512 * T, BF16, "y0").rearrange("r (f t) -> r f t", f=512)
        if upto == 2:
            return early_out()
        with ExitStack() as es1:
            R = setup_row(P, es1, d["cmat"])
            with ExitStack() as esh:
                hT = P.sb(esh, [128, 8, T], BF16, "hT")
                hT_b = Buf()
                phase_b(P, R, esh, xT, xT_b, hT, hT_b, d, 0)
                S.barrier()
            phase_a1(P, R, xT, xT_b, d)
            S.barrier()
            P.outbufs = []
            st1 = exchange_start(P, regs, x1, x1o, [(0, L1_SBV)], BF16, [], True, do_own=False)
            st2 = exchange_start(P, regs, x1, x1o, [(L1_SBV, L1_MQ - L1_SBV)], BF16, [], True, do_own=False)
            ownb = own_copy(P, regs, x1, x1o, 0, L1_MQ, [], True)
            d["xdep1k"] = exchange_finish(P, regs, st1) + [ownb]
            d["xdep1a"] = d["xdep1k"] + exchange_finish(P, regs, st2)
            d["xdep1b"] = exchange_finish(P, regs, exchange_start(P, regs, x1, x1o, [(L1_MQ, L1_SIZE - L1_MQ)],
                                                                 BF16, [], True))
        d["xout1"] = x1o
        phase_attn1(P, d)
        d["yout1"] = exchange(P, regs, ys[1], yos[1], 512 * T, BF16, "y1").rearrange("r (f t) -> r f t", f=512)
        with ExitStack() as es1:
            R = setup_row(P, es1, d["cmat"])
            with ExitStack() as esh:
                hT = P.sb(esh, [128, 8, T], BF16, "hT")
                hT_b = Buf()
                phase_b(P, R, esh, xT, xT_b, hT, hT_b, d, 1)
                S.barrier()
            P.outbufs = []
            gf, gfb = load_gain(P, es1, S, d["norm_final"], 8)
            R.ostage = Ring([(P.sb(es1, [128, 512], F32, "ostage"), Buf()) for _ in range(3)])
            rms_fm(R, xT, xT_b, 8, gf, gfb, None, None, D, out_dram=d["outT"])
            _finish(P)
    return P


_CACHE = {}


def _prog(key, fn):
    if key not in _CACHE:
        _CACHE[key] = fn()
    return _CACHE[key]


def kernel(x, positions, norm_mix, norm_mlp, norm_final, w_in_ab, b_forget, rel_bias, w_out_ab,
           w_in_cd, q_norm, kv_norm, w_uq, w_ukv, w_out_cd, w_up, w_down):
    f32 = lambda a: np.ascontiguousarray(np.asarray(a, dtype=np.float32))
    x = f32(x)
    cmat = _const_mats()
    cmask = _const_masks()
    ropec = _rope_consts()
    w_in0 = f32(np.asarray(w_in_ab)[0][:, _perm_w_in0()])
    w_in1 = f32(np.asarray(w_in_cd)[0][:, _perm_w_in1()])
    w_uq_p = f32(np.asarray(w_uq)[0][:, _perm_w_uq()])
    w_ukv_p = f32(np.asarray(w_ukv)[0][:, _perm_w_ukv()])
    w_out0 = f32(np.asarray(w_out_ab)[0][_perm_w_out(), :])
    w_out1 = f32(np.asarray(w_out_cd)[0][_perm_w_out(), :])
    pos = np.asarray(positions).astype(np.int32)
    gl = lambda v: f32(np.asarray(v, dtype=np.float32).reshape(-1, 128).T)
    xTs = [f32(x[c // 2, (c % 2) * T:(c % 2 + 1) * T, :].T) for c in range(NCORES)]
    if FUSED:
        P = _prog("fused", build_fused)
        rbias = np.asarray(rel_bias, dtype=np.float32)[0]
        common = {
            "cmat": cmat, "cmask": cmask, "norm_mix0": gl(norm_mix[0]), "w_in0": w_in0,
            "b_forget": f32(np.asarray(b_forget)[0].reshape(8, 1)),
            "norm_mix1": gl(norm_mix[1]), "w_in1": w_in1, "q_norm": gl(np.asarray(q_norm)[0]),
            "kv_norm": gl(np.asarray(kv_norm)[0]), "w_uq": w_uq_p, "w_ukv": w_ukv_p, "ropec": ropec,
            "norm_final": gl(norm_final), "w_out0": w_out0, "w_out1": w_out1,
            "norm_mlp0": gl(norm_mlp[0]), "norm_mlp1": gl(norm_mlp[1]),
            "w_up0": f32(np.asarray(w_up)[0]), "w_up1": f32(np.asarray(w_up)[1]),
            "w_down0": f32(np.asarray(w_down)[0]), "w_down1": f32(np.asarray(w_down)[1]),
        }
        maps = []
        for c in range(NCORES):
            g = c % 2
            m = dict(common)
            m.update({"xT": xTs[c], "gsel": np.array([[g, 1 - g]], np.int32),
                      "rel_bias": f32(rbias[4 * g:4 * g + 4]),
                      "pos": np.ascontiguousarray(pos[c // 2, g * T:(g + 1) * T].reshape(1, T))})
            maps.append(m)
        rr = _run(P, maps)
        out = np.empty((4, SEQ, D), np.float32)
        for c in range(NCORES):
            out[c // 2, (c % 2) * T:(c % 2 + 1) * T, :] = rr[c]["outT"].T
        return out

    P = _prog("a0", build_a0)
    maps = [{"xT": xTs[c], "norm_mix0": gl(norm_mix[0]), "w_in0": w_in0,
             "b_forget": f32(np.asarray(b_forget)[0].reshape(8, 1)), "cmat": cmat} for c in range(NCORES)]
    r1 = _run(P, maps)
    xout0 = _exchange([r["xin0"] for r in r1])
    xoutf0 = _exchange([r["xinf0"] for r in r1])
    P = _prog("attn0", build_attn0)
    rbias = np.asarray(rel_bias, dtype=np.float32)[0]
    maps = [{"xout0": xout0[c], "xoutf0": xoutf0[c], "rel_bias": f32(rbias[4 * (c % 2):4 * (c % 2) + 4]),
             "cmat": cmat, "cmask": cmask} for c in range(NCORES)]
    r2 = _run(P, maps)
    yout0 = _exchange([r["yin0"] for r in r2])
    P = _prog("b0", lambda: build_b(0, False))
    maps = [{"xT": xTs[c], "yout0": yout0[c], "w_out0": w_out0, "norm_mlp0": gl(norm_mlp[0]),
             "w_up0": f32(np.asarray(w_up)[0]), "w_down0": f32(np.asarray(w_down)[0]), "cmat": cmat,
             "norm_mix1": gl(norm_mix[1]), "w_in1": w_in1, "q_norm": gl(np.asarray(q_norm)[0]),
             "kv_norm": gl(np.asarray(kv_norm)[0]), "w_uq": w_uq_p, "w_ukv": w_ukv_p, "ropec": ropec,
             "pos": np.ascontiguousarray(pos[c // 2, (c % 2) * T:(c % 2 + 1) * T].reshape(1, T))}
            for c in range(NCORES)]
    r3 = _run(P, maps)
    xout1 = _exchange([r["xin1"] for r in r3])
    P = _prog("attn1", build_attn1)
    maps = [{"xout1": xout1[c], "cmat": cmat, "cmask": cmask} for c in range(NCORES)]
    r4 = _run(P, maps)
    yout1 = _exchange([r["yin1"] for r in r4])
    P = _prog("b1", lambda: build_b(1, True))
    maps = [{"xT": r3[c]["xT1"], "yout1": yout1[c], "w_out1": w_out1, "norm_mlp1": gl(norm_mlp[1]),
             "w_up1": f32(np.asarray(w_up)[1]), "w_down1": f32(np.asarray(w_down)[1]), "cmat": cmat,
             "norm_final": gl(norm_final)} for c in range(NCORES)]
    r5 = _run(P, maps)
    out = np.empty((4, SEQ, D), np.float32)
    for c in range(NCORES):
        out[c // 2, (c % 2) * T:(c % 2 + 1) * T, :] = r5[c]["outT"].T
    return out
```
